# Optimizing a Trainium2 kernel written in Bass

```python
import math
import jax, jax.numpy as jnp
from jax import lax
import numpy as np

D_MODEL = 1024
BATCH = 32
SEQ = 256
DEPTH = 2
DEC_BATCH = 8
DEC_SEQ = 4096
PAST_LEN = 512

GRID_W = 64
POS_BASE = 10000.0
N_DIR = 2
N_HEADS = 4
HEAD_DK = D_MODEL // (2 * N_HEADS)
HEAD_DV = D_MODEL // N_HEADS
DK = N_HEADS * HEAD_DK
DV = N_HEADS * HEAD_DV
GATE_RANK = 16
GATE_TAU = 16.0
CHUNK = 64
S5_WIDTH = D_MODEL // 2
S5_GROUP = 16
S5_GROUPS = S5_WIDTH // S5_GROUP
S5_STATE = 64
D_FF = 4 * D_MODEL
N_MOD = 6
EPS = 1e-6
SPLITS = (DK, DK, DV, DV, N_DIR * GATE_RANK, S5_WIDTH, D_MODEL, D_MODEL)
SPLIT_IDX = tuple(int(i) for i in np.cumsum(SPLITS)[:-1])
D_IN = int(sum(SPLITS))

kernel_name = "hybrid_gla_s5_diffusion_step"


def rms_norm(x, g):
    x32 = x.astype(jnp.float32)
    y = x32 * lax.rsqrt(jnp.mean(jnp.square(x32), -1, keepdims=True) + EPS) * g.astype(jnp.float32)
    return y.astype(x.dtype)


def grid_pos_embed(n_tokens, dim):
    rows = n_tokens // GRID_W
    r = jnp.repeat(jnp.arange(rows, dtype=jnp.float32), GRID_W)
    col = jnp.tile(jnp.arange(GRID_W, dtype=jnp.float32), rows)
    quarter = dim // 4
    omega = 1.0 / (POS_BASE ** (jnp.arange(quarter, dtype=jnp.float32) / quarter))
    ar = r[:, None] * omega
    ac = col[:, None] * omega
    return jnp.concatenate([jnp.sin(ar), jnp.cos(ar), jnp.sin(ac), jnp.cos(ac)], axis=-1)


def gla_scan(q, k, v, log_a, s0):
    B, T, H, _ = q.shape
    dv = v.shape[-1]
    n = T // CHUNK

    def blocks(a):
        return a.reshape(B, n, CHUNK, H, a.shape[-1]).transpose(1, 0, 3, 2, 4)

    qb, kb, vb, gb = blocks(q), blocks(k), blocks(v), blocks(log_a)
    cum = jnp.cumsum(gb, axis=3)
    total = cum[:, :, :, -1:, :]
    q_dec = qb * jnp.exp(cum)
    k_in = kb * jnp.exp(-cum)
    k_out = kb * jnp.exp(total - cum)
    mask = jnp.tril(jnp.ones((CHUNK, CHUNK), dtype=bool))
    scores = jnp.where(mask, jnp.einsum('nbhid,nbhjd->nbhij', q_dec, k_in), 0.0)
    o_intra = jnp.einsum('nbhij,nbhjv->nbhiv', scores, vb)

    def step(s, inp):
        qd, ko, vv, tot = inp
        o = jnp.einsum('bhid,bhdv->bhiv', qd, s)
        s = jnp.exp(tot[:, :, 0, :])[..., None] * s + jnp.einsum('bhjd,bhjv->bhdv', ko, vv)
        return s, o

    s_final, o_inter = lax.scan(step, s0, (q_dec, k_out, vb, total))
    o = (o_intra + o_inter).transpose(1, 0, 3, 2, 4).reshape(B, T, H, dv)
    return o, s_final


def gla_mixer(q, k, v, r, glr, w_gate_up, b_gate, norm_g, s0):
    B, T, _ = q.shape
    dtype = q.dtype
    f32 = jnp.float32
    qh = q.astype(f32).reshape(B, T, N_HEADS, HEAD_DK) * (HEAD_DK ** -0.5)
    kh = k.astype(f32).reshape(B, T, N_HEADS, HEAD_DK)
    vh = v.astype(f32).reshape(B, T, N_HEADS, HEAD_DV)
    z = jnp.einsum('btnr,nrk->btnk', glr.astype(f32).reshape(B, T, N_DIR, GATE_RANK),
                   w_gate_up.astype(f32)) + b_gate.astype(f32)
    log_a = (jax.nn.log_sigmoid(z) / GATE_TAU).reshape(B, T, N_DIR, N_HEADS, HEAD_DK)
    s0 = s0.astype(f32)
    o_f, s_f = gla_scan(qh, kh, vh, log_a[:, :, 0], s0[:, 0])
    flip = lambda a: jnp.flip(a, axis=1)
    o_b, s_b = gla_scan(flip(qh), flip(kh), flip(vh), flip(log_a[:, :, 1]), s0[:, 1])
    o = o_f + flip(o_b)
    o = o * lax.rsqrt(jnp.mean(jnp.square(o), -1, keepdims=True) + EPS) * norm_g.astype(f32)
    o = o.reshape(B, T, DV) * jax.nn.silu(r.astype(f32))
    return o.astype(dtype), jnp.stack([s_f, s_b], axis=1)


def _linear_recurrence(earlier, later):
    a_e, b_e = earlier
    a_l, b_l = later
    return a_l * a_e, a_l * b_e + b_l


def s5_direction(u, lam_re, lam_im, log_step, b_re, b_im, c_re, c_im, s0_re, s0_im):
    T = u.shape[1]
    lam = lax.complex(lam_re, lam_im)
    lam_bar = jnp.exp(lam * jnp.exp(log_step)[:, None])
    b_bar = ((lam_bar - 1.0) / lam)[..., None] * lax.complex(b_re, b_im)
    bu = jnp.einsum('btgn,gpn->tbgp', u.astype(jnp.complex64), b_bar)
    bu = bu.at[0].add(lam_bar * lax.complex(s0_re, s0_im))
    a = jnp.broadcast_to(lam_bar, (T, 1) + lam_bar.shape)
    _, s = lax.associative_scan(_linear_recurrence, (a, bu), axis=0)
    y = jnp.einsum('tbgp,gnp->btgn', s, lax.complex(c_re, c_im)).real
    return y, s[-1].real, s[-1].imag


def s5_mixer(u, s5_params, d, w_glu, b_glu, s0_re, s0_im):
    B, T, _ = u.shape
    f32 = jnp.float32
    u32 = u.astype(f32)
    ug = u32.reshape(B, T, S5_GROUPS, S5_GROUP)
    prm = [p_.astype(f32) for p_ in s5_params]
    s0_re = s0_re.astype(f32)
    s0_im = s0_im.astype(f32)
    y_f, f_re, f_im = s5_direction(ug, *[p_[0] for p_ in prm], s0_re[:, 0], s0_im[:, 0])
    y_b, b_re, b_im = s5_direction(jnp.flip(ug, 1), *[p_[1] for p_ in prm], s0_re[:, 1], s0_im[:, 1])
    y = (y_f + jnp.flip(y_b, 1)).reshape(B, T, S5_WIDTH) + d.astype(f32) * u32
    y = jax.nn.gelu(y)
    y = y * jax.nn.sigmoid(y @ w_glu.astype(f32) + b_glu.astype(f32))
    return y.astype(u.dtype), jnp.stack([f_re, b_re], axis=1), jnp.stack([f_im, b_im], axis=1)


def trunk_layer(x, mod, gla_s0, s5_s0_re, s5_s0_im, p):
    shift1, scale1, gate1, shift2, scale2, gate2 = jnp.split(mod[:, None, :], N_MOD, axis=-1)
    h = rms_norm(x, p["norm1_g"]) * (1 + scale1) + shift1
    q, k, v, r, glr, u, ga, gb = jnp.split(h @ p["w_in"], SPLIT_IDX, axis=-1)
    o_gla, gla_state = gla_mixer(q, k, v, r, glr, p["w_gate_up"], p["b_gate"], p["gla_norm_g"], gla_s0)
    o_s5, s5_re, s5_im = s5_mixer(u, p["s5"], p["s5_d"], p["w_glu"], p["b_glu"], s5_s0_re, s5_s0_im)
    merged = (jax.nn.sigmoid(ga) * (o_gla @ p["w_proj_gla"])
              + jax.nn.sigmoid(gb) * (o_s5 @ p["w_proj_s5"]))
    x = x + gate1 * (merged @ p["w_out"])
    h2 = rms_norm(x, p["norm2_g"]) * (1 + scale2) + shift2
    x = x + gate2 * (jnp.square(jax.nn.relu(h2 @ p["w_ff1"])) @ p["w_ff2"])
    return x, gla_state, s5_re, s5_im


def setup_inputs(seed: int = 0) -> dict:
    key = jax.random.key(seed)
    ks = iter(jax.random.split(key, 40))
    f32 = jnp.float32

    def nrm(shape, scale):
        return scale * jax.random.normal(next(ks), shape, f32)

    n_idx = jnp.arange(S5_STATE, dtype=f32)
    s5_shape = (DEPTH, N_DIR, S5_GROUPS, S5_STATE)
    return {
        "x_prompt": nrm((BATCH, SEQ, D_MODEL), 1.0),
        "x_sample": nrm((DEC_BATCH, DEC_SEQ, D_MODEL), 1.0),
        "c": nrm((DEC_BATCH, D_MODEL), 1.0),
        "cache_gla_state": nrm((DEC_BATCH, DEPTH, N_DIR, N_HEADS, HEAD_DK, HEAD_DV), 1.0),
        "state_s5_re": nrm((DEC_BATCH, DEPTH, N_DIR, S5_GROUPS, S5_STATE), 0.5),
        "state_s5_im": nrm((DEC_BATCH, DEPTH, N_DIR, S5_GROUPS, S5_STATE), 0.5),
        "c_ctx": nrm((D_MODEL,), 1.0),
        "w_mod": nrm((DEPTH, D_MODEL, N_MOD * D_MODEL), 0.5 * D_MODEL ** -0.5),
        "b_mod": nrm((DEPTH, N_MOD * D_MODEL), 0.02),
        "norm1_g": 1.0 + nrm((DEPTH, D_MODEL), 0.02),
        "w_in": nrm((DEPTH, D_MODEL, D_IN), D_MODEL ** -0.5),
        "w_gate_up": nrm((DEPTH, N_DIR, GATE_RANK, DK), GATE_RANK ** -0.5),
        "b_gate": nrm((DEPTH, N_DIR, DK), 0.1),
        "gla_norm_g": 1.0 + nrm((DEPTH, HEAD_DV), 0.02),
        "w_proj_gla": nrm((DEPTH, DV, D_MODEL), DV ** -0.5),
        "s5_lam_re": -0.5 + nrm(s5_shape, 0.01),
        "s5_lam_im": math.pi * n_idx + nrm(s5_shape, 0.01),
        "s5_log_step": jax.random.uniform(next(ks), (DEPTH, N_DIR, S5_GROUPS), f32,
                                          math.log(1e-3), math.log(1e-1)),
        "s5_b_re": nrm((DEPTH, N_DIR, S5_GROUPS, S5_STATE, S5_GROUP), (2 * S5_GROUP) ** -0.5),
        "s5_b_im": nrm((DEPTH, N_DIR, S5_GROUPS, S5_STATE, S5_GROUP), (2 * S5_GROUP) ** -0.5),
        "s5_c_re": nrm((DEPTH, N_DIR, S5_GROUPS, S5_GROUP, S5_STATE), S5_STATE ** -0.5),
        "s5_c_im": nrm((DEPTH, N_DIR, S5_GROUPS, S5_GROUP, S5_STATE), S5_STATE ** -0.5),
        "s5_d": nrm((DEPTH, S5_WIDTH), 0.5),
        "w_glu": nrm((DEPTH, S5_WIDTH, S5_WIDTH), S5_WIDTH ** -0.5),
        "b_glu": nrm((DEPTH, S5_WIDTH), 0.02),
        "w_proj_s5": nrm((DEPTH, S5_WIDTH, D_MODEL), S5_WIDTH ** -0.5),
        "w_out": nrm((DEPTH, D_MODEL, D_MODEL), D_MODEL ** -0.5),
        "norm2_g": 1.0 + nrm((DEPTH, D_MODEL), 0.02),
        "w_ff1": nrm((DEPTH, D_MODEL, D_FF), D_MODEL ** -0.5),
        "w_ff2": nrm((DEPTH, D_FF, D_MODEL), D_FF ** -0.5),
        "final_g": 1.0 + nrm((D_MODEL,), 0.02),
    }


def reference(x_prompt, x_sample, c, cache_gla_state, state_s5_re, state_s5_im, c_ctx,
              w_mod, b_mod, norm1_g, w_in, w_gate_up, b_gate, gla_norm_g, w_proj_gla,
              s5_lam_re, s5_lam_im, s5_log_step, s5_b_re, s5_b_im, s5_c_re, s5_c_im, s5_d,
              w_glu, b_glu, w_proj_s5, w_out, norm2_g, w_ff1, w_ff2, final_g):
    layers = [dict(norm1_g=norm1_g[l], w_in=w_in[l], w_gate_up=w_gate_up[l], b_gate=b_gate[l],
                   gla_norm_g=gla_norm_g[l], w_proj_gla=w_proj_gla[l],
                   s5=(s5_lam_re[l], s5_lam_im[l], s5_log_step[l], s5_b_re[l], s5_b_im[l],
                       s5_c_re[l], s5_c_im[l]),
                   s5_d=s5_d[l], w_glu=w_glu[l], b_glu=b_glu[l], w_proj_s5=w_proj_s5[l],
                   w_out=w_out[l], norm2_g=norm2_g[l], w_ff1=w_ff1[l], w_ff2=w_ff2[l])
              for l in range(DEPTH)]

    nb = x_prompt.shape[0]
    gla0 = jnp.zeros((nb, N_DIR, N_HEADS, HEAD_DK, HEAD_DV), jnp.float32)
    s50 = jnp.zeros((nb, N_DIR, S5_GROUPS, S5_STATE), jnp.float32)
    xc = x_prompt
    gla_list, s5_re_list, s5_im_list = [], [], []
    for l in range(DEPTH):
        mod_ctx = (jax.nn.silu(c_ctx) @ w_mod[l] + b_mod[l])[None]
        xc, g_st, s_re, s_im = trunk_layer(xc, mod_ctx, gla0, s50, s50, layers[l])
        gla_list.append(g_st)
        s5_re_list.append(s_re)
        s5_im_list.append(s_im)
    y_prompt = rms_norm(xc, final_g)
    new_gla_state = jnp.stack(gla_list, axis=1)
    new_s5_re = jnp.stack(s5_re_list, axis=1)
    new_s5_im = jnp.stack(s5_im_list, axis=1)

    xs = x_sample + grid_pos_embed(x_sample.shape[1], D_MODEL).astype(x_sample.dtype)
    for l in range(DEPTH):
        mod = jax.nn.silu(c) @ w_mod[l] + b_mod[l]
        xs, _, _, _ = trunk_layer(xs, mod, cache_gla_state[:, l], state_s5_re[:, l],
                                  state_s5_im[:, l], layers[l])
    y_sample = rms_norm(xs, final_g)

    return (y_prompt, y_sample, new_gla_state, new_s5_re, new_s5_im)
```

```python
import math
from contextlib import ExitStack

import numpy as np
import concourse.bass as bass
import concourse.mybir as mybir
from concourse.bass_utils import run_bass_kernel_spmd

F32 = mybir.dt.float32
BF16 = mybir.dt.bfloat16
I32 = mybir.dt.int32
AF = mybir.ActivationFunctionType
ALU = mybir.AluOpType

D = 1024
KC = 8
H = 4
DK = 128
DV = 256
G = 32
DFF = 4096
DIN = 5664
DEPTH = 2
EPS = 1e-6
C_Q, C_K, C_V, C_R, C_GLR, C_U, C_GA, C_GB = 0, 512, 1024, 2048, 3072, 3104, 3616, 4640
TWO_PI = 2.0 * math.pi
NLANES = 24
NHW = 16
NWS = 3


class Op:
    __slots__ = ("eng", "fn", "deps", "needed", "cnt", "dma", "lane", "lane_total", "lane_prev")

    def __init__(self, eng, fn, dma):
        self.eng = eng
        self.fn = fn
        self.dma = dma
        self.deps = ()
        self.needed = False
        self.cnt = 0
        self.lane = -1
        self.lane_total = 0
        self.lane_prev = 0


class KeyState:
    __slots__ = ("w", "r")

    def __init__(self):
        self.w = None
        self.r = []


class Prog:
    ENGS = ("pe", "act", "dve", "pool", "sp")

    def __init__(self, nc, gstack):
        self.nc = nc
        self.gstack = gstack
        self.ops = []
        self.keys = {}
        self.lane_sems = [gstack.enter_context(nc.semaphore(f"lane{i}")) for i in range(NLANES)]
        self.lane_tot = [0] * NLANES
        self.next_lane = 0
        self.next_sw_lane = NHW
        self.prev_final = []
        self.phase_no = 0
        self.ps_rr = 0
        self.w_rr = 0
        self.n_inst = 0

    def op(self, eng, fn, reads=(), writes=(), dma=False):
        o = Op(eng, fn, dma)
        deps = []
        for k in reads:
            st = self.keys.get(k)
            if st is None:
                st = self.keys[k] = KeyState()
            if st.w is not None:
                deps.append(st.w)
        for k in writes:
            st = self.keys.get(k)
            if st is None:
                st = self.keys[k] = KeyState()
            if st.w is not None:
                deps.append(st.w)
            deps.extend(st.r)
        for k in reads:
            self.keys[k].r.append(o)
        for k in writes:
            st = self.keys[k]
            st.w = o
            st.r = []
        if dma:
            if eng == "pool":
                o.lane = self.next_sw_lane
                self.next_sw_lane = NHW + (self.next_sw_lane + 1 - NHW) % (NLANES - NHW)
            else:
                o.lane = self.next_lane
                self.next_lane = (self.next_lane + 1) % NHW
            o.lane_prev = self.lane_tot[o.lane]
            self.lane_tot[o.lane] += 16
            o.lane_total = self.lane_tot[o.lane]
            o.needed = True
        o.deps = [d for d in deps if d is not o and not (d.eng == "pe" and eng == "pe" and not d.dma)]
        self.ops.append(o)
        return o

    def flush(self):
        nc = self.nc
        ops = self.ops
        self.ops = []
        self.keys = {}
        self.phase_no += 1
        sems = {e: self.gstack.enter_context(nc.semaphore(f"ph{self.phase_no}_{e}")) for e in self.ENGS}
        for o in ops:
            for d in o.deps:
                d.needed = True
        per = {e: [] for e in self.ENGS}
        for o in ops:
            per[o.eng].append(o)
        for e in self.ENGS:
            if per[e]:
                last = per[e][-1]
                last.needed = True
        cnt = {e: 0 for e in self.ENGS}
        for o in ops:
            if o.needed and not o.dma:
                cnt[o.eng] += 1
                o.cnt = cnt[o.eng]
        prev_final = self.prev_final
        lane_sems = self.lane_sems
        self.n_inst += len(ops)

        def emit(engname, eng):
            seen = {}

            def wait(sem, val):
                if val <= 0:
                    return
                k = id(sem)
                if seen.get(k, 0) >= val:
                    return
                seen[k] = val
                eng.wait_ge(sem, val)

            for (s, v) in prev_final:
                wait(s, v)
            for o in per[engname]:
                for d in o.deps:
                    if d.dma:
                        wait(lane_sems[d.lane], d.lane_total)
                    else:
                        wait(sems[d.eng], d.cnt)
                if o.dma:
                    wait(lane_sems[o.lane], o.lane_prev)
                ins = o.fn(eng)
                if o.dma:
                    ins.then_inc(lane_sems[o.lane], 16)
                elif o.needed:
                    ins.then_inc(sems[o.eng], 1)

        with nc.Block() as block:
            if per["pe"]:
                @block.tensor
                def _(eng):
                    emit("pe", eng)
            if per["act"]:
                @block.scalar
                def _(eng):
                    emit("act", eng)
            if per["dve"]:
                @block.vector
                def _(eng):
                    emit("dve", eng)
            if per["pool"]:
                @block.gpsimd
                def _(eng):
                    emit("pool", eng)
            if per["sp"]:
                @block.sync
                def _(eng):
                    emit("sp", eng)
        pf = []
        for e in self.ENGS:
            if cnt[e] > 0:
                pf.append((sems[e], cnt[e]))
        for i in range(NLANES):
            if self.lane_tot[i] > 0:
                pf.append((lane_sems[i], self.lane_tot[i]))
        self.prev_final = pf

    def final_wait(self):
        nc = self.nc
        pf = self.prev_final
        with nc.Block() as block:
            @block.sync
            def _(eng):
                for (s, v) in pf:
                    eng.wait_ge(s, v)

    def mm(self, out, lhsT, rhs, start, stop, r, w):
        return self.op("pe", lambda e: e.matmul(out, lhsT=lhsT, rhs=rhs, start=start, stop=stop), r, w)

    def transpose(self, out, in_, ident, r, w):
        return self.op("pe", lambda e: e.transpose(out=out, in_=in_, identity=ident), r, w)

    def act(self, out, in_, func, r, w, scale=1.0, bias=None):
        if bias is None:
            return self.op("act", lambda e: e.activation(out=out, in_=in_, func=func, scale=scale), r, w)
        return self.op("act", lambda e: e.activation(out=out, in_=in_, func=func, scale=scale, bias=bias), r, w)

    def tt(self, eng, out, in0, in1, op, r, w):
        return self.op(eng, lambda e: e.tensor_tensor(out=out, in0=in0, in1=in1, op=op), r, w)

    def ts(self, eng, out, in0, s1, s2, op0, op1, r, w):
        if s2 is None:
            return self.op(eng, lambda e: e.tensor_scalar(out=out, in0=in0, scalar1=s1, scalar2=None, op0=op0), r, w)
        return self.op(eng, lambda e: e.tensor_scalar(out=out, in0=in0, scalar1=s1, scalar2=s2, op0=op0, op1=op1), r, w)

    def stt(self, out, in0, scalar, in1, op0, op1, r, w):
        return self.op("dve", lambda e: e.scalar_tensor_tensor(out=out, in0=in0, scalar=scalar, in1=in1, op0=op0, op1=op1), r, w)

    def copy(self, eng, out, in_, r, w):
        if eng == "act":
            return self.act(out, in_, AF.Copy, r, w)
        return self.op(eng, lambda e: e.tensor_copy(out=out, in_=in_), r, w)

    def memset(self, eng, ap, val, w):
        return self.op(eng, lambda e: e.memset(ap, val), (), w)

    def dma(self, out, in_, r, w, eng="sp", **kw):
        return self.op(eng, lambda e: e.dma_start(out=out, in_=in_, **kw), r, w, dma=True)


def sap(t, off, dims, rowsize):
    return bass.AP(t, off, [[rowsize, 128]] + [list(d) for d in dims])


def sapp(t, p0, npart, off, dims, rowsize):
    return bass.AP(t, p0 * rowsize + off, [[rowsize, npart]] + [list(d) for d in dims])


def dap(t, off, dims):
    return bass.AP(t, off, [list(d) for d in dims])


BIGW = (("w_in", D, DIN), ("w_pg", D, D), ("w_glu", 512, 512), ("w_ps", 512, D), ("w_out", D, D),
        ("w_ff1", D, DFF), ("w_ff2", DFF, D))


def build(cfg, debug=None):
    NP, TP, TS = cfg
    NTOK = NP * TP + TS
    nc = bass.Bass("TRN2", target_bir_lowering=False)
    gs = ExitStack()

    def din(name, shape, dt=F32):
        return nc.dram_tensor(name, list(shape), dt, kind="ExternalInput")

    def dout(name, shape, dt=F32):
        return nc.dram_tensor(name, list(shape), dt, kind="ExternalOutput")

    def dscr(name, shape, dt):
        return nc.dram_tensor(name, list(shape), dt, kind="Internal")

    def gsb(name, shape, dt):
        return gs.enter_context(nc.sbuf_tensor(name, list(shape), dt))

    xT = din("xT", [D, NTOK])
    condT = din("condT", [128, 16])
    w_mod = din("w_mod", [DEPTH, D, 6 * D])
    b_modT = din("b_modT", [128, DEPTH * 48])
    n1g_d = din("n1g", [128, DEPTH * 8])
    n2g_d = din("n2g", [128, DEPTH * 8])
    fng_d = din("fng", [128, 8])
    wsrc = {n: din(n, [DEPTH, k, m]) for (n, k, m) in BIGW}
    w_gu = din("w_gu", [16, DEPTH * 2 * 512])
    bg_d = din("bgT", [128, DEPTH * 2 * 4])
    gng_d = din("gngT", [128, DEPTH * 2])
    bglu_d = din("bgluT", [128, DEPTH * 4])
    d8_d = din("d8", [128, DEPTH * 32])
    lre_d = din("lre2", [128, DEPTH * 2 * 32])
    lim_d = din("lim2", [128, DEPTH * 2 * 32])
    lst_d = din("lst2", [128, DEPTH * 2 * 32])
    Bre_d = din("Bre2", [128, DEPTH * 2 * 32 * 16])
    Bim_d = din("Bim2", [128, DEPTH * 2 * 32 * 16])
    Cre_d = din("Cre2", [128, DEPTH * 2 * 32 * 16])
    Cim_d = din("Cim2", [128, DEPTH * 2 * 32 * 16])
    gla0_d = din("gla0", [DEPTH * 2 * 4 * 128, 256])
    x0_d = din("s5x0", [128, DEPTH * 2 * 32])
    x0s_d = din("s5x0s", [128, DEPTH * 2 * 32])
    cst_d = din("cst", [128, 1280])
    csm_d = din("csm", [128, 32])
    pos_d = din("pos", [128, 512])

    yT = dout("yT", [D, NTOK])
    gla_out = dout("gla_out", [NP * DEPTH * 2 * 4 * 128, 256])
    s5_out = dout("s5_out", [128, NP * DEPTH * 2 * 32])

    STREAMS = {"w_inA": ("w_in", 0, 3072, D, DIN), "w_inB": ("w_in", C_U, 2560, D, DIN), "w_inG": ("w_in", C_GLR, 32, D, DIN),
               "w_pg": ("w_pg", 0, D, D, D), "w_glu": ("w_glu", 0, 512, 512, 512), "w_ps": ("w_ps", 0, D, 512, D),
               "w_out": ("w_out", 0, D, D, D), "w_ff1": ("w_ff1", 0, DFF, D, DFF), "w_ff2": ("w_ff2", 0, D, DFF, D)}

    def sgeom(sn):
        src, c0, ncols, K, Msrc = STREAMS[sn]
        KCn = K // 128
        bw = min(512, 4096 // KCn, ncols)
        return src, c0, ncols, K, Msrc, KCn, bw, ncols // bw
    wb = {}
    for sn in STREAMS:
        _, _, _, _, _, KCn_, bw_, nblk_ = sgeom(sn)
        wb[sn] = dscr(sn + "_bf", [DEPTH * nblk_ * 128, KCn_ * bw_], BF16)

    def wstream(name, m0):
        if name == "w_in":
            if C_GLR <= m0 < C_U:
                return "w_inG", None
            sn = "w_inA" if m0 < C_GLR else "w_inB"
        else:
            sn = name
        _, c0, _, _, _, _, bw, _ = sgeom(sn)
        assert (m0 - c0) % bw == 0, (name, m0)
        return sn, (m0 - c0) // bw

    def wblock_ap(sn, l, blk):
        _, _, _, _, _, KCn, bw, nblk = sgeom(sn)
        return dap(wb[sn], (l * nblk + blk) * 128 * KCn * bw, [[KCn * bw, 128], [1, KCn * bw]])
    xm = dscr("xm", [D, NTOK], F32)
    xn = dscr("xn", [D, NTOK], F32)
    xp = dscr("xp", [D, NTOK], F32)
    obs = dscr("obs", [D, NTOK], BF16)
    ybs = dscr("ybs", [512, NTOK], BF16)
    D1 = dscr("D1", [8, 512, 64], BF16)
    D2 = dscr("D2", [8, 512, 64], BF16)
    stabd = dscr("stabd", [DEPTH * 2 * 4, 128, 4096], BF16)
    abrd = dscr("abrd", [DEPTH * 2, 128, 1152], F32)

    P = Prog(nc, gs)

    ident_f = gsb("ident_f", [128, 128], F32)
    ident_b = gsb("ident_b", [128, 128], BF16)
    ones_b = gsb("ones_b", [128, 128], BF16)
    trim = gsb("trim", [128, 2, 128], BF16)
    bmask = gsb("bmask", [128, 2, 128], F32)
    rmask = gsb("rmask", [128, 512], F32)
    csm = gsb("csm_s", [128, 32], F32)
    pos = gsb("pos_s", [128, 512], F32)
    modv = gsb("modv", [128, DEPTH, 48, 2], F32)
    A1 = gsb("A1", [128, DEPTH, 8, 2], F32)
    A2 = gsb("A2", [128, DEPTH, 8, 2], F32)
    n1g = gsb("n1g_s", [128, DEPTH * 8], F32)
    n2g = gsb("n2g_s", [128, DEPTH * 8], F32)
    fng = gsb("fng_s", [128, 8], F32)
    bgneg = gsb("bgneg", [128, DEPTH * 2 * 4], F32)
    gng = gsb("gng_s", [128, DEPTH * 2], F32)
    bglu = gsb("bglu_s", [128, DEPTH * 4], F32)
    d8 = gsb("d8_s", [128, DEPTH * 32], F32)
    wg = gsb("wg", [16, DEPTH * 2 * 512], BF16)
    AAt = gsb("AAt", [128, DEPTH * 2, 2, 32], F32)
    BBt = gsb("BBt", [128, DEPTH * 2, 2, 32], F32)
    NPSF = 6
    psf = [gs.enter_context(nc.psum_tensor(f"ps{i}", [128, 512], F32)) for i in range(NPSF)]
    psbs = [gs.enter_context(nc.psum_tensor(f"psb{i}", [128, 1024], BF16)) for i in range(2)]

    def psum():
        i = P.ps_rr
        P.ps_rr = (i + 1) % NPSF
        return psf[i], ("ps", i)

    SGN = csm[:, 0:1]
    PHS = [csm[:, 1 + i:2 + i] for i in range(4)]
    PHC = [csm[:, 5:6], csm[:, 6:7]]

    with ExitStack() as ph:
        def sb(name, shape, dt):
            return ph.enter_context(nc.sbuf_tensor(f"{name}_p{P.phase_no}", list(shape), dt))

        for sn in STREAMS:
            src, c0, ncols, K, Msrc, KCn, bw, nblk = sgeom(sn)
            for l in range(DEPTH):
                for blk in range(nblk):
                    P.dma(out=dap(wb[sn], (l * nblk + blk) * 128 * KCn * bw, [[KCn * bw, 128], [bw, KCn], [1, bw]]),
                          in_=dap(wsrc[src], l * K * Msrc + c0 + blk * bw, [[Msrc, 128], [128 * Msrc, KCn], [1, bw]]),
                          r=(), w=[("wb", sn)], eng="pool")

        cstt = sb("cstt", [128, 1280], F32)
        P.dma(cstt[:], cst_d.ap(), (), ["cstt"])
        P.dma(csm[:], csm_d.ap(), (), ["csm"])
        P.dma(pos[:], pos_d.ap(), (), ["pos"])
        P.dma(n1g[:], n1g_d.ap(), (), ["n1g"])
        P.dma(n2g[:], n2g_d.ap(), (), ["n2g"])
        P.dma(fng[:], fng_d.ap(), (), ["fng"])
        P.dma(gng[:], gng_d.ap(), (), ["gng"])
        P.dma(bglu[:], bglu_d.ap(), (), ["bglu"])
        P.dma(d8[:], d8_d.ap(), (), ["d8"])
        bgt = sb("bgt", [128, DEPTH * 8], F32)
        P.dma(bgt[:], bg_d.ap(), (), ["bgt"])
        P.act(bgneg[:], bgt[:], AF.Copy, ["bgt"], ["bgneg"], scale=-1.0)
        wgf = sb("wgf", [16, DEPTH * 2 * 512], F32)
        P.dma(wgf[:], w_gu.ap(), (), ["wgf"])
        P.copy("dve", wg[:], wgf[:], ["wgf"], ["wg"])
        P.copy("dve", ident_f[:], cstt[:, 0:128], ["cstt"], ["ident_f"])
        P.copy("dve", ident_b[:], cstt[:, 0:128], ["cstt"], ["ident_b"])
        P.memset("dve", ones_b[:], 1.0, ["ones_b"])
        P.copy("dve", trim[:, 0, :], cstt[:, 128:256], ["cstt"], ["trim"])
        P.copy("dve", trim[:, 1, :], cstt[:, 256:384], ["cstt"], ["trim"])
        P.copy("dve", bmask[:, 0, :], cstt[:, 384:512], ["cstt"], ["bmask"])
        P.copy("dve", bmask[:, 1, :], cstt[:, 512:640], ["cstt"], ["bmask"])
        P.copy("dve", rmask[:], cstt[:, 640:1152], ["cstt"], ["rmask"])

        condt = sb("condt", [128, 16], F32)
        sct = sb("sct", [128, 8, 2], F32)
        bmod = sb("bmod", [128, DEPTH * 48], F32)
        wm = sb("wm", [128, 2, 8 * 512], F32)
        P.dma(condt[:], condT.ap(), (), ["condt"])
        P.dma(bmod[:], b_modT.ap(), (), ["bmod"])
        P.act(sct[:], condt[:].rearrange("p (k c) -> p k c", c=2), AF.Silu, ["condt"], ["sct"])
        for l in range(DEPTH):
            pm, pmk = psf[l], ("ps", l)
            for blk in range(12):
                slot = blk % 2
                P.dma(out=sap(wm, slot * 4096, [[512, 8], [1, 512]], 8192),
                      in_=dap(w_mod, l * D * 6 * D + blk * 512, [[6 * D, 128], [128 * 6 * D, 8], [1, 512]]),
                      r=(), w=[("wm", slot)])
                for mc4 in range(4):
                    mc = blk * 4 + mc4
                    for kc in range(8):
                        P.mm(pm[:, mc * 2:mc * 2 + 2],
                             sap(wm, slot * 4096 + kc * 512 + mc4 * 128, [[1, 128]], 8192),
                             sct[:, kc, :], kc == 0, kc == 7, [("wm", slot), "sct"], [pmk])
            P.tt("dve", modv[:, l, :, :], sap(pm, 0, [[2, 48], [1, 2]], 512),
                 sap(bmod, l * 48, [[1, 48], [0, 2]], DEPTH * 48), ALU.add, [pmk, "bmod"], ["modv"])
            for (Ax, ng, ngk, off) in ((A1, n1g, "n1g", 8), (A2, n2g, "n2g", 32)):
                P.ts("dve", Ax[:, l, :, :], modv[:, l, off:off + 8, :], 1.0, None, ALU.add, None, ["modv"], ["A12"])
                P.tt("dve", Ax[:, l, :, :], Ax[:, l, :, :], sap(ng, l * 8, [[1, 8], [0, 2]], DEPTH * 8), ALU.mult,
                     ["A12", ngk], ["A12"])

        lre = sb("lre", [128, DEPTH * 2 * 32], F32)
        lim = sb("lim", [128, DEPTH * 2 * 32], F32)
        lst = sb("lst", [128, DEPTH * 2 * 32], F32)
        P.dma(lre[:], lre_d.ap(), (), ["lre"])
        P.dma(lim[:], lim_d.ap(), (), ["lim"])
        P.dma(lst[:], lst_d.ap(), (), ["lst"])
        Bre = sb("Bre", [128, 512], F32)
        Bim = sb("Bim", [128, 512], F32)
        Cre = sb("Cre", [128, 512], F32)
        Cim = sb("Cim", [128, 512], F32)
        s32 = {n: sb("s5_" + n, [128, 32], F32) for n in
               ("st", "ar", "ai", "e1", "c1", "s1", "lbr", "lbi", "nr", "den", "t1", "t2", "wr", "wi", "m8", "th8")}
        s256 = {n: sb("s5v_" + n, [128, 288], F32) for n in
                ("th", "ea", "mgn", "mgp", "tA", "tB", "sn", "X1", "X2", "Y1", "Y2")}
        ni = sb("s5_ni", [128, 288], I32)
        abr_t = sb("abr_t", [128, 1152], F32)
        bbr = sb("bbr", [128, 512], F32)
        bbi = sb("bbi", [128, 512], F32)
        bt = sb("bt", [128, 512], F32)
        BZ = sb("BZ", [128, 4096], F32)
        QC = sb("QC", [128, 4096], F32)
        T3 = sb("T3", [128, 4096], F32)
        tWB = sb("tWB", [128, 4096], BF16)
        tWT = sb("tWT", [128, 4096], BF16)
        tM = sb("tM", [128, 4096], BF16)
        tQC = sb("tQC", [128, 4096], BF16)
        Mtmp = sb("Mtmp", [128, 512], F32)

        def sinr(out, outk, th, thk, phase, n):
            tA, tB = s256["tA"][:, 0:n], s256["tB"][:, 0:n]
            P.ts("dve", tA, th, phase, None, ALU.add, None, [thk, "csm"], ["tA"])
            P.ts("dve", ni[:, 0:n], tA, 1.0 / TWO_PI, None, ALU.mult, None, ["tA"], ["ni"])
            P.copy("dve", tB, ni[:, 0:n], ["ni"], ["tB"])
            P.stt(tA, tB, -TWO_PI, tA, ALU.mult, ALU.add, ["tA", "tB"], ["tA"])
            P.ts("dve", tA, tA, 3.14159, -3.14159, ALU.min, ALU.max, ["tA"], ["tA"])
            P.act(out, tA, AF.Sin, ["tA"], [outk])

        def mul(out, a, b, r, w):
            P.tt("dve", out, a, b, ALU.mult, r, w)

        for l in range(DEPTH):
            for dr in range(2):
                ld = l * 2 + dr
                sl = slice(ld * 32, ld * 32 + 32)
                for (t_s, t_d, nm) in ((Bre, Bre_d, "Bre"), (Bim, Bim_d, "Bim"), (Cre, Cre_d, "Cre"), (Cim, Cim_d, "Cim")):
                    P.dma(t_s[:], dap(t_d, ld * 512, [[DEPTH * 2 * 512, 128], [1, 512]]), (), [nm])
                S = {k: v[:] for k, v in s32.items()}
                P.act(S["st"], lst[:, sl], AF.Exp, ["lst"], ["st"])
                mul(S["ar"], lre[:, sl], S["st"], ["lre", "st"], ["ar"])
                mul(S["ai"], lim[:, sl], S["st"], ["lim", "st"], ["ai"])
                P.act(S["e1"], S["ar"], AF.Exp, ["ar"], ["e1"])
                sinr(S["c1"], "c1", S["ai"], "ai", PHC[0], 32)
                sinr(S["s1"], "s1", S["ai"], "ai", PHC[1], 32)
                mul(S["lbr"], S["e1"], S["c1"], ["e1", "c1"], ["lbr"])
                mul(S["lbi"], S["e1"], S["s1"], ["e1", "s1"], ["lbi"])
                P.ts("dve", S["nr"], S["lbr"], -1.0, None, ALU.add, None, ["lbr"], ["nr"])
                mul(S["t1"], lre[:, sl], lre[:, sl], ["lre"], ["t1"])
                mul(S["t2"], lim[:, sl], lim[:, sl], ["lim"], ["t2"])
                P.tt("dve", S["den"], S["t1"], S["t2"], ALU.add, ["t1", "t2"], ["den"])
                P.op("dve", lambda e, o=S["den"]: e.reciprocal(out=o, in_=o), ["den"], ["den"])
                mul(S["t1"], S["nr"], lre[:, sl], ["nr", "lre"], ["t1"])
                mul(S["t2"], S["lbi"], lim[:, sl], ["lbi", "lim"], ["t2"])
                P.tt("dve", S["wr"], S["t1"], S["t2"], ALU.add, ["t1", "t2"], ["wr"])
                mul(S["wr"], S["wr"], S["den"], ["wr", "den"], ["wr"])
                mul(S["t1"], S["lbi"], lre[:, sl], ["lbi", "lre"], ["t1"])
                mul(S["t2"], S["nr"], lim[:, sl], ["nr", "lim"], ["t2"])
                P.tt("dve", S["wi"], S["t1"], S["t2"], ALU.subtract, ["t1", "t2"], ["wi"])
                mul(S["wi"], S["wi"], S["den"], ["wi", "den"], ["wi"])
                wrb = sap(s32["wr"], 0, [[1, 32], [0, 16]], 32)
                wib = sap(s32["wi"], 0, [[1, 32], [0, 16]], 32)
                v3 = lambda t: t[:].rearrange("p (g m) -> p g m", m=16)
                mul(v3(bbr), v3(Bre), wrb, ["Bre", "wr"], ["bbr"])
                mul(v3(bt), v3(Bim), wib, ["Bim", "wi"], ["bt"])
                P.tt("dve", bbr[:], bbr[:], bt[:], ALU.subtract, ["bbr", "bt"], ["bbr"])
                mul(v3(bbi), v3(Bim), wrb, ["Bim", "wr"], ["bbi"])
                mul(v3(bt), v3(Bre), wib, ["Bre", "wi"], ["bt"])
                P.tt("dve", bbi[:], bbi[:], bt[:], ALU.add, ["bbi", "bt"], ["bbi"])
                P.act(S["m8"], S["ar"], AF.Exp, ["ar"], ["m8"], scale=8.0)
                P.ts("dve", S["th8"], S["ai"], 8.0, None, ALU.mult, None, ["ai"], ["th8"])
                sinr(S["c1"], "c1", S["th8"], "th8", PHC[0], 32)
                sinr(S["s1"], "s1", S["th8"], "th8", PHC[1], 32)
                mul(AAt[:, ld, 0, :], S["m8"], S["c1"], ["m8", "c1"], ["AAt"])
                mul(AAt[:, ld, 1, :], S["m8"], S["c1"], ["m8", "c1"], ["AAt"])
                mul(BBt[:, ld, 0, :], S["m8"], S["s1"], ["m8", "s1"], ["BBt"])
                P.ts("dve", BBt[:, ld, 1, :], BBt[:, ld, 0, :], -1.0, None, ALU.mult, None, ["BBt"], ["BBt"])
                W9 = {k: v[:, 0:288] for k, v in s256.items()}
                v9 = lambda t: t[:, 0:288].rearrange("p (j g) -> p j g", g=32)
                erb = sap(csm, 23, [[1, 9], [0, 32]], 32)
                mul(v9(s256["th"]), sap(s32["ai"], 0, [[0, 9], [1, 32]], 32), erb, ["ai", "csm"], ["th"])
                mul(v9(s256["ea"]), sap(s32["ar"], 0, [[0, 9], [1, 32]], 32), erb, ["ar", "csm"], ["ea"])
                P.act(W9["mgp"], W9["ea"], AF.Exp, ["ea"], ["mgp"])
                sinr(W9["sn"], "sn", W9["th"], "th", PHC[0], 288)
                mul(W9["X1"], W9["sn"], W9["mgp"], ["sn", "mgp"], ["X1"])
                sinr(W9["sn"], "sn", W9["th"], "th", PHC[1], 288)
                mul(W9["X2"], W9["sn"], W9["mgp"], ["sn", "mgp"], ["X2"])
                for (off_, src_, sc_) in ((0, "X1", 1.0), (32, "X1", 1.0), (64, "X2", 1.0), (96, "X2", -1.0)):
                    P.ts("dve", sap(abr_t, off_, [[128, 9], [1, 32]], 1152), v9(s256[src_]), sc_, None, ALU.mult, None, [src_], ["abr_t"])
                P.dma(dap(abrd, ld * 128 * 1152, [[1152, 128], [1, 1152]]), abr_t[:], ["abr_t"], [("abrd", ld)])
                V = {k: v[:, 0:256] for k, v in s256.items()}
                v2 = lambda t: t[:, 0:256].rearrange("p (j g) -> p j g", g=32)
                ejb = sap(csm, 7 + dr * 8, [[1, 8], [0, 32]], 32)
                mul(v2(s256["th"]), sap(s32["ai"], 0, [[0, 8], [1, 32]], 32), ejb, ["ai", "csm"], ["th"])
                mul(v2(s256["ea"]), sap(s32["ar"], 0, [[0, 8], [1, 32]], 32), ejb, ["ar", "csm"], ["ea"])
                P.act(V["mgn"], V["ea"], AF.Exp, ["ea"], ["mgn"], scale=-1.0)
                P.act(V["mgp"], V["ea"], AF.Exp, ["ea"], ["mgp"])
                for (nm, ph_i, mg) in (("X1", 0, "mgn"), ("X2", 1, "mgn"), ("Y1", 2, "mgp"), ("Y2", 3, "mgp")):
                    sinr(V["sn"], "sn", V["th"], "th", PHS[ph_i], 256)
                    mul(V[nm], V["sn"], V[mg], ["sn", mg], [nm])
                v4 = lambda t: t[:].rearrange("p (g j m) -> p g j m", j=8, m=16)
                xv = lambda t: sap(t, 0, [[1, 32], [32, 8], [0, 16]], 288)
                bv = lambda t: sap(t, 0, [[16, 32], [0, 8], [1, 16]], 512)
                mul(v4(BZ), xv(s256["X1"]), bv(bbr), ["X1", "bbr"], ["BZ"])
                mul(v4(T3), xv(s256["X2"]), bv(bbi), ["X2", "bbi"], ["T3"])
                P.tt("dve", BZ[:], BZ[:], T3[:], ALU.add, ["BZ", "T3"], ["BZ"])
                mul(v4(QC), xv(s256["Y1"]), bv(Cre), ["Y1", "Cre"], ["QC"])
                mul(v4(T3), xv(s256["Y2"]), bv(Cim), ["Y2", "Cim"], ["T3"])
                P.tt("dve", QC[:], QC[:], T3[:], ALU.add, ["QC", "T3"], ["QC"])
                P.copy("act", tQC[:], QC[:], ["QC"], ["tQC"])
                for gq in range(8):
                    pa, pak = psum()
                    for gi in range(4):
                        g = gq * 4 + gi
                        P.mm(pa[:, gi * 128:(gi + 1) * 128], BZ[:, g * 128:(g + 1) * 128], ident_f[:], True, True,
                             ["BZ", "ident_f"], [pak])
                    pv = pa[:].rearrange("p (a k) -> p a k", k=128)
                    gsl = slice(gq * 512, gq * 512 + 512)
                    P.copy("act", tWB[:, gsl], pa[:], [pak], ["tWB"])
                    tw = tWT[:, gsl].rearrange("p (a k) -> p a k", k=128)
                    P.act(tw[:, :, 0:64], pv[:, :, 64:128], AF.Copy, [pak], ["tWT"], scale=-1.0)
                    P.copy("dve", tw[:, :, 64:128], pv[:, :, 0:64], [pak], ["tWT"])
                    pb_, pbk = psum()
                    for gi in range(4):
                        g = gq * 4 + gi
                        P.mm(pb_[:, gi * 128:(gi + 1) * 128], BZ[:, g * 128:(g + 1) * 128], QC[:, g * 128:(g + 1) * 128],
                             True, True, ["BZ", "QC"], [pbk])
                    P.tt("dve", Mtmp[:].rearrange("p (a k) -> p a k", k=128), pb_[:].rearrange("p (a k) -> p a k", k=128),
                         sap(bmask, dr * 128, [[0, 4], [1, 128]], 256), ALU.mult, [pbk, "bmask"], ["Mtmp"])
                    for gi in range(4):
                        g = gq * 4 + gi
                        sc_ = d8[:, l * 32 + g:l * 32 + g + 1] if dr == 0 else 0.0
                        P.stt(tM[:, g * 128:(g + 1) * 128], ident_f[:], sc_, Mtmp[:, gi * 128:(gi + 1) * 128],
                              ALU.mult, ALU.add, ["ident_f", "d8", "Mtmp"], ["tM"])
                for k, (tt_, nm) in enumerate(((tWB, "tWB"), (tWT, "tWT"), (tM, "tM"), (tQC, "tQC"))):
                    P.dma(dap(stabd, (ld * 4 + k) * 128 * 4096, [[4096, 128], [1, 4096]]), tt_[:], [nm], [("stabd", ld)])
        P.flush()

    seqs = [(i * TP, TP, 0, i) for i in range(NP)] + [(NP * TP, TS, 1, -1)]

    def linear(wt, name, l, K, M, m0, m1, rhs_fn, N, evac, rkeys):
        KCn = K // 128
        sn, blk0 = wstream(name, m0)
        if blk0 is None:
            bw = 32
            blocks = [(0, m0 - C_GLR, m1 - m0)]
        else:
            bw = sgeom(sn)[6]
            assert (m1 - m0) % bw == 0
            blocks = [(blk0 + i, 0, bw) for i in range((m1 - m0) // bw)]
        done = 0
        for (blk, cofs, cuse) in blocks:
            slot = P.w_rr
            P.w_rr = (slot + 1) % P.nws
            P.dma(out=sap(wt, slot * 4096, [[1, KCn * bw]], P.nws * 4096), in_=wblock_ap(sn, l, blk),
                  r=(), w=[("wt", slot)], eng="pool")
            for c0 in range(cofs, cofs + cuse, 128):
                cw = min(128, cofs + cuse - c0)
                for n0 in range(0, N, 512):
                    nn = min(512, N - n0)
                    ps, psk = psum()
                    for kc in range(KCn):
                        P.mm(ps[0:cw, 0:nn], sap(wt, slot * 4096 + kc * bw + c0, [[1, cw]], P.nws * 4096), rhs_fn(kc, n0, nn),
                             kc == 0, kc == KCn - 1, [("wt", slot)] + rkeys, [psk])
                    evac(done // 128, ps, psk, n0, nn)
                done += cw

    def norm_mod(xt, xrow, ht, hrow, sq, rs, tmpf, Ax, shift0, l, cnd, N):
        xk = xrow or "xt"
        hk = hrow or "ht"
        for n0 in range(0, N, 512):
            nn = min(512, N - n0)
            pn, pnk = psum()
            for fc in range(8):
                i2 = fc % 2
                P.act(sq[:, i2, 0:nn], xt[:, fc, n0:n0 + nn], AF.Square, [(xk, fc)], [("sq", i2)])
                P.mm(pn[:, 0:nn], ones_b[:], sq[:, i2, 0:nn], fc == 0, fc == 7, [("sq", i2), "ones_b"], [pnk])
            P.act(rs[:, 0:nn], pn[:, 0:nn], AF.Ln, [pnk], ["rs"], scale=1.0 / D, bias=EPS)
            P.act(rs[:, 0:nn], rs[:, 0:nn], AF.Exp, ["rs"], ["rs"], scale=-0.5)
            for fc in range(8):
                i2 = fc % 2
                P.stt(tmpf[:, i2, 0:nn], xt[:, fc, n0:n0 + nn], Ax[:, l, fc, cnd:cnd + 1], rs[:, 0:nn], ALU.mult, ALU.mult,
                      [(xk, fc), "rs", "A12"], [("tmpf", i2)])
                P.act(ht[:, fc, n0:n0 + nn], tmpf[:, i2, 0:nn], AF.Identity, [("tmpf", i2), "modv"], [(hk, fc)],
                      bias=modv[:, l, shift0 + fc, cnd:cnd + 1])

    def mixer_phase(l, dr, passF, xsrc, xdst0):
        ld = l * 2 + dr
        TM = 512
        with ExitStack() as ph:
            def sb(name, shape, dt):
                return ph.enter_context(nc.sbuf_tensor(f"{name}_p{P.phase_no}", list(shape), dt))
            xt = sb("xt", [128, 8, TM], F32)
            ht = sb("ht", [128, 8, TM], BF16)
            sq = sb("sq", [128, 2, TM], BF16)
            rs = sb("rs", [128, TM], F32)
            tmpf = sb("tmpf", [128, 2, TM], F32)
            qk = sb("qk", [128, 8, TM], BF16)
            vT = sb("vT", [128, 4, 1024], BF16)
            glr = sb("glr", [16, TM], BF16)
            e1 = sb("e1", [128, 2, TM], F32)
            cums = sb("cums", [128, 2, TM], F32)
            ee = sb("ee", [128, 8, TM], BF16)
            etot = sb("etot", [128, 4, 4], F32)
            scm = sb("scm", [128, 16, 128], BF16)
            kT = sb("kT", [128, 8, 128], BF16)
            Sf = sb("Sf", [128, 4, 256], F32)
            Sb = sb("Sb", [128, 4, 256], BF16)
            ob = sb("ob", [128, 8, TM], BF16)
            Uj = sb("Uj", [128, 4, 512], BF16)
            U8 = sb("U8", [128, 2, 32, 64], BF16)
            DD = sb("DD", [128, 2, 32, 64], F32)
            Zt = sb("Zt", [128, 2, 64], F32)
            P1 = sb("P1", [128, 64], F32)
            P2 = sb("P2", [128, 64], F32)
            Tb = sb("Tb", [128, 512], F32)
            P2b = sb("P2b", [128, 512], F32)
            Zs = sb("Zs", [128, 9, 64], F32)
            CX = sb("CX", [128, 512], F32)
            CY = sb("CY", [128, 512], F32)
            abr = sb("abr", [128, 1152], F32)
            Sg = sb("Sg", [128, 32, 64], BF16)
            Yim = sb("Yim", [128, 4, TM], BF16)
            stab = sb("stab", [128, 4, 4096], BF16)
            P.nws = 3
            P.w_rr = 0
            wt = sb("wt", [128, P.nws, 4096], BF16)
            x0s = sb("x0s", [128, 32], F32)

            P.dma(abr[:], dap(abrd, ld * 128 * 1152, [[1152, 128], [1, 1152]]), (), ["abr"])
            sgt = sq
            for k in range(4):
                P.dma(stab[:, k, :], dap(stabd, (ld * 4 + k) * 128 * 4096, [[4096, 128], [1, 4096]]), (), ["stab"])

            items = []
            for (tok0_, T_, cnd_, pidx_) in seqs:
                TT_ = min(TM, T_)
                nt_ = T_ // TT_
                for ti_ in (range(nt_) if dr == 0 else range(nt_ - 1, -1, -1)):
                    items.append((tok0_ + ti_ * TT_, TT_))
            XKall = [("xt", fc) for fc in range(8)]

            def load_x(k):
                t0_, TT_ = items[k]
                P.dma(xt[:, :, 0:TT_], dap(xsrc, t0_, [[NTOK, 128], [128 * NTOK, 8], [1, TT_]]), (), XKall)
            load_x(0)

            Y8v = e1.bitcast(BF16)
            titems = []
            for (tok0_, T_, cnd_, pidx_) in seqs:
                TT_ = min(TM, T_)
                nt_ = T_ // TT_
                order_ = list(range(nt_)) if dr == 0 else list(range(nt_ - 1, -1, -1))
                for n_, ti_ in enumerate(order_):
                    titems.append((tok0_ + ti_ * TT_, TT_, cnd_, pidx_, ti_, n_ == 0, n_ == nt_ - 1))

            def init_states(pidx):
                if pidx < 0:
                    P.dma(Sf[:], dap(gla0_d, ld * 4 * 128 * 256, [[256, 128], [128 * 256, 4], [1, 256]]), (),
                          [("Sf", h) for h in range(4)])
                    P.copy("dve", Sb[:], Sf[:], [("Sf", h) for h in range(4)], [("Sb", h) for h in range(4)])
                    P.dma(Zt[:, 0, 0:32], x0_d.ap()[:, ld * 32:ld * 32 + 32], (), [("Zt", 0)])
                    P.dma(x0s[:], x0s_d.ap()[:, ld * 32:ld * 32 + 32], (), ["x0s"])
                    P.ts("dve", Zt[:, 0, 32:64], x0s[:], SGN, None, ALU.mult, None, ["x0s", "csm"], [("Zt", 0)])
                else:
                    P.memset("dve", Sf[:], 0.0, [("Sf", h) for h in range(4)])
                    P.memset("dve", Sb[:], 0.0, [("Sb", h) for h in range(4)])
                    P.memset("dve", Zt[:, 0, :], 0.0, [("Zt", 0)])

            def tile_gen(k):
                (t0, TT, cnd, pidx, ti, first, lastt) = titems[k]
                NC = TT // 8
                nch = TT // 128
                ub = k % 2
                XK = [("xt", fc) for fc in range(8)]
                HK = [("ht", fc) for fc in range(8)]
                if l == 0 and pidx < 0 and not passF:
                    r0 = (ti * TT) // 64
                    nr_ = TT // 64
                    P.tt("dve", sap(xt, 0, [[TM, 4], [64, nr_], [1, 64]], 8 * TM), sap(xt, 0, [[TM, 4], [64, nr_], [1, 64]], 8 * TM),
                         sap(pos, r0, [[64, 4], [1, nr_], [0, 64]], 512), ALU.add, XK[0:4] + ["pos"], XK[0:4])
                    P.tt("dve", sap(xt, 4 * TM, [[TM, 4], [64, nr_], [1, 64]], 8 * TM),
                         sap(xt, 4 * TM, [[TM, 4], [64, nr_], [1, 64]], 8 * TM),
                         sap(pos, 256, [[64, 4], [0, nr_], [1, 64]], 512), ALU.add, XK[4:8] + ["pos"], XK[4:8])
                if xdst0 is not None:
                    P.dma(dap(xdst0, t0, [[NTOK, 128], [128 * NTOK, 8], [1, TT]]), xt[:, :, 0:TT], XK, ["xp"])
                norm_mod(xt, None, ht, None, sq, rs, tmpf, A1, 0, l, cnd, TT)
                if k + 1 < len(items):
                    load_x(k + 1)
                yield
                rhs_h = lambda kc, n0, nn: ht[:, kc, n0:n0 + nn]

                slot = P.w_rr
                P.w_rr = (slot + 1) % P.nws
                P.dma(out=sap(wt, slot * 4096, [[1, 4096]], P.nws * 4096), in_=wblock_ap("w_inB", l, 0),
                      r=(), w=[("wt", slot)], eng="pool")
                for uc in range(4):
                    ps, psk = psum()
                    for kc in range(8):
                        P.mm(ps[:, 0:TT], sap(wt, slot * 4096 + kc * 512 + uc * 128, [[1, 128]], P.nws * 4096),
                             sap(ht, kc * TM, [[1, 8], [8, NC]], 8 * TM), kc == 0, kc == 7, [("wt", slot), ("ht", kc)], [psk])
                    P.copy("act", Uj[:, uc, 0:TT], ps[:, 0:TT], [psk], [("Uj", uc)])
                for uc in range(4):
                    P.dma(dap(D1, uc * 128 * 64, [[64, 128], [512 * 64, 8], [1, NC]]), sap(Uj, uc * 512, [[NC, 8], [1, NC]], 2048),
                          [("Uj", uc)], ["D1"])
                for j in range(8):
                    P.dma(sapp(U8, 16 * j, 16, ub * 2048, [[64, 32], [1, NC]], 4096), dap(D1, j * 512 * 64, [[64, 16], [16 * 64, 32], [1, NC]]),
                          ["D1"], [("U8", ub)])

                def ev_qk(mc, ps, psk, n0, nn):
                    if mc < 4:
                        P.act(qk[:, mc, 0:nn], ps[:, 0:nn], AF.Copy, [psk], [("qk", mc)], scale=DK ** -0.5)
                    else:
                        P.copy("act", qk[:, mc, 0:nn], ps[:, 0:nn], [psk], [("qk", mc)])
                linear(wt, "w_in", l, D, DIN, C_Q, C_K + 512, rhs_h, TT, ev_qk, HK)

                def ev_glr(mc, ps, psk, n0, nn):
                    P.copy("act", glr[0:16, 0:nn], ps[0:16, 0:nn], [psk], ["glr"])
                linear(wt, "w_in", l, D, DIN, C_GLR + 16 * dr, C_GLR + 16 * dr + 16, rhs_h, TT, ev_glr, HK)

                for half in range(2):
                    slot = P.w_rr
                    P.w_rr = (slot + 1) % P.nws
                    P.dma(out=sap(wt, slot * 4096, [[1, 4096]], P.nws * 4096), in_=wblock_ap("w_inA", l, 2 + half),
                          r=(), w=[("wt", slot)], eng="pool")
                    for tc in range(nch):
                        ps, psk = psum()
                        for kc in range(8):
                            P.mm(ps[:, 0:512], ht[:, kc, tc * 128:(tc + 1) * 128], sap(wt, slot * 4096 + kc * 512, [[1, 512]], P.nws * 4096),
                                 kc == 0, kc == 7, [("wt", slot), ("ht", kc)], [psk])
                        P.copy("act", vT[:, tc, half * 512:(half + 1) * 512], ps[:, 0:512],
                               [psk], [("vT", tc)])

                yield
                if first:
                    init_states(pidx)
                for hh in range(4):
                    s2 = hh % 2
                    pz, pzk = psum()
                    P.mm(pz[:, 0:TT], wg[0:16, ld * 512 + hh * 128:ld * 512 + (hh + 1) * 128], glr[0:16, 0:TT], True, True,
                         ["wg", "glr"], [pzk])
                    P.act(e1[:, s2, 0:TT], pz[:, 0:TT], AF.Exp, [pzk, "bgneg"], [("e1", s2)], scale=-1.0,
                          bias=bgneg[:, ld * 4 + hh:ld * 4 + hh + 1])
                    P.act(e1[:, s2, 0:TT], e1[:, s2, 0:TT], AF.Ln, [("e1", s2)], [("e1", s2)], bias=1.0)
                    if dr == 0:
                        d1_ap, o_ap = e1[:, s2, 0:TT], cums[:, s2, 0:TT]
                    else:
                        d1_ap = sap(e1, s2 * TM + TT - 1, [[-1, TT]], 2 * TM)
                        o_ap = sap(cums, s2 * TM + TT - 1, [[-1, TT]], 2 * TM)
                    P.op("dve", lambda e, o=o_ap, d1=d1_ap, n=TT: e.tensor_tensor_scan(
                        out=o, data0=rmask[:, 0:n], data1=d1, initial=0.0, op0=ALU.mult, op1=ALU.add),
                        [("e1", s2), "rmask"], [("cums", s2)])
                    P.act(ee[:, hh, 0:TT], cums[:, s2, 0:TT], AF.Exp, [("cums", s2)], [("ee", hh)], scale=-1.0 / 16)
                    P.act(ee[:, 4 + hh, 0:TT], cums[:, s2, 0:TT], AF.Exp, [("cums", s2)], [("ee", 4 + hh)], scale=1.0 / 16)
                    P.act(etot[:, hh, 0:nch], sap(cums, s2 * TM + (127 if dr == 0 else 0), [[128, nch]], 2 * TM), AF.Exp,
                          [("cums", s2)], ["etot"], scale=-1.0 / 16)
                    P.tt("pool", qk[:, hh, 0:TT], qk[:, hh, 0:TT], ee[:, hh, 0:TT], ALU.mult, [("qk", hh), ("ee", hh)], [("qk", hh)])
                    P.tt("dve", qk[:, 4 + hh, 0:TT], qk[:, 4 + hh, 0:TT], ee[:, 4 + hh, 0:TT], ALU.mult,
                         [("qk", 4 + hh), ("ee", 4 + hh)], [("qk", 4 + hh)])

                OK_ = [("ob", h) for h in range(4)]
                if passF:
                    P.dma(ob[:, :, 0:TT], dap(obs, t0, [[NTOK, 128], [128 * NTOK, 8], [1, TT]]), (), OK_)

                corder = list(range(nch)) if dr == 0 else list(range(nch - 1, -1, -1))

                def stage1m(c):
                    cs = slice(c * 128, (c + 1) * 128)
                    for hh in range(4):
                        ix = c * 4 + hh
                        psc, psck = psum()
                        P.mm(psc[:, 0:128], qk[:, 4 + hh, cs], qk[:, hh, cs], True, True, [("qk", 4 + hh), ("qk", hh)], [psck])
                        P.tt("dve", scm[:, ix, :], psc[:, 0:128], trim[:, dr, :], ALU.mult, [psck, "trim"], [("scm", ix)])

                def stage1(c):
                    cs = slice(c * 128, (c + 1) * 128)
                    for hh in range(4):
                        ix = (c % 2) * 4 + hh
                        i2 = ix % 2
                        P.transpose(psbs[i2][:, 0:128], qk[:, 4 + hh, cs], ident_b[:], [("qk", 4 + hh), "ident_b"],
                                    [("psb", i2)])
                        P.copy("act", kT[:, ix, :], psbs[i2][:, 0:128], [("psb", i2)], [("kT", ix)])

                for c_ in corder:
                    stage1m(c_)
                stage1(corder[0])

                for gq in range(4):
                    pa, pak = psum()
                    pb_, pbk = psum()
                    for gi in range(8):
                        g = gq * 8 + gi
                        P.mm(pa[:, gi * NC:(gi + 1) * NC], stab[:, 0, g * 128:(g + 1) * 128], U8[:, ub, g, 0:NC], True, True,
                             ["stab", ("U8", ub)], [pak])
                    for gi in range(8):
                        g = gq * 8 + gi
                        P.mm(pb_[:, gi * NC:(gi + 1) * NC], stab[:, 1, g * 128:(g + 1) * 128], U8[:, ub, g, 0:NC], True, True,
                             ["stab", ("U8", ub)], [pbk])
                    P.copy("act", DD[:, 0, gq * 8:(gq + 1) * 8, 0:NC], pa[:, 0:8 * NC].rearrange("p (a c) -> p a c", c=NC), [pak], ["DD"])
                    P.copy("act", DD[:, 1, gq * 8:(gq + 1) * 8, 0:NC], pb_[:, 0:8 * NC].rearrange("p (a c) -> p a c", c=NC), [pbk], ["DD"])

                NB = NC // 8
                cst_ = 8 if dr == 0 else -8
                rs_ = 1 if dr == 0 else -1
                off = (lambda r, b0: r + 8 * b0) if dr == 0 else (lambda r, b0: NC - 1 - r - 8 * b0)
                dd_full = lambda r, nb: sap(DD, off(r, 0), [[2048, 2], [64, 32], [cst_, nb]], 4096)
                dd_swap = lambda r, nb: sap(DD, off(r, 0) + 2048, [[-2048, 2], [64, 32], [cst_, nb]], 4096)
                AAb = lambda r, nb: sap(abr, r * 128, [[32, 2], [1, 32], [0, nb]], 1152)
                BBb = lambda r, nb: sap(abr, r * 128 + 64, [[32, 2], [1, 32], [0, nb]], 1152)
                Tv = sap(Tb, 0, [[256, 2], [8, 32], [1, NB]], 512)
                Tsw = sap(Tb, 256, [[-256, 2], [8, 32], [1, NB]], 512)
                P2v = sap(P2b, 0, [[256, 2], [8, 32], [1, NB]], 512)
                v3 = lambda t: t[:].rearrange("p (a g) -> p a g", g=32)

                def scan_gen():
                    for r in range(8):
                        if r == 0:
                            P.tt("dve", P2v, dd_swap(0, NB), BBb(1, NB), ALU.mult, ["DD", "abr"], ["P2b"])
                            P.tt("dve", Tv, dd_full(0, NB), AAb(1, NB), ALU.mult, ["DD", "abr"], ["Tb"])
                        else:
                            P.tt("dve", Tv, dd_full(r - 1, NB), dd_full(r, NB), ALU.add, ["DD"], ["Tb"])
                            P.tt("dve", P2v, Tsw, BBb(1, NB), ALU.mult, ["Tb", "abr"], ["P2b"])
                            P.tt("dve", Tv, Tv, AAb(1, NB), ALU.mult, ["Tb", "abr"], ["Tb"])
                        P.tt("dve", dd_full(r, NB), Tv, P2v, ALU.add, ["Tb", "P2b"], ["DD"])
                        yield
                    P.copy("dve", Zs[:, 0, :], Zt[:, 0, :], [("Zt", 0)], ["Zs"])
                    for b in range(NB):
                        P.tt("dve", P1[:], Zs[:, b, :], sap(abr, 8 * 128, [[1, 64]], 1152), ALU.mult, ["Zs", "abr"], ["P1"])
                        P.tt("dve", v3(P2), sap(Zs, b * 64 + 32, [[-32, 2], [1, 32]], 576), sap(abr, 8 * 128 + 64, [[32, 2], [1, 32]], 1152),
                             ALU.mult, ["Zs", "abr"], ["P2"])
                        P.tt("dve", P1[:], P1[:], P2[:], ALU.add, ["P1", "P2"], ["P1"])
                        P.tt("dve", sap(Zs, (b + 1) * 64, [[32, 2], [1, 32]], 576), v3(P1), sap(DD, off(7, b), [[2048, 2], [64, 32]], 4096),
                             ALU.add, ["P1", "DD"], ["Zs"])
                        yield
                    for gqr in range(4):
                        g0 = gqr * 8
                        CXv = sap(CX, 0, [[NB * 8, 8], [8, NB], [1, 8]], 512)
                        CYv = sap(CY, 0, [[NB * 8, 8], [8, NB], [1, 8]], 512)
                        P.tt("dve", CXv, sap(Zs, g0, [[1, 8], [64, NB], [0, 8]], 576), sap(abr, g0, [[1, 8], [0, NB], [128, 8]], 1152),
                             ALU.mult, ["Zs", "abr"], ["CX"])
                        P.tt("dve", CYv, sap(Zs, 32 + g0, [[1, 8], [64, NB], [0, 8]], 576), sap(abr, 64 + g0, [[1, 8], [0, NB], [128, 8]], 1152),
                             ALU.mult, ["Zs", "abr"], ["CY"])
                        P.tt("dve", CXv, CXv, CYv, ALU.add, ["CX", "CY"], ["CX"])
                        P.copy("dve", sap(Sg, off(0, 0) + g0 * 64, [[64, 8], [cst_, NB]], 2048), sap(CX, 0, [[NB * 8, 8], [8, NB]], 512),
                               ["CX"], ["Sg"])
                        P.tt("dve", sap(Sg, off(1, 0) + g0 * 64, [[64, 8], [cst_, NB], [rs_, 7]], 2048),
                             sap(DD, off(0, 0) + g0 * 64, [[64, 8], [cst_, NB], [rs_, 7]], 4096),
                             sap(CX, 1, [[NB * 8, 8], [8, NB], [1, 7]], 512), ALU.add, ["DD", "CX"], ["Sg"])
                        yield
                    P.copy("dve", Zt[:, 0, :], Zs[:, NB, :], ["Zs"], [("Zt", 0)])

                chain = scan_gen()

                def pump(n):
                    for _ in range(n):
                        if next(chain, "done") == "done":
                            return

                def chain_finish():
                    pump(64)

                yield
                if passF:
                    def ev_ga(mc, ps, psk, n0, nn):
                        P.act(ee[:, mc, 0:nn], ps[:, 0:nn], AF.Sigmoid, [psk], [("ee", mc)])
                    linear(wt, "w_in", l, D, DIN, C_GA, C_GA + 1024, rhs_h, TT, ev_ga, HK)
                for ci, c in enumerate(corder):
                    cs = slice(c * 128, (c + 1) * 128)
                    if ci + 1 < nch:
                        stage1(corder[ci + 1])
                    for hh in range(4):
                        ix = (c % 2) * 4 + hh
                        sx = c * 4 + hh
                        po, pok = psum()
                        for v2 in range(2):
                            P.mm(po[:, v2 * 128:(v2 + 1) * 128], vT[:, c, hh * 256 + v2 * 128:hh * 256 + (v2 + 1) * 128],
                                 scm[:, sx, :], True, False, [("vT", c), ("scm", sx)], [pok])
                            P.mm(po[:, v2 * 128:(v2 + 1) * 128], Sb[:, hh, v2 * 128:(v2 + 1) * 128], qk[:, hh, cs], False, not passF,
                                 [("Sb", hh), ("qk", hh)], [pok])
                            if passF:
                                P.mm(po[:, v2 * 128:(v2 + 1) * 128], ident_b[:], ob[:, hh * 2 + v2, cs], False, True,
                                     ["ident_b", ("ob", hh)], [pok])
                        o_out = ob[:, hh * 2:hh * 2 + 2, cs]
                        o_in = po[:, 0:256].rearrange("p (a k) -> p a k", k=128)
                        P.copy("act", o_out, o_in, [pok], [("ob", hh)])
                        pd, pdk = psum()
                        P.mm(pd[:, 0:256], kT[:, ix, :], vT[:, c, hh * 256:(hh + 1) * 256], True, False, [("kT", ix), ("vT", c)], [pdk])
                        P.mm(pd[:, 0:256], ident_f[:], Sf[:, hh, :], False, True, ["ident_f", ("Sf", hh)], [pdk])
                        P.act(Sf[:, hh, :], pd[:, 0:256], AF.Identity, [pdk, "etot"], [("Sf", hh)], scale=etot[:, hh, c:c + 1])
                        P.act(Sb[:, hh, :], pd[:, 0:256], AF.Identity, [pdk, "etot"], [("Sb", hh)], scale=etot[:, hh, c:c + 1])
                        pump(2)

                def s5_out_block():
                    chain_finish()
                    for gq in range(4):
                        py, pyk = psum()
                        for gi in range(8):
                            g = gq * 8 + gi
                            P.mm(py[:, gi * NC:(gi + 1) * NC], stab[:, 2, g * 128:(g + 1) * 128], U8[:, ub, g, 0:NC], True, False,
                                 ["stab", ("U8", ub)], [pyk])
                            P.mm(py[:, gi * NC:(gi + 1) * NC], stab[:, 3, g * 128:(g + 1) * 128], Sg[:, g, 0:NC], False, True,
                                 ["stab", "Sg"], [pyk])
                        P.copy("act", sap(Y8v, gq * 512, [[64, 8], [1, NC]], 2048),
                               py[:, 0:8 * NC].rearrange("p (a c) -> p a c", c=NC), [pyk], [("e1", 0), ("e1", 1)])
                    for i in range(8):
                        P.dma(dap(D2, i * 512 * 64, [[64, 16], [16 * 64, 32], [1, NC]]), sapp(Y8v, 16 * i, 16, 0, [[64, 32], [1, NC]], 2048),
                              [("e1", 0), ("e1", 1)], ["D2"])
                    for uc in range(4):
                        P.dma(sap(Yim, uc * TM, [[NC, 8], [1, NC]], 4 * TM), dap(D2, uc * 128 * 64, [[64, 128], [512 * 64, 8], [1, NC]]),
                              ["D2"], ["Yim"])


                yield
                if not passF:
                    s5_out_block()
                    P.dma(dap(ybs, t0, [[NTOK, 128], [128 * NTOK, 4], [1, TT]]), Yim[:, :, 0:TT], ["Yim"], ["ybs"])
                    P.dma(dap(obs, t0, [[NTOK, 128], [128 * NTOK, 8], [1, TT]]), ob[:, :, 0:TT], OK_, ["obs"])
                else:

                    VK = [("vT", i) for i in range(4)]
                    rs2 = [e1[:, 0, 0:TT], e1[:, 1, 0:TT], cums[:, 0, 0:TT], cums[:, 1, 0:TT]]
                    rs2k = [("e1", 0), ("e1", 1), ("cums", 0), ("cums", 1)]
                    for hh in range(4):
                        pn, pnk = psum()
                        for v2 in range(2):
                            P.act(sq[:, v2, 0:TT], ob[:, hh * 2 + v2, 0:TT], AF.Square, [("ob", hh)], [("sq", v2)])
                            P.mm(pn[:, 0:TT], ones_b[:], sq[:, v2, 0:TT], v2 == 0, v2 == 1, [("sq", v2), "ones_b"], [pnk])
                        P.act(rs2[hh], pn[:, 0:TT], AF.Ln, [pnk], [rs2k[hh]], scale=1.0 / DV, bias=EPS)
                        P.act(rs2[hh], rs2[hh], AF.Exp, [rs2k[hh]], [rs2k[hh]], scale=-0.5)

                    def ev_r(vc, ps, psk, n0, nn):
                        i2 = vc % 2
                        hh = vc // 2
                        P.act(sgt[:, i2, 0:nn], ps[:, 0:nn], AF.Silu, [psk], [("sq", i2)])
                        P.stt(tmpf[:, i2, 0:nn], ob[:, vc, 0:nn], gng[:, l * 2 + i2:l * 2 + i2 + 1], rs2[hh], ALU.mult, ALU.mult,
                              [("ob", hh), "gng", rs2k[hh]], [("tmpf", i2)])
                        P.tt("dve", ob[:, vc, 0:nn], tmpf[:, i2, 0:nn], sgt[:, i2, 0:nn], ALU.mult, [("tmpf", i2), ("sq", i2)], [("ob", hh)])
                    linear(wt, "w_in", l, D, DIN, C_R, C_R + 1024, rhs_h, TT, ev_r, HK)


                    s5_out_block()

                    def ev_pg(mc, ps, psk, n0, nn):
                        P.tt("dve", qk[:, mc, 0:nn], ps[:, 0:nn], ee[:, mc, 0:nn], ALU.mult, [psk, ("ee", mc)], [("qk", mc)])
                    linear(wt, "w_pg", l, D, D, 0, D, lambda kc, n0, nn: ob[:, kc, n0:n0 + nn], TT, ev_pg, OK_)

                    def ev_gb(mc, ps, psk, n0, nn):
                        P.act(ee[:, mc, 0:nn], ps[:, 0:nn], AF.Sigmoid, [psk], [("ee", mc)])
                    linear(wt, "w_in", l, D, DIN, C_GB, C_GB + 1024, rhs_h, TT, ev_gb, HK)
                    yield

                    ybv = sap(vT, 2048, [[TM, 4], [1, TT]], 4096)
                    P.dma(ybv, dap(ybs, t0, [[NTOK, 128], [128 * NTOK, 4], [1, TT]]), (), VK[2:4])
                    P.tt("dve", ybv, Yim[:, :, 0:TT], ybv, ALU.add, ["Yim"] + VK[2:4], VK[2:4])
                    P.act(Yim[:, :, 0:TT], ybv, AF.Gelu_apprx_tanh, VK[2:4], ["Yim"])

                    def ev_glu(mc, ps, psk, n0, nn):
                        i2 = mc % 2
                        P.act(sgt[:, i2, 0:nn], ps[:, 0:nn], AF.Sigmoid, [psk, "bglu"], [("sq", i2)], bias=bglu[:, l * 4 + mc:l * 4 + mc + 1])
                        P.tt("dve", sap(vT, mc * TM, [[1, nn]], 4096), Yim[:, mc, 0:nn], sgt[:, i2, 0:nn], ALU.mult,
                             ["Yim", ("sq", i2)], [("vT", mc // 2)])
                    linear(wt, "w_glu", l, 512, 512, 0, 512, lambda kc, n0, nn: Yim[:, kc, n0:n0 + nn], TT, ev_glu, ["Yim"])

                    def ev_ps(mc, ps, psk, n0, nn):
                        i2 = mc % 2
                        P.tt("dve", tmpf[:, i2, 0:nn].rearrange("p (c i) -> p c i", i=8), sap(ps, 0, [[1, NC], [NC, 8]], 512),
                             ee[:, mc, 0:nn].rearrange("p (c i) -> p c i", i=8), ALU.mult, [psk, ("ee", mc)], [("tmpf", i2)])
                        P.tt("dve", qk[:, mc, 0:nn], tmpf[:, i2, 0:nn], qk[:, mc, 0:nn], ALU.add, [("tmpf", i2), ("qk", mc)], [("qk", mc)])
                    linear(wt, "w_ps", l, 512, D, 0, D, lambda kc, n0, nn: sap(vT, kc * TM + n0, [[1, nn]], 4096), TT, ev_ps, VK[0:2])

                    def ev_out(mc, ps, psk, n0, nn):
                        i2 = mc % 2
                        P.dma(tmpf[:, i2, 0:nn], dap(xsrc, mc * 128 * NTOK + t0, [[NTOK, 128], [1, nn]]), (), [("tmpf", i2)])
                        P.stt(tmpf[:, i2, 0:nn], ps[:, 0:nn], modv[:, l, 16 + mc, cnd:cnd + 1], tmpf[:, i2, 0:nn], ALU.mult, ALU.add,
                              [psk, "modv", ("tmpf", i2)], [("tmpf", i2)])
                        P.dma(dap(xm, mc * 128 * NTOK + t0, [[NTOK, 128], [1, nn]]), tmpf[:, i2, 0:nn], [("tmpf", i2)], ["xm"])
                    linear(wt, "w_out", l, D, D, 0, D, lambda kc, n0, nn: qk[:, kc, n0:n0 + nn], TT, ev_out, [("qk", i) for i in range(8)])

                if lastt and pidx >= 0:
                    so = (pidx * DEPTH + l) * 2 + dr
                    P.dma(dap(gla_out, so * 4 * 128 * 256, [[256, 128], [128 * 256, 4], [1, 256]]), Sf[:],
                          [("Sf", h) for h in range(4)], ["gla_out"])
                    P.dma(s5_out.ap()[:, so * 32:so * 32 + 32], Zt[:, 0, 0:32], [("Zt", 0)], ["s5_out"])

            ntl = len(titems)
            gens = {0: tile_gen(0)}
            next(gens[0])
            next(gens[0])
            for k in range(ntl):
                g = gens[k]
                nxt = None
                if k + 1 < ntl:
                    nxt = gens[k + 1] = tile_gen(k + 1)
                next(g)
                if not passF and nxt is not None:
                    next(nxt)
                next(g)
                if not passF and nxt is not None:
                    next(nxt)
                if passF:
                    next(g)
                    if nxt is not None:
                        next(nxt)
                    next(g, None)
                    if nxt is not None:
                        next(nxt)
                else:
                    next(g, None)
            P.flush()

    def mlp_phase(l, last):
        TM = 1024
        segs = []
        if NP > 0:
            segs.append((0, NP * TP, 0))
        segs.append((NP * TP, TS, 1))
        with ExitStack() as ph:
            def sb(name, shape, dt):
                return ph.enter_context(nc.sbuf_tensor(f"{name}_p{P.phase_no}", list(shape), dt))
            xts = [sb("xt2a", [128, 8, TM], F32), sb("xt2b", [128, 8, TM], F32)]
            hts = [sb("ht2a", [128, 8, TM], BF16), sb("ht2b", [128, 8, TM], BF16)]
            hid = sb("hid", [128, 32, TM], BF16)
            sq = sb("sq2", [128, 2, 512], BF16)
            rs = sb("rs_m", [128, 512], F32)
            tmpf = sb("tmpf2", [128, 2, 512], F32)
            rl = sb("rl", [128, 2, 512], BF16)
            P.nws = 2
            P.w_rr = 0
            wt = sb("wt2", [128, P.nws, 4096], BF16)
            dst = yT if last else xn
            tiles = []
            for (s0, slen, cnd) in segs:
                for t0 in range(s0, s0 + slen, TM):
                    tiles.append((t0, min(TM, s0 + slen - t0), cnd))

            def front(k):
                t0, TT, cnd = tiles[k]
                sl2 = k % 2
                xt, ht = xts[sl2], hts[sl2]
                xkn, hkn = f"xt{sl2}", f"ht{sl2}"
                P.dma(xt[:, :, 0:TT], dap(xm, t0, [[NTOK, 128], [128 * NTOK, 8], [1, TT]]), (), [(xkn, fc) for fc in range(8)])
                norm_mod(xt, xkn, ht, hkn, sq, rs, tmpf, A2, 24, l, cnd, TT)

            front(0)
            for k, (t0, TT, cnd) in enumerate(tiles):
                sl2 = k % 2
                xt, ht = xts[sl2], hts[sl2]
                xkn, hkn = f"xt{sl2}", f"ht{sl2}"
                XK = [(xkn, fc) for fc in range(8)]
                HK = [(hkn, fc) for fc in range(8)]

                def ev_ff1(mc, ps, psk, n0, nn):
                    i2 = (mc + n0 // 512) % 2
                    P.act(rl[:, i2, 0:nn], ps[:, 0:nn], AF.Relu, [psk], [("rl", i2)])
                    P.tt("dve", hid[:, mc, n0:n0 + nn], rl[:, i2, 0:nn], rl[:, i2, 0:nn], ALU.mult, [("rl", i2)], [("hid", mc)])
                linear(wt, "w_ff1", l, D, DFF, 0, DFF, lambda kc, n0, nn: ht[:, kc, n0:n0 + nn], TT, ev_ff1, HK)

                if k + 1 < len(tiles):
                    front(k + 1)

                def ev_ff2(mc, ps, psk, n0, nn):
                    P.stt(xt[:, mc, n0:n0 + nn], ps[:, 0:nn], modv[:, l, 40 + mc, cnd:cnd + 1], xt[:, mc, n0:n0 + nn], ALU.mult, ALU.add,
                          [psk, "modv", (xkn, mc)], [(xkn, mc)])
                linear(wt, "w_ff2", l, DFF, D, 0, D, lambda kc, n0, nn: hid[:, kc, n0:n0 + nn], TT, ev_ff2,
                       [("hid", i) for i in range(32)])
                if last:
                    for n0 in range(0, TT, 512):
                        nn = min(512, TT - n0)
                        pn, pnk = psum()
                        for fc in range(8):
                            i2 = fc % 2
                            P.act(sq[:, i2, 0:nn], xt[:, fc, n0:n0 + nn], AF.Square, [(xkn, fc)], [("sq", i2)])
                            P.mm(pn[:, 0:nn], ones_b[:], sq[:, i2, 0:nn], fc == 0, fc == 7, [("sq", i2), "ones_b"], [pnk])
                        P.act(rs[:, 0:nn], pn[:, 0:nn], AF.Ln, [pnk], ["rs"], scale=1.0 / D, bias=EPS)
                        P.act(rs[:, 0:nn], rs[:, 0:nn], AF.Exp, ["rs"], ["rs"], scale=-0.5)
                        for fc in range(8):
                            P.stt(xt[:, fc, n0:n0 + nn], xt[:, fc, n0:n0 + nn], fng[:, fc:fc + 1], rs[:, 0:nn], ALU.mult, ALU.mult,
                                  [(xkn, fc), "fng", "rs"], [(xkn, fc)])
                P.dma(dap(dst, t0, [[NTOK, 128], [128 * NTOK, 8], [1, TT]]), xt[:, :, 0:TT], XK, ["dst"])
            P.flush()

    for l in range(DEPTH):
        if l == 0:
            mixer_phase(l, 1, False, xT, xp)
            mixer_phase(l, 0, True, xp, None)
        else:
            mixer_phase(l, 1, False, xn, None)
            mixer_phase(l, 0, True, xn, None)
        mlp_phase(l, l == DEPTH - 1)
    P.final_wait()
    build.n_inst = P.n_inst
    return nc


def _fm_vec(v):
    v = np.asarray(v, np.float32)
    lead = v.shape[:-1]
    n = v.shape[-1] // 128
    v = v.reshape(lead + (n, 128))
    v = np.moveaxis(v, -1, 0)
    return np.ascontiguousarray(v).reshape(128, -1)


def _constants():
    j = np.arange(128)
    ident = np.eye(128, dtype=np.float32)
    trif = (j[:, None] <= j[None, :]).astype(np.float32)
    trib = (j[:, None] >= j[None, :]).astype(np.float32)
    blk = j // 16
    bmf = (blk[None, :] >= blk[:, None]).astype(np.float32)
    bmb = (blk[None, :] <= blk[:, None]).astype(np.float32)
    rmask = np.ones((128, 512), np.float32)
    rmask[:, ::128] = 0.0
    cst = np.concatenate([ident, trif, trib, bmf, bmb, rmask, np.zeros((128, 128), np.float32)], axis=1)
    csm = np.zeros((128, 32), np.float32)
    top = (j < 64)
    csm[:, 0] = np.where(top, -1.0, 1.0)
    hp = math.pi / 2
    csm[:, 1] = np.where(top, hp, math.pi)
    csm[:, 2] = np.where(top, 0.0, hp)
    csm[:, 3] = np.where(top, hp, math.pi)
    csm[:, 4] = np.where(top, math.pi, 3 * hp)
    csm[:, 5] = hp
    csm[:, 6] = 0.0
    csm[:, 7:15] = np.arange(1, 9, dtype=np.float32)[None, :]
    csm[:, 15:23] = (8 - np.arange(8, dtype=np.float32))[None, :]
    csm[:, 23:32] = (8.0 * np.arange(9, dtype=np.float32))[None, :]
    quarter = D // 4
    omega = (1.0 / (10000.0 ** (np.arange(quarter, dtype=np.float32) / quarter))).astype(np.float32)
    rr = np.arange(64, dtype=np.float32)
    ang = (rr[:, None] * omega[None, :]).astype(np.float32)
    tab = np.concatenate([np.sin(ang), np.cos(ang)], axis=1).astype(np.float32)
    fm = np.ascontiguousarray(tab.T.reshape(4, 128, 64).transpose(1, 0, 2)).reshape(128, 256)
    pos = np.concatenate([fm, fm], axis=1).astype(np.float32)
    return cst, csm, pos


def _prep_shared(inp):
    f = lambda a: np.ascontiguousarray(np.asarray(a, np.float32))
    sh = {}
    sh["w_mod"] = f(inp["w_mod"])
    sh["b_modT"] = _fm_vec(inp["b_mod"])
    sh["n1g"] = _fm_vec(inp["norm1_g"])
    sh["n2g"] = _fm_vec(inp["norm2_g"])
    sh["fng"] = _fm_vec(inp["final_g"])
    sh["w_in"] = f(inp["w_in"])
    sh["w_pg"] = f(inp["w_proj_gla"])
    sh["w_glu"] = f(inp["w_glu"])
    sh["w_ps"] = f(inp["w_proj_s5"])
    sh["w_out"] = f(inp["w_out"])
    sh["w_ff1"] = f(inp["w_ff1"])
    sh["w_ff2"] = f(inp["w_ff2"])
    sh["w_gu"] = np.ascontiguousarray(f(inp["w_gate_up"]).transpose(2, 0, 1, 3)).reshape(16, -1)
    sh["bgT"] = _fm_vec(inp["b_gate"])
    sh["gngT"] = _fm_vec(inp["gla_norm_g"])
    sh["bgluT"] = _fm_vec(inp["b_glu"])
    d = f(inp["s5_d"]).reshape(DEPTH, 32, 16)
    d8 = np.broadcast_to(d.transpose(2, 0, 1)[None], (8, 16, DEPTH, 32))
    sh["d8"] = np.ascontiguousarray(d8).reshape(128, -1)

    def klay(a):
        a = f(a).transpose(3, 0, 1, 2)
        return np.ascontiguousarray(np.concatenate([a, a], axis=0)).reshape(128, -1)
    sh["lre2"] = klay(inp["s5_lam_re"])
    sh["lim2"] = klay(inp["s5_lam_im"])
    ls = np.broadcast_to(f(inp["s5_log_step"])[None], (128, DEPTH, 2, 32))
    sh["lst2"] = np.ascontiguousarray(ls).reshape(128, -1)

    def klayB(a):
        a = f(a).transpose(3, 0, 1, 2, 4)
        return np.ascontiguousarray(np.concatenate([a, a], axis=0)).reshape(128, -1)

    def klayC(a):
        a = f(a).transpose(4, 0, 1, 2, 3)
        return np.ascontiguousarray(np.concatenate([a, a], axis=0)).reshape(128, -1)
    sh["Bre2"] = klayB(inp["s5_b_re"])
    sh["Bim2"] = klayB(inp["s5_b_im"])
    sh["Cre2"] = klayC(inp["s5_c_re"])
    sh["Cim2"] = klayC(inp["s5_c_im"])
    cst, csm, pos = _constants()
    sh["cst"], sh["csm"], sh["pos"] = cst, csm, pos
    return sh


def _prep_core(inp, core, NP, TP, TS):
    f = lambda a: np.asarray(a, np.float32)
    xp = f(inp["x_prompt"])[core * NP:(core + 1) * NP].reshape(NP * TP, D)
    xs = f(inp["x_sample"])[core]
    m = {}
    m["xT"] = np.ascontiguousarray(np.concatenate([xp, xs], axis=0).T)
    cond = np.stack([f(inp["c_ctx"]), f(inp["c"])[core]], axis=-1)
    m["condT"] = np.ascontiguousarray(cond.reshape(8, 128, 2).transpose(1, 0, 2)).reshape(128, 16)
    m["gla0"] = np.ascontiguousarray(f(inp["cache_gla_state"])[core]).reshape(-1, 256)
    re = f(inp["state_s5_re"])[core].transpose(3, 0, 1, 2)
    im = f(inp["state_s5_im"])[core].transpose(3, 0, 1, 2)
    m["s5x0"] = np.ascontiguousarray(np.concatenate([re, im], axis=0)).reshape(128, -1)
    m["s5x0s"] = np.ascontiguousarray(np.concatenate([im, re], axis=0)).reshape(128, -1)
    return m


_NC_CACHE = {}


def run_cfg(inp, NP, TP, TS, ncores):
    key = (NP, TP, TS)
    if key not in _NC_CACHE:
        _NC_CACHE[key] = build(key)
    nc = _NC_CACHE[key]
    sh = _prep_shared(inp)
    in_maps = []
    for c in range(ncores):
        m = dict(sh)
        m.update(_prep_core(inp, c, NP, TP, TS))
        in_maps.append(m)
    res = run_bass_kernel_spmd(nc, in_maps, core_ids=list(range(ncores)))
    B = NP * ncores
    y_prompt = np.zeros((B, TP, D), np.float32)
    y_sample = np.zeros((ncores, TS, D), np.float32)
    gla = np.zeros((B, DEPTH, 2, H, DK, DV), np.float32)
    s5re = np.zeros((B, DEPTH, 2, G, 64), np.float32)
    s5im = np.zeros((B, DEPTH, 2, G, 64), np.float32)
    for c in range(ncores):
        r = res.results[c]
        y = np.asarray(r["yT"], np.float32).T
        y_prompt[c * NP:(c + 1) * NP] = y[:NP * TP].reshape(NP, TP, D)
        y_sample[c] = y[NP * TP:]
        gla[c * NP:(c + 1) * NP] = np.asarray(r["gla_out"], np.float32).reshape(NP, DEPTH, 2, H, DK, DV)
        s = np.asarray(r["s5_out"], np.float32).reshape(2, 64, NP, DEPTH, 2, G)
        s5re[c * NP:(c + 1) * NP] = s[0].transpose(1, 2, 3, 4, 0)
        s5im[c * NP:(c + 1) * NP] = s[1].transpose(1, 2, 3, 4, 0)
    return (y_prompt, y_sample, gla, s5re, s5im)


def kernel(**inputs):
    return run_cfg(inputs, 4, 256, 4096, 8)
```

```python
import math
from contextlib import ExitStack

import numpy as np
import concourse.bass as bass
import concourse.mybir as mybir
from concourse.bass_utils import run_bass_kernel_spmd

F32 = mybir.dt.float32
BF16 = mybir.dt.bfloat16
I32 = mybir.dt.int32
AF = mybir.ActivationFunctionType
ALU = mybir.AluOpType

D = 1024
KC = 8
H = 4
DK = 128
DV = 256
G = 32
DFF = 4096
DIN = 5664
DEPTH = 2
EPS = 1e-6
C_Q, C_K, C_V, C_R, C_GLR, C_U, C_GA, C_GB = 0, 512, 1024, 2048, 3072, 3104, 3616, 4640
TWO_PI = 2.0 * math.pi
NLANES = 24
NHW = 16
NWS = 3


class Op:
    __slots__ = ("eng", "fn", "deps", "needed", "cnt", "dma", "lane", "lane_total", "lane_prev")

    def __init__(self, eng, fn, dma):
        self.eng = eng
        self.fn = fn
        self.dma = dma
        self.deps = ()
        self.needed = False
        self.cnt = 0
        self.lane = -1
        self.lane_total = 0
        self.lane_prev = 0


class KeyState:
    __slots__ = ("w", "r")

    def __init__(self):
        self.w = None
        self.r = []


class Prog:
    ENGS = ("pe", "act", "dve", "pool", "sp")

    def __init__(self, nc, gstack):
        self.nc = nc
        self.gstack = gstack
        self.ops = []
        self.keys = {}
        self.lane_sems = [gstack.enter_context(nc.semaphore(f"lane{i}")) for i in range(NLANES)]
        self.lane_tot = [0] * NLANES
        self.next_lane = 0
        self.next_sw_lane = NHW
        self.prev_final = []
        self.phase_no = 0
        self.ps_rr = 0
        self.w_rr = 0
        self.n_inst = 0

    def op(self, eng, fn, reads=(), writes=(), dma=False):
        o = Op(eng, fn, dma)
        deps = []
        for k in reads:
            st = self.keys.get(k)
            if st is None:
                st = self.keys[k] = KeyState()
            if st.w is not None:
                deps.append(st.w)
        for k in writes:
            st = self.keys.get(k)
            if st is None:
                st = self.keys[k] = KeyState()
            if st.w is not None:
                deps.append(st.w)
            deps.extend(st.r)
        for k in reads:
            self.keys[k].r.append(o)
        for k in writes:
            st = self.keys[k]
            st.w = o
            st.r = []
        if dma:
            if eng == "pool":
                o.lane = self.next_sw_lane
                self.next_sw_lane = NHW + (self.next_sw_lane + 1 - NHW) % (NLANES - NHW)
            else:
                o.lane = self.next_lane
                self.next_lane = (self.next_lane + 1) % NHW
            o.lane_prev = self.lane_tot[o.lane]
            self.lane_tot[o.lane] += 16
            o.lane_total = self.lane_tot[o.lane]
            o.needed = True
        o.deps = [d for d in deps if d is not o and not (d.eng == "pe" and eng == "pe" and not d.dma)]
        self.ops.append(o)
        return o

    def flush(self):
        nc = self.nc
        ops = self.ops
        self.ops = []
        self.keys = {}
        self.phase_no += 1
        sems = {e: self.gstack.enter_context(nc.semaphore(f"ph{self.phase_no}_{e}")) for e in self.ENGS}
        for o in ops:
            for d in o.deps:
                d.needed = True
        per = {e: [] for e in self.ENGS}
        for o in ops:
            per[o.eng].append(o)
        for e in self.ENGS:
            if per[e]:
                last = per[e][-1]
                last.needed = True
        cnt = {e: 0 for e in self.ENGS}
        for o in ops:
            if o.needed and not o.dma:
                cnt[o.eng] += 1
                o.cnt = cnt[o.eng]
        prev_final = self.prev_final
        lane_sems = self.lane_sems
        self.n_inst += len(ops)

        def emit(engname, eng):
            seen = {}

            def wait(sem, val):
                if val <= 0:
                    return
                k = id(sem)
                if seen.get(k, 0) >= val:
                    return
                seen[k] = val
                eng.wait_ge(sem, val)

            for (s, v) in prev_final:
                wait(s, v)
            for o in per[engname]:
                for d in o.deps:
                    if d.dma:
                        wait(lane_sems[d.lane], d.lane_total)
                    else:
                        wait(sems[d.eng], d.cnt)
                if o.dma:
                    wait(lane_sems[o.lane], o.lane_prev)
                ins = o.fn(eng)
                if o.dma:
                    ins.then_inc(lane_sems[o.lane], 16)
                elif o.needed:
                    ins.then_inc(sems[o.eng], 1)

        with nc.Block() as block:
            if per["pe"]:
                @block.tensor
                def _(eng):
                    emit("pe", eng)
            if per["act"]:
                @block.scalar
                def _(eng):
                    emit("act", eng)
            if per["dve"]:
                @block.vector
                def _(eng):
                    emit("dve", eng)
            if per["pool"]:
                @block.gpsimd
                def _(eng):
                    emit("pool", eng)
            if per["sp"]:
                @block.sync
                def _(eng):
                    emit("sp", eng)
        pf = []
        for e in self.ENGS:
            if cnt[e] > 0:
                pf.append((sems[e], cnt[e]))
        for i in range(NLANES):
            if self.lane_tot[i] > 0:
                pf.append((lane_sems[i], self.lane_tot[i]))
        self.prev_final = pf

    def final_wait(self):
        nc = self.nc
        pf = self.prev_final
        with nc.Block() as block:
            @block.sync
            def _(eng):
                for (s, v) in pf:
                    eng.wait_ge(s, v)

    def mm(self, out, lhsT, rhs, start, stop, r, w):
        return self.op("pe", lambda e: e.matmul(out, lhsT=lhsT, rhs=rhs, start=start, stop=stop), r, w)

    def transpose(self, out, in_, ident, r, w):
        return self.op("pe", lambda e: e.transpose(out=out, in_=in_, identity=ident), r, w)

    def act(self, out, in_, func, r, w, scale=1.0, bias=None):
        if bias is None:
            return self.op("act", lambda e: e.activation(out=out, in_=in_, func=func, scale=scale), r, w)
        return self.op("act", lambda e: e.activation(out=out, in_=in_, func=func, scale=scale, bias=bias), r, w)

    def tt(self, eng, out, in0, in1, op, r, w):
        return self.op(eng, lambda e: e.tensor_tensor(out=out, in0=in0, in1=in1, op=op), r, w)

    def ts(self, eng, out, in0, s1, s2, op0, op1, r, w):
        if s2 is None:
            return self.op(eng, lambda e: e.tensor_scalar(out=out, in0=in0, scalar1=s1, scalar2=None, op0=op0), r, w)
        return self.op(eng, lambda e: e.tensor_scalar(out=out, in0=in0, scalar1=s1, scalar2=s2, op0=op0, op1=op1), r, w)

    def stt(self, out, in0, scalar, in1, op0, op1, r, w):
        return self.op("dve", lambda e: e.scalar_tensor_tensor(out=out, in0=in0, scalar=scalar, in1=in1, op0=op0, op1=op1), r, w)

    def copy(self, eng, out, in_, r, w):
        if eng == "act":
            return self.act(out, in_, AF.Copy, r, w)
        return self.op(eng, lambda e: e.tensor_copy(out=out, in_=in_), r, w)

    def memset(self, eng, ap, val, w):
        return self.op(eng, lambda e: e.memset(ap, val), (), w)

    def dma(self, out, in_, r, w, eng="sp", **kw):
        return self.op(eng, lambda e: e.dma_start(out=out, in_=in_, **kw), r, w, dma=True)


def sap(t, off, dims, rowsize):
    return bass.AP(t, off, [[rowsize, 128]] + [list(d) for d in dims])


def sapp(t, p0, npart, off, dims, rowsize):
    return bass.AP(t, p0 * rowsize + off, [[rowsize, npart]] + [list(d) for d in dims])


def dap(t, off, dims):
    return bass.AP(t, off, [list(d) for d in dims])


BIGW = (("w_in", D, DIN), ("w_pg", D, D), ("w_glu", 512, 512), ("w_ps", 512, D), ("w_out", D, D),
        ("w_ff1", D, DFF), ("w_ff2", DFF, D))


def build(cfg, debug=None):
    NP, TP, TS = cfg
    NTOK = NP * TP + TS
    nc = bass.Bass("TRN2", target_bir_lowering=False)
    gs = ExitStack()

    def din(name, shape, dt=F32):
        return nc.dram_tensor(name, list(shape), dt, kind="ExternalInput")

    def dout(name, shape, dt=F32):
        return nc.dram_tensor(name, list(shape), dt, kind="ExternalOutput")

    def dscr(name, shape, dt):
        return nc.dram_tensor(name, list(shape), dt, kind="Internal")

    def gsb(name, shape, dt):
        return gs.enter_context(nc.sbuf_tensor(name, list(shape), dt))

    xT = din("xT", [D, NTOK])
    condT = din("condT", [128, 16])
    w_mod = din("w_mod", [DEPTH, D, 6 * D])
    b_modT = din("b_modT", [128, DEPTH * 48])
    n1g_d = din("n1g", [128, DEPTH * 8])
    n2g_d = din("n2g", [128, DEPTH * 8])
    fng_d = din("fng", [128, 8])
    wsrc = {n: din(n, [DEPTH, k, m]) for (n, k, m) in BIGW}
    w_gu = din("w_gu", [16, DEPTH * 2 * 512])
    bg_d = din("bgT", [128, DEPTH * 2 * 4])
    gng_d = din("gngT", [128, DEPTH * 2])
    bglu_d = din("bgluT", [128, DEPTH * 4])
    d8_d = din("d8", [128, DEPTH * 32])
    lre_d = din("lre2", [128, DEPTH * 2 * 32])
    lim_d = din("lim2", [128, DEPTH * 2 * 32])
    lst_d = din("lst2", [128, DEPTH * 2 * 32])
    Bre_d = din("Bre2", [128, DEPTH * 2 * 32 * 16])
    Bim_d = din("Bim2", [128, DEPTH * 2 * 32 * 16])
    Cre_d = din("Cre2", [128, DEPTH * 2 * 32 * 16])
    Cim_d = din("Cim2", [128, DEPTH * 2 * 32 * 16])
    gla0_d = din("gla0", [DEPTH * 2 * 4 * 128, 256])
    x0_d = din("s5x0", [128, DEPTH * 2 * 32])
    x0s_d = din("s5x0s", [128, DEPTH * 2 * 32])
    cst_d = din("cst", [128, 1280])
    csm_d = din("csm", [128, 32])
    pos_d = din("pos", [128, 512])

    yT = dout("yT", [D, NTOK])
    gla_out = dout("gla_out", [NP * DEPTH * 2 * 4 * 128, 256])
    s5_out = dout("s5_out", [128, NP * DEPTH * 2 * 32])

    STREAMS = {"w_inA": ("w_in", 0, 3072, D, DIN), "w_inB": ("w_in", C_U, 2560, D, DIN), "w_inG": ("w_in", C_GLR, 32, D, DIN),
               "w_pg": ("w_pg", 0, D, D, D), "w_glu": ("w_glu", 0, 512, 512, 512), "w_ps": ("w_ps", 0, D, 512, D),
               "w_out": ("w_out", 0, D, D, D), "w_ff1": ("w_ff1", 0, DFF, D, DFF), "w_ff2": ("w_ff2", 0, D, DFF, D)}

    def sgeom(sn):
        src, c0, ncols, K, Msrc = STREAMS[sn]
        KCn = K // 128
        bw = min(512, 4096 // KCn, ncols)
        return src, c0, ncols, K, Msrc, KCn, bw, ncols // bw
    wb = {}
    for sn in STREAMS:
        _, _, _, _, _, KCn_, bw_, nblk_ = sgeom(sn)
        wb[sn] = dscr(sn + "_bf", [DEPTH * nblk_ * 128, KCn_ * bw_], BF16)

    def wstream(name, m0):
        if name == "w_in":
            if C_GLR <= m0 < C_U:
                return "w_inG", None
            sn = "w_inA" if m0 < C_GLR else "w_inB"
        else:
            sn = name
        _, c0, _, _, _, _, bw, _ = sgeom(sn)
        assert (m0 - c0) % bw == 0, (name, m0)
        return sn, (m0 - c0) // bw

    def wblock_ap(sn, l, blk):
        _, _, _, _, _, KCn, bw, nblk = sgeom(sn)
        return dap(wb[sn], (l * nblk + blk) * 128 * KCn * bw, [[KCn * bw, 128], [1, KCn * bw]])
    xm = dscr("xm", [D, NTOK], F32)
    xn = dscr("xn", [D, NTOK], F32)
    xp = dscr("xp", [D, NTOK], F32)
    obs = dscr("obs", [D, NTOK], BF16)
    ybs = dscr("ybs", [512, NTOK], BF16)
    D1 = dscr("D1", [8, 512, 64], BF16)
    D2 = dscr("D2", [8, 512, 64], BF16)
    stabd = dscr("stabd", [DEPTH * 2 * 4, 128, 4096], BF16)
    abrd = dscr("abrd", [DEPTH * 2, 128, 1152], F32)

    P = Prog(nc, gs)

    ident_f = gsb("ident_f", [128, 128], F32)
    ident_b = gsb("ident_b", [128, 128], BF16)
    ones_b = gsb("ones_b", [128, 128], BF16)
    trim = gsb("trim", [128, 2, 128], BF16)
    bmask = gsb("bmask", [128, 2, 128], F32)
    rmask = gsb("rmask", [128, 512], F32)
    csm = gsb("csm_s", [128, 32], F32)
    pos = gsb("pos_s", [128, 512], F32)
    modv = gsb("modv", [128, DEPTH, 48, 2], F32)
    A1 = gsb("A1", [128, DEPTH, 8, 2], F32)
    A2 = gsb("A2", [128, DEPTH, 8, 2], F32)
    n1g = gsb("n1g_s", [128, DEPTH * 8], F32)
    n2g = gsb("n2g_s", [128, DEPTH * 8], F32)
    fng = gsb("fng_s", [128, 8], F32)
    bgneg = gsb("bgneg", [128, DEPTH * 2 * 4], F32)
    gng = gsb("gng_s", [128, DEPTH * 2], F32)
    bglu = gsb("bglu_s", [128, DEPTH * 4], F32)
    d8 = gsb("d8_s", [128, DEPTH * 32], F32)
    wg = gsb("wg", [16, DEPTH * 2 * 512], BF16)
    AAt = gsb("AAt", [128, DEPTH * 2, 2, 32], F32)
    BBt = gsb("BBt", [128, DEPTH * 2, 2, 32], F32)
    NPSF = 6
    psf = [gs.enter_context(nc.psum_tensor(f"ps{i}", [128, 512], F32)) for i in range(NPSF)]
    psbs = [gs.enter_context(nc.psum_tensor(f"psb{i}", [128, 1024], BF16)) for i in range(2)]

    def psum():
        i = P.ps_rr
        P.ps_rr = (i + 1) % NPSF
        return psf[i], ("ps", i)

    SGN = csm[:, 0:1]
    PHS = [csm[:, 1 + i:2 + i] for i in range(4)]
    PHC = [csm[:, 5:6], csm[:, 6:7]]

    with ExitStack() as ph:
        def sb(name, shape, dt):
            return ph.enter_context(nc.sbuf_tensor(f"{name}_p{P.phase_no}", list(shape), dt))

        for sn in STREAMS:
            src, c0, ncols, K, Msrc, KCn, bw, nblk = sgeom(sn)
            for l in range(DEPTH):
                for blk in range(nblk):
                    P.dma(out=dap(wb[sn], (l * nblk + blk) * 128 * KCn * bw, [[KCn * bw, 128], [bw, KCn], [1, bw]]),
                          in_=dap(wsrc[src], l * K * Msrc + c0 + blk * bw, [[Msrc, 128], [128 * Msrc, KCn], [1, bw]]),
                          r=(), w=[("wb", sn)], eng="pool")

        cstt = sb("cstt", [128, 1280], F32)
        P.dma(cstt[:], cst_d.ap(), (), ["cstt"])
        P.dma(csm[:], csm_d.ap(), (), ["csm"])
        P.dma(pos[:], pos_d.ap(), (), ["pos"])
        P.dma(n1g[:], n1g_d.ap(), (), ["n1g"])
        P.dma(n2g[:], n2g_d.ap(), (), ["n2g"])
        P.dma(fng[:], fng_d.ap(), (), ["fng"])
        P.dma(gng[:], gng_d.ap(), (), ["gng"])
        P.dma(bglu[:], bglu_d.ap(), (), ["bglu"])
        P.dma(d8[:], d8_d.ap(), (), ["d8"])
        bgt = sb("bgt", [128, DEPTH * 8], F32)
        P.dma(bgt[:], bg_d.ap(), (), ["bgt"])
        P.act(bgneg[:], bgt[:], AF.Copy, ["bgt"], ["bgneg"], scale=-1.0)
        wgf = sb("wgf", [16, DEPTH * 2 * 512], F32)
        P.dma(wgf[:], w_gu.ap(), (), ["wgf"])
        P.copy("dve", wg[:], wgf[:], ["wgf"], ["wg"])
        P.copy("dve", ident_f[:], cstt[:, 0:128], ["cstt"], ["ident_f"])
        P.copy("dve", ident_b[:], cstt[:, 0:128], ["cstt"], ["ident_b"])
        P.memset("dve", ones_b[:], 1.0, ["ones_b"])
        P.copy("dve", trim[:, 0, :], cstt[:, 128:256], ["cstt"], ["trim"])
        P.copy("dve", trim[:, 1, :], cstt[:, 256:384], ["cstt"], ["trim"])
        P.copy("dve", bmask[:, 0, :], cstt[:, 384:512], ["cstt"], ["bmask"])
        P.copy("dve", bmask[:, 1, :], cstt[:, 512:640], ["cstt"], ["bmask"])
        P.copy("dve", rmask[:], cstt[:, 640:1152], ["cstt"], ["rmask"])

        condt = sb("condt", [128, 16], F32)
        sct = sb("sct", [128, 8, 2], F32)
        bmod = sb("bmod", [128, DEPTH * 48], F32)
        wm = sb("wm", [128, 2, 8 * 512], F32)
        P.dma(condt[:], condT.ap(), (), ["condt"])
        P.dma(bmod[:], b_modT.ap(), (), ["bmod"])
        P.act(sct[:], condt[:].rearrange("p (k c) -> p k c", c=2), AF.Silu, ["condt"], ["sct"])
        for l in range(DEPTH):
            pm, pmk = psf[l], ("ps", l)
            for blk in range(12):
                slot = blk % 2
                P.dma(out=sap(wm, slot * 4096, [[512, 8], [1, 512]], 8192),
                      in_=dap(w_mod, l * D * 6 * D + blk * 512, [[6 * D, 128], [128 * 6 * D, 8], [1, 512]]),
                      r=(), w=[("wm", slot)])
                for mc4 in range(4):
                    mc = blk * 4 + mc4
                    for kc in range(8):
                        P.mm(pm[:, mc * 2:mc * 2 + 2],
                             sap(wm, slot * 4096 + kc * 512 + mc4 * 128, [[1, 128]], 8192),
                             sct[:, kc, :], kc == 0, kc == 7, [("wm", slot), "sct"], [pmk])
            P.tt("dve", modv[:, l, :, :], sap(pm, 0, [[2, 48], [1, 2]], 512),
                 sap(bmod, l * 48, [[1, 48], [0, 2]], DEPTH * 48), ALU.add, [pmk, "bmod"], ["modv"])
            for (Ax, ng, ngk, off) in ((A1, n1g, "n1g", 8), (A2, n2g, "n2g", 32)):
                P.ts("dve", Ax[:, l, :, :], modv[:, l, off:off + 8, :], 1.0, None, ALU.add, None, ["modv"], ["A12"])
                P.tt("dve", Ax[:, l, :, :], Ax[:, l, :, :], sap(ng, l * 8, [[1, 8], [0, 2]], DEPTH * 8), ALU.mult,
                     ["A12", ngk], ["A12"])

        lre = sb("lre", [128, DEPTH * 2 * 32], F32)
        lim = sb("lim", [128, DEPTH * 2 * 32], F32)
        lst = sb("lst", [128, DEPTH * 2 * 32], F32)
        P.dma(lre[:], lre_d.ap(), (), ["lre"])
        P.dma(lim[:], lim_d.ap(), (), ["lim"])
        P.dma(lst[:], lst_d.ap(), (), ["lst"])
        Bre = sb("Bre", [128, 512], F32)
        Bim = sb("Bim", [128, 512], F32)
        Cre = sb("Cre", [128, 512], F32)
        Cim = sb("Cim", [128, 512], F32)
        s32 = {n: sb("s5_" + n, [128, 32], F32) for n in
               ("st", "ar", "ai", "e1", "c1", "s1", "lbr", "lbi", "nr", "den", "t1", "t2", "wr", "wi", "m8", "th8")}
        s256 = {n: sb("s5v_" + n, [128, 288], F32) for n in
                ("th", "ea", "mgn", "mgp", "tA", "tB", "sn", "X1", "X2", "Y1", "Y2")}
        ni = sb("s5_ni", [128, 288], I32)
        abr_t = sb("abr_t", [128, 1152], F32)
        bbr = sb("bbr", [128, 512], F32)
        bbi = sb("bbi", [128, 512], F32)
        bt = sb("bt", [128, 512], F32)
        BZ = sb("BZ", [128, 4096], F32)
        QC = sb("QC", [128, 4096], F32)
        T3 = sb("T3", [128, 4096], F32)
        tWB = sb("tWB", [128, 4096], BF16)
        tWT = sb("tWT", [128, 4096], BF16)
        tM = sb("tM", [128, 4096], BF16)
        tQC = sb("tQC", [128, 4096], BF16)
        Mtmp = sb("Mtmp", [128, 512], F32)

        def sinr(out, outk, th, thk, phase, n):
            tA, tB = s256["tA"][:, 0:n], s256["tB"][:, 0:n]
            P.ts("dve", tA, th, phase, None, ALU.add, None, [thk, "csm"], ["tA"])
            P.ts("dve", ni[:, 0:n], tA, 1.0 / TWO_PI, None, ALU.mult, None, ["tA"], ["ni"])
            P.copy("dve", tB, ni[:, 0:n], ["ni"], ["tB"])
            P.stt(tA, tB, -TWO_PI, tA, ALU.mult, ALU.add, ["tA", "tB"], ["tA"])
            P.ts("dve", tA, tA, 3.14159, -3.14159, ALU.min, ALU.max, ["tA"], ["tA"])
            P.act(out, tA, AF.Sin, ["tA"], [outk])

        def mul(out, a, b, r, w):
            P.tt("dve", out, a, b, ALU.mult, r, w)

        for l in range(DEPTH):
            for dr in range(2):
                ld = l * 2 + dr
                sl = slice(ld * 32, ld * 32 + 32)
                for (t_s, t_d, nm) in ((Bre, Bre_d, "Bre"), (Bim, Bim_d, "Bim"), (Cre, Cre_d, "Cre"), (Cim, Cim_d, "Cim")):
                    P.dma(t_s[:], dap(t_d, ld * 512, [[DEPTH * 2 * 512, 128], [1, 512]]), (), [nm])
                S = {k: v[:] for k, v in s32.items()}
                P.act(S["st"], lst[:, sl], AF.Exp, ["lst"], ["st"])
                mul(S["ar"], lre[:, sl], S["st"], ["lre", "st"], ["ar"])
                mul(S["ai"], lim[:, sl], S["st"], ["lim", "st"], ["ai"])
                P.act(S["e1"], S["ar"], AF.Exp, ["ar"], ["e1"])
                sinr(S["c1"], "c1", S["ai"], "ai", PHC[0], 32)
                sinr(S["s1"], "s1", S["ai"], "ai", PHC[1], 32)
                mul(S["lbr"], S["e1"], S["c1"], ["e1", "c1"], ["lbr"])
                mul(S["lbi"], S["e1"], S["s1"], ["e1", "s1"], ["lbi"])
                P.ts("dve", S["nr"], S["lbr"], -1.0, None, ALU.add, None, ["lbr"], ["nr"])
                mul(S["t1"], lre[:, sl], lre[:, sl], ["lre"], ["t1"])
                mul(S["t2"], lim[:, sl], lim[:, sl], ["lim"], ["t2"])
                P.tt("dve", S["den"], S["t1"], S["t2"], ALU.add, ["t1", "t2"], ["den"])
                P.op("dve", lambda e, o=S["den"]: e.reciprocal(out=o, in_=o), ["den"], ["den"])
                mul(S["t1"], S["nr"], lre[:, sl], ["nr", "lre"], ["t1"])
                mul(S["t2"], S["lbi"], lim[:, sl], ["lbi", "lim"], ["t2"])
                P.tt("dve", S["wr"], S["t1"], S["t2"], ALU.add, ["t1", "t2"], ["wr"])
                mul(S["wr"], S["wr"], S["den"], ["wr", "den"], ["wr"])
                mul(S["t1"], S["lbi"], lre[:, sl], ["lbi", "lre"], ["t1"])
                mul(S["t2"], S["nr"], lim[:, sl], ["nr", "lim"], ["t2"])
                P.tt("dve", S["wi"], S["t1"], S["t2"], ALU.subtract, ["t1", "t2"], ["wi"])
                mul(S["wi"], S["wi"], S["den"], ["wi", "den"], ["wi"])
                wrb = sap(s32["wr"], 0, [[1, 32], [0, 16]], 32)
                wib = sap(s32["wi"], 0, [[1, 32], [0, 16]], 32)
                v3 = lambda t: t[:].rearrange("p (g m) -> p g m", m=16)
                mul(v3(bbr), v3(Bre), wrb, ["Bre", "wr"], ["bbr"])
                mul(v3(bt), v3(Bim), wib, ["Bim", "wi"], ["bt"])
                P.tt("dve", bbr[:], bbr[:], bt[:], ALU.subtract, ["bbr", "bt"], ["bbr"])
                mul(v3(bbi), v3(Bim), wrb, ["Bim", "wr"], ["bbi"])
                mul(v3(bt), v3(Bre), wib, ["Bre", "wi"], ["bt"])
                P.tt("dve", bbi[:], bbi[:], bt[:], ALU.add, ["bbi", "bt"], ["bbi"])
                P.act(S["m8"], S["ar"], AF.Exp, ["ar"], ["m8"], scale=8.0)
                P.ts("dve", S["th8"], S["ai"], 8.0, None, ALU.mult, None, ["ai"], ["th8"])
                sinr(S["c1"], "c1", S["th8"], "th8", PHC[0], 32)
                sinr(S["s1"], "s1", S["th8"], "th8", PHC[1], 32)
                mul(AAt[:, ld, 0, :], S["m8"], S["c1"], ["m8", "c1"], ["AAt"])
                mul(AAt[:, ld, 1, :], S["m8"], S["c1"], ["m8", "c1"], ["AAt"])
                mul(BBt[:, ld, 0, :], S["m8"], S["s1"], ["m8", "s1"], ["BBt"])
                P.ts("dve", BBt[:, ld, 1, :], BBt[:, ld, 0, :], -1.0, None, ALU.mult, None, ["BBt"], ["BBt"])
                W9 = {k: v[:, 0:288] for k, v in s256.items()}
                v9 = lambda t: t[:, 0:288].rearrange("p (j g) -> p j g", g=32)
                erb = sap(csm, 23, [[1, 9], [0, 32]], 32)
                mul(v9(s256["th"]), sap(s32["ai"], 0, [[0, 9], [1, 32]], 32), erb, ["ai", "csm"], ["th"])
                mul(v9(s256["ea"]), sap(s32["ar"], 0, [[0, 9], [1, 32]], 32), erb, ["ar", "csm"], ["ea"])
                P.act(W9["mgp"], W9["ea"], AF.Exp, ["ea"], ["mgp"])
                sinr(W9["sn"], "sn", W9["th"], "th", PHC[0], 288)
                mul(W9["X1"], W9["sn"], W9["mgp"], ["sn", "mgp"], ["X1"])
                sinr(W9["sn"], "sn", W9["th"], "th", PHC[1], 288)
                mul(W9["X2"], W9["sn"], W9["mgp"], ["sn", "mgp"], ["X2"])
                for (off_, src_, sc_) in ((0, "X1", 1.0), (32, "X1", 1.0), (64, "X2", 1.0), (96, "X2", -1.0)):
                    P.ts("dve", sap(abr_t, off_, [[128, 9], [1, 32]], 1152), v9(s256[src_]), sc_, None, ALU.mult, None, [src_], ["abr_t"])
                P.dma(dap(abrd, ld * 128 * 1152, [[1152, 128], [1, 1152]]), abr_t[:], ["abr_t"], [("abrd", ld)])
                V = {k: v[:, 0:256] for k, v in s256.items()}
                v2 = lambda t: t[:, 0:256].rearrange("p (j g) -> p j g", g=32)
                ejb = sap(csm, 7 + dr * 8, [[1, 8], [0, 32]], 32)
                mul(v2(s256["th"]), sap(s32["ai"], 0, [[0, 8], [1, 32]], 32), ejb, ["ai", "csm"], ["th"])
                mul(v2(s256["ea"]), sap(s32["ar"], 0, [[0, 8], [1, 32]], 32), ejb, ["ar", "csm"], ["ea"])
                P.act(V["mgn"], V["ea"], AF.Exp, ["ea"], ["mgn"], scale=-1.0)
                P.act(V["mgp"], V["ea"], AF.Exp, ["ea"], ["mgp"])
                for (nm, ph_i, mg) in (("X1", 0, "mgn"), ("X2", 1, "mgn"), ("Y1", 2, "mgp"), ("Y2", 3, "mgp")):
                    sinr(V["sn"], "sn", V["th"], "th", PHS[ph_i], 256)
                    mul(V[nm], V["sn"], V[mg], ["sn", mg], [nm])
                v4 = lambda t: t[:].rearrange("p (g j m) -> p g j m", j=8, m=16)
                xv = lambda t: sap(t, 0, [[1, 32], [32, 8], [0, 16]], 288)
                bv = lambda t: sap(t, 0, [[16, 32], [0, 8], [1, 16]], 512)
                mul(v4(BZ), xv(s256["X1"]), bv(bbr), ["X1", "bbr"], ["BZ"])
                mul(v4(T3), xv(s256["X2"]), bv(bbi), ["X2", "bbi"], ["T3"])
                P.tt("dve", BZ[:], BZ[:], T3[:], ALU.add, ["BZ", "T3"], ["BZ"])
                mul(v4(QC), xv(s256["Y1"]), bv(Cre), ["Y1", "Cre"], ["QC"])
                mul(v4(T3), xv(s256["Y2"]), bv(Cim), ["Y2", "Cim"], ["T3"])
                P.tt("dve", QC[:], QC[:], T3[:], ALU.add, ["QC", "T3"], ["QC"])
                P.copy("act", tQC[:], QC[:], ["QC"], ["tQC"])
                for gq in range(8):
                    pa, pak = psum()
                    for gi in range(4):
                        g = gq * 4 + gi
                        P.mm(pa[:, gi * 128:(gi + 1) * 128], BZ[:, g * 128:(g + 1) * 128], ident_f[:], True, True,
                             ["BZ", "ident_f"], [pak])
                    pv = pa[:].rearrange("p (a k) -> p a k", k=128)
                    gsl = slice(gq * 512, gq * 512 + 512)
                    P.copy("act", tWB[:, gsl], pa[:], [pak], ["tWB"])
                    tw = tWT[:, gsl].rearrange("p (a k) -> p a k", k=128)
                    P.act(tw[:, :, 0:64], pv[:, :, 64:128], AF.Copy, [pak], ["tWT"], scale=-1.0)
                    P.copy("dve", tw[:, :, 64:128], pv[:, :, 0:64], [pak], ["tWT"])
                    pb_, pbk = psum()
                    for gi in range(4):
                        g = gq * 4 + gi
                        P.mm(pb_[:, gi * 128:(gi + 1) * 128], BZ[:, g * 128:(g + 1) * 128], QC[:, g * 128:(g + 1) * 128],
                             True, True, ["BZ", "QC"], [pbk])
                    P.tt("dve", Mtmp[:].rearrange("p (a k) -> p a k", k=128), pb_[:].rearrange("p (a k) -> p a k", k=128),
                         sap(bmask, dr * 128, [[0, 4], [1, 128]], 256), ALU.mult, [pbk, "bmask"], ["Mtmp"])
                    for gi in range(4):
                        g = gq * 4 + gi
                        sc_ = d8[:, l * 32 + g:l * 32 + g + 1] if dr == 0 else 0.0
                        P.stt(tM[:, g * 128:(g + 1) * 128], ident_f[:], sc_, Mtmp[:, gi * 128:(gi + 1) * 128],
                              ALU.mult, ALU.add, ["ident_f", "d8", "Mtmp"], ["tM"])
                for k, (tt_, nm) in enumerate(((tWB, "tWB"), (tWT, "tWT"), (tM, "tM"), (tQC, "tQC"))):
                    P.dma(dap(stabd, (ld * 4 + k) * 128 * 4096, [[4096, 128], [1, 4096]]), tt_[:], [nm], [("stabd", ld)])
        P.flush()

    seqs = [(i * TP, TP, 0, i) for i in range(NP)] + [(NP * TP, TS, 1, -1)]

    def linear(wt, name, l, K, M, m0, m1, rhs_fn, N, evac, rkeys):
        KCn = K // 128
        sn, blk0 = wstream(name, m0)
        if blk0 is None:
            bw = 32
            blocks = [(0, m0 - C_GLR, m1 - m0)]
        else:
            bw = sgeom(sn)[6]
            assert (m1 - m0) % bw == 0
            blocks = [(blk0 + i, 0, bw) for i in range((m1 - m0) // bw)]
        done = 0
        for (blk, cofs, cuse) in blocks:
            slot = P.w_rr
            P.w_rr = (slot + 1) % P.nws
            P.dma(out=sap(wt, slot * 4096, [[1, KCn * bw]], P.nws * 4096), in_=wblock_ap(sn, l, blk),
                  r=(), w=[("wt", slot)], eng="pool")
            for c0 in range(cofs, cofs + cuse, 128):
                cw = min(128, cofs + cuse - c0)
                for n0 in range(0, N, 512):
                    nn = min(512, N - n0)
                    ps, psk = psum()
                    for kc in range(KCn):
                        P.mm(ps[0:cw, 0:nn], sap(wt, slot * 4096 + kc * bw + c0, [[1, cw]], P.nws * 4096), rhs_fn(kc, n0, nn),
                             kc == 0, kc == KCn - 1, [("wt", slot)] + rkeys, [psk])
                    evac(done // 128, ps, psk, n0, nn)
                done += cw

    def norm_mod(xt, xrow, ht, hrow, sq, rs, tmpf, Ax, shift0, l, cnd, N):
        xk = xrow or "xt"
        hk = hrow or "ht"
        for n0 in range(0, N, 512):
            nn = min(512, N - n0)
            pn, pnk = psum()
            for fc in range(8):
                i2 = fc % 2
                P.act(sq[:, i2, 0:nn], xt[:, fc, n0:n0 + nn], AF.Square, [(xk, fc)], [("sq", i2)])
                P.mm(pn[:, 0:nn], ones_b[:], sq[:, i2, 0:nn], fc == 0, fc == 7, [("sq", i2), "ones_b"], [pnk])
            P.act(rs[:, 0:nn], pn[:, 0:nn], AF.Ln, [pnk], ["rs"], scale=1.0 / D, bias=EPS)
            P.act(rs[:, 0:nn], rs[:, 0:nn], AF.Exp, ["rs"], ["rs"], scale=-0.5)
            for fc in range(8):
                i2 = fc % 2
                P.stt(tmpf[:, i2, 0:nn], xt[:, fc, n0:n0 + nn], Ax[:, l, fc, cnd:cnd + 1], rs[:, 0:nn], ALU.mult, ALU.mult,
                      [(xk, fc), "rs", "A12"], [("tmpf", i2)])
                P.act(ht[:, fc, n0:n0 + nn], tmpf[:, i2, 0:nn], AF.Identity, [("tmpf", i2), "modv"], [(hk, fc)],
                      bias=modv[:, l, shift0 + fc, cnd:cnd + 1])

    def mixer_phase(l, dr, passF, xsrc, xdst0):
        ld = l * 2 + dr
        TM = 512
        with ExitStack() as ph:
            def sb(name, shape, dt):
                return ph.enter_context(nc.sbuf_tensor(f"{name}_p{P.phase_no}", list(shape), dt))
            xt = sb("xt", [128, 8, TM], F32)
            ht = sb("ht", [128, 8, TM], BF16)
            sq = sb("sq", [128, 2, TM], BF16)
            rs = sb("rs", [128, TM], F32)
            tmpf = sb("tmpf", [128, 2, TM], F32)
            qk = sb("qk", [128, 8, TM], BF16)
            vT = sb("vT", [128, 4, 1024], BF16)
            glr = sb("glr", [16, TM], BF16)
            e1 = sb("e1", [128, 2, TM], F32)
            cums = sb("cums", [128, 2, TM], F32)
            ee = sb("ee", [128, 8, TM], BF16)
            etot = sb("etot", [128, 4, 4], F32)
            scm = sb("scm", [128, 16, 128], BF16)
            kT = sb("kT", [128, 8, 128], BF16)
            Sf = sb("Sf", [128, 4, 256], F32)
            Sb = sb("Sb", [128, 4, 256], BF16)
            ob = sb("ob", [128, 8, TM], BF16)
            Uj = sb("Uj", [128, 4, 512], BF16)
            U8 = sb("U8", [128, 2, 32, 64], BF16)
            DD = sb("DD", [128, 2, 32, 64], F32)
            Zt = sb("Zt", [128, 2, 64], F32)
            P1 = sb("P1", [128, 64], F32)
            P2 = sb("P2", [128, 64], F32)
            Tb = sb("Tb", [128, 512], F32)
            P2b = sb("P2b", [128, 512], F32)
            Zs = sb("Zs", [128, 9, 64], F32)
            CX = sb("CX", [128, 512], F32)
            CY = sb("CY", [128, 512], F32)
            abr = sb("abr", [128, 1152], F32)
            Sg = sb("Sg", [128, 32, 64], BF16)
            Yim = sb("Yim", [128, 4, TM], BF16)
            stab = sb("stab", [128, 4, 4096], BF16)
            P.nws = 3
            P.w_rr = 0
            wt = sb("wt", [128, P.nws, 4096], BF16)
            x0s = sb("x0s", [128, 32], F32)
            Zfin = sb("Zfin", [128, 32], F32)

            P.dma(abr[:], dap(abrd, ld * 128 * 1152, [[1152, 128], [1, 1152]]), (), ["abr"])
            sgt = sq
            for k in range(4):
                P.dma(stab[:, k, :], dap(stabd, (ld * 4 + k) * 128 * 4096, [[4096, 128], [1, 4096]]), (), ["stab"])

            XKall = [("xt", fc) for fc in range(8)]

            def load_x(k):
                t0_, TT_ = items[k]
                P.dma(xt[:, :, 0:TT_], dap(xsrc, t0_, [[NTOK, 128], [128 * NTOK, 8], [1, TT_]]), (), XKall)

            Y8v = e1.bitcast(BF16)
            titems = []
            merge = (NP % 2 == 0) and (2 * TP == TM)
            si = 0
            while si < len(seqs):
                (tok0_, T_, cnd_, pidx_) = seqs[si]
                if merge and pidx_ >= 0:
                    titems.append((tok0_, TM, cnd_, pidx_, 0, True, True, (pidx_, pidx_ + 1)))
                    si += 2
                    continue
                TT_ = min(TM, T_)
                nt_ = T_ // TT_
                order_ = list(range(nt_)) if dr == 0 else list(range(nt_ - 1, -1, -1))
                for n_, ti_ in enumerate(order_):
                    titems.append((tok0_ + ti_ * TT_, TT_, cnd_, pidx_, ti_, n_ == 0, n_ == nt_ - 1, (pidx_,)))
                si += 1
            items = [(t_[0], t_[1]) for t_ in titems]

            load_x(0)

            def init_states(pidx):
                if pidx < 0:
                    P.dma(Sf[:], dap(gla0_d, ld * 4 * 128 * 256, [[256, 128], [128 * 256, 4], [1, 256]]), (),
                          [("Sf", h) for h in range(4)])
                    P.copy("dve", Sb[:], Sf[:], [("Sf", h) for h in range(4)], [("Sb", h) for h in range(4)])
                    P.dma(Zt[:, 0, 0:32], x0_d.ap()[:, ld * 32:ld * 32 + 32], (), [("Zt", 0)])
                    P.dma(x0s[:], x0s_d.ap()[:, ld * 32:ld * 32 + 32], (), ["x0s"])
                    P.ts("dve", Zt[:, 0, 32:64], x0s[:], SGN, None, ALU.mult, None, ["x0s", "csm"], [("Zt", 0)])
                else:
                    P.memset("dve", Sf[:], 0.0, [("Sf", h) for h in range(4)])
                    P.memset("dve", Sb[:], 0.0, [("Sb", h) for h in range(4)])
                    P.memset("dve", Zt[:, 0, :], 0.0, [("Zt", 0)])

            def tile_gen(k):
                (t0, TT, cnd, pidx, ti, first, lastt, pids) = titems[k]
                two = len(pids) == 2
                pfirst = (pids[0] if dr == 0 else pids[-1])
                plast = (pids[-1] if dr == 0 else pids[0])
                NC = TT // 8
                nch = TT // 128
                ub = k % 2
                XK = [("xt", fc) for fc in range(8)]
                HK = [("ht", fc) for fc in range(8)]
                if l == 0 and pidx < 0 and not passF:
                    r0 = (ti * TT) // 64
                    nr_ = TT // 64
                    P.tt("dve", sap(xt, 0, [[TM, 4], [64, nr_], [1, 64]], 8 * TM), sap(xt, 0, [[TM, 4], [64, nr_], [1, 64]], 8 * TM),
                         sap(pos, r0, [[64, 4], [1, nr_], [0, 64]], 512), ALU.add, XK[0:4] + ["pos"], XK[0:4])
                    P.tt("dve", sap(xt, 4 * TM, [[TM, 4], [64, nr_], [1, 64]], 8 * TM),
                         sap(xt, 4 * TM, [[TM, 4], [64, nr_], [1, 64]], 8 * TM),
                         sap(pos, 256, [[64, 4], [0, nr_], [1, 64]], 512), ALU.add, XK[4:8] + ["pos"], XK[4:8])
                if xdst0 is not None:
                    P.dma(dap(xdst0, t0, [[NTOK, 128], [128 * NTOK, 8], [1, TT]]), xt[:, :, 0:TT], XK, ["xp"])
                norm_mod(xt, None, ht, None, sq, rs, tmpf, A1, 0, l, cnd, TT)
                if k + 1 < len(items):
                    load_x(k + 1)
                yield
                rhs_h = lambda kc, n0, nn: ht[:, kc, n0:n0 + nn]

                slot = P.w_rr
                P.w_rr = (slot + 1) % P.nws
                P.dma(out=sap(wt, slot * 4096, [[1, 4096]], P.nws * 4096), in_=wblock_ap("w_inB", l, 0),
                      r=(), w=[("wt", slot)], eng="pool")
                for uc in range(4):
                    ps, psk = psum()
                    for kc in range(8):
                        P.mm(ps[:, 0:TT], sap(wt, slot * 4096 + kc * 512 + uc * 128, [[1, 128]], P.nws * 4096),
                             sap(ht, kc * TM, [[1, 8], [8, NC]], 8 * TM), kc == 0, kc == 7, [("wt", slot), ("ht", kc)], [psk])
                    P.copy("act", Uj[:, uc, 0:TT], ps[:, 0:TT], [psk], [("Uj", uc)])
                for uc in range(4):
                    P.dma(dap(D1, uc * 128 * 64, [[64, 128], [512 * 64, 8], [1, NC]]), sap(Uj, uc * 512, [[NC, 8], [1, NC]], 2048),
                          [("Uj", uc)], ["D1"])
                for j in range(8):
                    P.dma(sapp(U8, 16 * j, 16, ub * 2048, [[64, 32], [1, NC]], 4096), dap(D1, j * 512 * 64, [[64, 16], [16 * 64, 32], [1, NC]]),
                          ["D1"], [("U8", ub)])

                def ev_qk(mc, ps, psk, n0, nn):
                    if mc < 4:
                        P.act(qk[:, mc, 0:nn], ps[:, 0:nn], AF.Copy, [psk], [("qk", mc)], scale=DK ** -0.5)
                    else:
                        P.copy("act", qk[:, mc, 0:nn], ps[:, 0:nn], [psk], [("qk", mc)])
                linear(wt, "w_in", l, D, DIN, C_Q, C_K + 512, rhs_h, TT, ev_qk, HK)

                def ev_glr(mc, ps, psk, n0, nn):
                    P.copy("act", glr[0:16, 0:nn], ps[0:16, 0:nn], [psk], ["glr"])
                linear(wt, "w_in", l, D, DIN, C_GLR + 16 * dr, C_GLR + 16 * dr + 16, rhs_h, TT, ev_glr, HK)

                for half in range(2):
                    slot = P.w_rr
                    P.w_rr = (slot + 1) % P.nws
                    P.dma(out=sap(wt, slot * 4096, [[1, 4096]], P.nws * 4096), in_=wblock_ap("w_inA", l, 2 + half),
                          r=(), w=[("wt", slot)], eng="pool")
                    for tc in range(nch):
                        ps, psk = psum()
                        for kc in range(8):
                            P.mm(ps[:, 0:512], ht[:, kc, tc * 128:(tc + 1) * 128], sap(wt, slot * 4096 + kc * 512, [[1, 512]], P.nws * 4096),
                                 kc == 0, kc == 7, [("wt", slot), ("ht", kc)], [psk])
                        P.copy("act", vT[:, tc, half * 512:(half + 1) * 512], ps[:, 0:512],
                               [psk], [("vT", tc)])

                yield
                if first:
                    init_states(pidx)
                for hh in range(4):
                    s2 = hh % 2
                    pz, pzk = psum()
                    P.mm(pz[:, 0:TT], wg[0:16, ld * 512 + hh * 128:ld * 512 + (hh + 1) * 128], glr[0:16, 0:TT], True, True,
                         ["wg", "glr"], [pzk])
                    P.act(e1[:, s2, 0:TT], pz[:, 0:TT], AF.Exp, [pzk, "bgneg"], [("e1", s2)], scale=-1.0,
                          bias=bgneg[:, ld * 4 + hh:ld * 4 + hh + 1])
                    P.act(e1[:, s2, 0:TT], e1[:, s2, 0:TT], AF.Ln, [("e1", s2)], [("e1", s2)], bias=1.0)
                    if dr == 0:
                        d1_ap, o_ap = e1[:, s2, 0:TT], cums[:, s2, 0:TT]
                    else:
                        d1_ap = sap(e1, s2 * TM + TT - 1, [[-1, TT]], 2 * TM)
                        o_ap = sap(cums, s2 * TM + TT - 1, [[-1, TT]], 2 * TM)
                    P.op("dve", lambda e, o=o_ap, d1=d1_ap, n=TT: e.tensor_tensor_scan(
                        out=o, data0=rmask[:, 0:n], data1=d1, initial=0.0, op0=ALU.mult, op1=ALU.add),
                        [("e1", s2), "rmask"], [("cums", s2)])
                    P.act(ee[:, hh, 0:TT], cums[:, s2, 0:TT], AF.Exp, [("cums", s2)], [("ee", hh)], scale=-1.0 / 16)
                    P.act(ee[:, 4 + hh, 0:TT], cums[:, s2, 0:TT], AF.Exp, [("cums", s2)], [("ee", 4 + hh)], scale=1.0 / 16)
                    P.act(etot[:, hh, 0:nch], sap(cums, s2 * TM + (127 if dr == 0 else 0), [[128, nch]], 2 * TM), AF.Exp,
                          [("cums", s2)], ["etot"], scale=-1.0 / 16)
                    P.tt("pool", qk[:, hh, 0:TT], qk[:, hh, 0:TT], ee[:, hh, 0:TT], ALU.mult, [("qk", hh), ("ee", hh)], [("qk", hh)])
                    P.tt("pool", qk[:, 4 + hh, 0:TT], qk[:, 4 + hh, 0:TT], ee[:, 4 + hh, 0:TT], ALU.mult,
                         [("qk", 4 + hh), ("ee", 4 + hh)], [("qk", 4 + hh)])

                OK_ = [("ob", h) for h in range(4)]
                if passF:
                    P.dma(ob[:, :, 0:TT], dap(obs, t0, [[NTOK, 128], [128 * NTOK, 8], [1, TT]]), (), OK_)

                corder = list(range(nch)) if dr == 0 else list(range(nch - 1, -1, -1))

                def stage1m(c):
                    cs = slice(c * 128, (c + 1) * 128)
                    for hh in range(4):
                        ix = c * 4 + hh
                        psc, psck = psum()
                        P.mm(psc[:, 0:128], qk[:, 4 + hh, cs], qk[:, hh, cs], True, True, [("qk", 4 + hh), ("qk", hh)], [psck])
                        P.tt("dve", scm[:, ix, :], psc[:, 0:128], trim[:, dr, :], ALU.mult, [psck, "trim"], [("scm", ix)])

                def stage1(c):
                    cs = slice(c * 128, (c + 1) * 128)
                    for hh in range(4):
                        ix = (c % 2) * 4 + hh
                        i2 = ix % 2
                        P.transpose(psbs[i2][:, 0:128], qk[:, 4 + hh, cs], ident_b[:], [("qk", 4 + hh), "ident_b"],
                                    [("psb", i2)])
                        P.copy("act", kT[:, ix, :], psbs[i2][:, 0:128], [("psb", i2)], [("kT", ix)])

                for c_ in corder:
                    stage1m(c_)
                stage1(corder[0])

                for gq in range(4):
                    pa, pak = psum()
                    pb_, pbk = psum()
                    for gi in range(8):
                        g = gq * 8 + gi
                        P.mm(pa[:, gi * NC:(gi + 1) * NC], stab[:, 0, g * 128:(g + 1) * 128], U8[:, ub, g, 0:NC], True, True,
                             ["stab", ("U8", ub)], [pak])
                    for gi in range(8):
                        g = gq * 8 + gi
                        P.mm(pb_[:, gi * NC:(gi + 1) * NC], stab[:, 1, g * 128:(g + 1) * 128], U8[:, ub, g, 0:NC], True, True,
                             ["stab", ("U8", ub)], [pbk])
                    P.copy("act", DD[:, 0, gq * 8:(gq + 1) * 8, 0:NC], pa[:, 0:8 * NC].rearrange("p (a c) -> p a c", c=NC), [pak], ["DD"])
                    P.copy("act", DD[:, 1, gq * 8:(gq + 1) * 8, 0:NC], pb_[:, 0:8 * NC].rearrange("p (a c) -> p a c", c=NC), [pbk], ["DD"])

                NB = NC // 8
                cst_ = 8 if dr == 0 else -8
                rs_ = 1 if dr == 0 else -1
                off = (lambda r, b0: r + 8 * b0) if dr == 0 else (lambda r, b0: NC - 1 - r - 8 * b0)
                dd_full = lambda r, nb: sap(DD, off(r, 0), [[2048, 2], [64, 32], [cst_, nb]], 4096)
                dd_swap = lambda r, nb: sap(DD, off(r, 0) + 2048, [[-2048, 2], [64, 32], [cst_, nb]], 4096)
                AAb = lambda r, nb: sap(abr, r * 128, [[32, 2], [1, 32], [0, nb]], 1152)
                BBb = lambda r, nb: sap(abr, r * 128 + 64, [[32, 2], [1, 32], [0, nb]], 1152)
                Tv = sap(Tb, 0, [[256, 2], [8, 32], [1, NB]], 512)
                Tsw = sap(Tb, 256, [[-256, 2], [8, 32], [1, NB]], 512)
                P2v = sap(P2b, 0, [[256, 2], [8, 32], [1, NB]], 512)
                v3 = lambda t: t[:].rearrange("p (a g) -> p a g", g=32)

                def scan_gen():
                    for r in range(8):
                        if r == 0:
                            P.tt("dve", P2v, dd_swap(0, NB), BBb(1, NB), ALU.mult, ["DD", "abr"], ["P2b"])
                            P.tt("dve", Tv, dd_full(0, NB), AAb(1, NB), ALU.mult, ["DD", "abr"], ["Tb"])
                        else:
                            P.tt("dve", Tv, dd_full(r - 1, NB), dd_full(r, NB), ALU.add, ["DD"], ["Tb"])
                            P.tt("dve", P2v, Tsw, BBb(1, NB), ALU.mult, ["Tb", "abr"], ["P2b"])
                            P.tt("dve", Tv, Tv, AAb(1, NB), ALU.mult, ["Tb", "abr"], ["Tb"])
                        P.tt("dve", dd_full(r, NB), Tv, P2v, ALU.add, ["Tb", "P2b"], ["DD"])
                        yield
                    P.copy("dve", Zs[:, 0, :], Zt[:, 0, :], [("Zt", 0)], ["Zs"])
                    for b in range(NB):
                        P.tt("dve", P1[:], Zs[:, b, :], sap(abr, 8 * 128, [[1, 64]], 1152), ALU.mult, ["Zs", "abr"], ["P1"])
                        P.tt("dve", v3(P2), sap(Zs, b * 64 + 32, [[-32, 2], [1, 32]], 576), sap(abr, 8 * 128 + 64, [[32, 2], [1, 32]], 1152),
                             ALU.mult, ["Zs", "abr"], ["P2"])
                        P.tt("dve", P1[:], P1[:], P2[:], ALU.add, ["P1", "P2"], ["P1"])
                        P.tt("dve", sap(Zs, (b + 1) * 64, [[32, 2], [1, 32]], 576), v3(P1), sap(DD, off(7, b), [[2048, 2], [64, 32]], 4096),
                             ALU.add, ["P1", "DD"], ["Zs"])
                        if two and b == NB // 2 - 1:
                            so_ = (pfirst * DEPTH + l) * 2 + dr
                            P.copy("dve", Zfin[:], Zs[:, b + 1, 0:32], ["Zs"], ["Zfin"])
                            P.dma(s5_out.ap()[:, so_ * 32:so_ * 32 + 32], Zfin[:], ["Zfin"], ["s5_out"])
                            P.memset("dve", Zs[:, b + 1, :], 0.0, ["Zs"])
                        yield
                    for gqr in range(4):
                        g0 = gqr * 8
                        CXv = sap(CX, 0, [[NB * 8, 8], [8, NB], [1, 8]], 512)
                        CYv = sap(CY, 0, [[NB * 8, 8], [8, NB], [1, 8]], 512)
                        P.tt("dve", CXv, sap(Zs, g0, [[1, 8], [64, NB], [0, 8]], 576), sap(abr, g0, [[1, 8], [0, NB], [128, 8]], 1152),
                             ALU.mult, ["Zs", "abr"], ["CX"])
                        P.tt("dve", CYv, sap(Zs, 32 + g0, [[1, 8], [64, NB], [0, 8]], 576), sap(abr, 64 + g0, [[1, 8], [0, NB], [128, 8]], 1152),
                             ALU.mult, ["Zs", "abr"], ["CY"])
                        P.tt("dve", CXv, CXv, CYv, ALU.add, ["CX", "CY"], ["CX"])
                        P.copy("dve", sap(Sg, off(0, 0) + g0 * 64, [[64, 8], [cst_, NB]], 2048), sap(CX, 0, [[NB * 8, 8], [8, NB]], 512),
                               ["CX"], ["Sg"])
                        P.tt("dve", sap(Sg, off(1, 0) + g0 * 64, [[64, 8], [cst_, NB], [rs_, 7]], 2048),
                             sap(DD, off(0, 0) + g0 * 64, [[64, 8], [cst_, NB], [rs_, 7]], 4096),
                             sap(CX, 1, [[NB * 8, 8], [8, NB], [1, 7]], 512), ALU.add, ["DD", "CX"], ["Sg"])
                        yield
                    P.copy("dve", Zt[:, 0, :], Zs[:, NB, :], ["Zs"], [("Zt", 0)])

                chain = scan_gen()

                def pump(n):
                    for _ in range(n):
                        if next(chain, "done") == "done":
                            return

                def chain_finish():
                    pump(64)

                yield
                if passF:
                    def ev_ga(mc, ps, psk, n0, nn):
                        P.act(ee[:, mc, 0:nn], ps[:, 0:nn], AF.Sigmoid, [psk], [("ee", mc)])
                    linear(wt, "w_in", l, D, DIN, C_GA, C_GA + 1024, rhs_h, TT, ev_ga, HK)
                for ci, c in enumerate(corder):
                    cs = slice(c * 128, (c + 1) * 128)
                    if two and ci == nch // 2:
                        so_ = (pfirst * DEPTH + l) * 2 + dr
                        P.dma(dap(gla_out, so_ * 4 * 128 * 256, [[256, 128], [128 * 256, 4], [1, 256]]), Sf[:],
                              [("Sf", h) for h in range(4)], ["gla_out"])
                        P.memset("pool", Sf[:], 0.0, [("Sf", h) for h in range(4)])
                        P.memset("pool", Sb[:], 0.0, [("Sb", h) for h in range(4)])
                    if ci + 1 < nch:
                        stage1(corder[ci + 1])
                    for hh in range(4):
                        ix = (c % 2) * 4 + hh
                        sx = c * 4 + hh
                        po, pok = psum()
                        for v2 in range(2):
                            P.mm(po[:, v2 * 128:(v2 + 1) * 128], vT[:, c, hh * 256 + v2 * 128:hh * 256 + (v2 + 1) * 128],
                                 scm[:, sx, :], True, False, [("vT", c), ("scm", sx)], [pok])
                            P.mm(po[:, v2 * 128:(v2 + 1) * 128], Sb[:, hh, v2 * 128:(v2 + 1) * 128], qk[:, hh, cs], False, not passF,
                                 [("Sb", hh), ("qk", hh)], [pok])
                            if passF:
                                P.mm(po[:, v2 * 128:(v2 + 1) * 128], ident_b[:], ob[:, hh * 2 + v2, cs], False, True,
                                     ["ident_b", ("ob", hh)], [pok])
                        o_out = ob[:, hh * 2:hh * 2 + 2, cs]
                        o_in = po[:, 0:256].rearrange("p (a k) -> p a k", k=128)
                        P.copy("act", o_out, o_in, [pok], [("ob", hh)])
                        pd, pdk = psum()
                        P.mm(pd[:, 0:256], kT[:, ix, :], vT[:, c, hh * 256:(hh + 1) * 256], True, False, [("kT", ix), ("vT", c)], [pdk])
                        P.mm(pd[:, 0:256], ident_f[:], Sf[:, hh, :], False, True, ["ident_f", ("Sf", hh)], [pdk])
                        P.act(Sf[:, hh, :], pd[:, 0:256], AF.Identity, [pdk, "etot"], [("Sf", hh)], scale=etot[:, hh, c:c + 1])
                        P.act(Sb[:, hh, :], pd[:, 0:256], AF.Identity, [pdk, "etot"], [("Sb", hh)], scale=etot[:, hh, c:c + 1])
                        pump(2)

                def s5_out_block():
                    chain_finish()
                    for gq in range(4):
                        py, pyk = psum()
                        for gi in range(8):
                            g = gq * 8 + gi
                            P.mm(py[:, gi * NC:(gi + 1) * NC], stab[:, 2, g * 128:(g + 1) * 128], U8[:, ub, g, 0:NC], True, False,
                                 ["stab", ("U8", ub)], [pyk])
                            P.mm(py[:, gi * NC:(gi + 1) * NC], stab[:, 3, g * 128:(g + 1) * 128], Sg[:, g, 0:NC], False, True,
                                 ["stab", "Sg"], [pyk])
                        P.copy("act", sap(Y8v, gq * 512, [[64, 8], [1, NC]], 2048),
                               py[:, 0:8 * NC].rearrange("p (a c) -> p a c", c=NC), [pyk], [("e1", 0), ("e1", 1)])
                    for i in range(8):
                        P.dma(dap(D2, i * 512 * 64, [[64, 16], [16 * 64, 32], [1, NC]]), sapp(Y8v, 16 * i, 16, 0, [[64, 32], [1, NC]], 2048),
                              [("e1", 0), ("e1", 1)], ["D2"])
                    for uc in range(4):
                        P.dma(sap(Yim, uc * TM, [[NC, 8], [1, NC]], 4 * TM), dap(D2, uc * 128 * 64, [[64, 128], [512 * 64, 8], [1, NC]]),
                              ["D2"], ["Yim"])


                yield
                if not passF:
                    s5_out_block()
                    P.dma(dap(ybs, t0, [[NTOK, 128], [128 * NTOK, 4], [1, TT]]), Yim[:, :, 0:TT], ["Yim"], ["ybs"])
                    P.dma(dap(obs, t0, [[NTOK, 128], [128 * NTOK, 8], [1, TT]]), ob[:, :, 0:TT], OK_, ["obs"])
                else:

                    VK = [("vT", i) for i in range(4)]
                    rs2 = [e1[:, 0, 0:TT], e1[:, 1, 0:TT], cums[:, 0, 0:TT], cums[:, 1, 0:TT]]
                    rs2k = [("e1", 0), ("e1", 1), ("cums", 0), ("cums", 1)]
                    for hh in range(4):
                        pn, pnk = psum()
                        for v2 in range(2):
                            P.act(sq[:, v2, 0:TT], ob[:, hh * 2 + v2, 0:TT], AF.Square, [("ob", hh)], [("sq", v2)])
                            P.mm(pn[:, 0:TT], ones_b[:], sq[:, v2, 0:TT], v2 == 0, v2 == 1, [("sq", v2), "ones_b"], [pnk])
                        P.act(rs2[hh], pn[:, 0:TT], AF.Ln, [pnk], [rs2k[hh]], scale=1.0 / DV, bias=EPS)
                        P.act(rs2[hh], rs2[hh], AF.Exp, [rs2k[hh]], [rs2k[hh]], scale=-0.5)

                    def ev_r(vc, ps, psk, n0, nn):
                        i2 = vc % 2
                        hh = vc // 2
                        P.act(sgt[:, i2, 0:nn], ps[:, 0:nn], AF.Silu, [psk], [("sq", i2)])
                        P.stt(tmpf[:, i2, 0:nn], ob[:, vc, 0:nn], gng[:, l * 2 + i2:l * 2 + i2 + 1], rs2[hh], ALU.mult, ALU.mult,
                              [("ob", hh), "gng", rs2k[hh]], [("tmpf", i2)])
                        P.tt("dve", ob[:, vc, 0:nn], tmpf[:, i2, 0:nn], sgt[:, i2, 0:nn], ALU.mult, [("tmpf", i2), ("sq", i2)], [("ob", hh)])
                    linear(wt, "w_in", l, D, DIN, C_R, C_R + 1024, rhs_h, TT, ev_r, HK)


                    s5_out_block()

                    def ev_pg(mc, ps, psk, n0, nn):
                        P.tt("dve", qk[:, mc, 0:nn], ps[:, 0:nn], ee[:, mc, 0:nn], ALU.mult, [psk, ("ee", mc)], [("qk", mc)])
                    linear(wt, "w_pg", l, D, D, 0, D, lambda kc, n0, nn: ob[:, kc, n0:n0 + nn], TT, ev_pg, OK_)

                    def ev_gb(mc, ps, psk, n0, nn):
                        P.act(ee[:, mc, 0:nn], ps[:, 0:nn], AF.Sigmoid, [psk], [("ee", mc)])
                    linear(wt, "w_in", l, D, DIN, C_GB, C_GB + 1024, rhs_h, TT, ev_gb, HK)
                    yield

                    ybv = sap(vT, 2048, [[TM, 4], [1, TT]], 4096)
                    P.dma(ybv, dap(ybs, t0, [[NTOK, 128], [128 * NTOK, 4], [1, TT]]), (), VK[2:4])
                    P.tt("dve", ybv, Yim[:, :, 0:TT], ybv, ALU.add, ["Yim"] + VK[2:4], VK[2:4])
                    P.act(Yim[:, :, 0:TT], ybv, AF.Gelu_apprx_tanh, VK[2:4], ["Yim"])

                    def ev_glu(mc, ps, psk, n0, nn):
                        i2 = mc % 2
                        P.act(sgt[:, i2, 0:nn], ps[:, 0:nn], AF.Sigmoid, [psk, "bglu"], [("sq", i2)], bias=bglu[:, l * 4 + mc:l * 4 + mc + 1])
                        P.tt("dve", sap(vT, mc * TM, [[1, nn]], 4096), Yim[:, mc, 0:nn], sgt[:, i2, 0:nn], ALU.mult,
                             ["Yim", ("sq", i2)], [("vT", mc // 2)])
                    linear(wt, "w_glu", l, 512, 512, 0, 512, lambda kc, n0, nn: Yim[:, kc, n0:n0 + nn], TT, ev_glu, ["Yim"])

                    def ev_ps(mc, ps, psk, n0, nn):
                        i2 = mc % 2
                        P.tt("dve", tmpf[:, i2, 0:nn].rearrange("p (c i) -> p c i", i=8), sap(ps, 0, [[1, NC], [NC, 8]], 512),
                             ee[:, mc, 0:nn].rearrange("p (c i) -> p c i", i=8), ALU.mult, [psk, ("ee", mc)], [("tmpf", i2)])
                        P.tt("dve", qk[:, mc, 0:nn], tmpf[:, i2, 0:nn], qk[:, mc, 0:nn], ALU.add, [("tmpf", i2), ("qk", mc)], [("qk", mc)])
                    linear(wt, "w_ps", l, 512, D, 0, D, lambda kc, n0, nn: sap(vT, kc * TM + n0, [[1, nn]], 4096), TT, ev_ps, VK[0:2])

                    def ev_out(mc, ps, psk, n0, nn):
                        i2 = mc % 2
                        P.dma(tmpf[:, i2, 0:nn], dap(xsrc, mc * 128 * NTOK + t0, [[NTOK, 128], [1, nn]]), (), [("tmpf", i2)])
                        P.stt(tmpf[:, i2, 0:nn], ps[:, 0:nn], modv[:, l, 16 + mc, cnd:cnd + 1], tmpf[:, i2, 0:nn], ALU.mult, ALU.add,
                              [psk, "modv", ("tmpf", i2)], [("tmpf", i2)])
                        P.dma(dap(xm, mc * 128 * NTOK + t0, [[NTOK, 128], [1, nn]]), tmpf[:, i2, 0:nn], [("tmpf", i2)], ["xm"])
                    linear(wt, "w_out", l, D, D, 0, D, lambda kc, n0, nn: qk[:, kc, n0:n0 + nn], TT, ev_out, [("qk", i) for i in range(8)])

                if lastt and pidx >= 0:
                    so = (plast * DEPTH + l) * 2 + dr
                    P.dma(dap(gla_out, so * 4 * 128 * 256, [[256, 128], [128 * 256, 4], [1, 256]]), Sf[:],
                          [("Sf", h) for h in range(4)], ["gla_out"])
                    P.dma(s5_out.ap()[:, so * 32:so * 32 + 32], Zt[:, 0, 0:32], [("Zt", 0)], ["s5_out"])

            ntl = len(titems)
            gens = {0: tile_gen(0)}
            next(gens[0])
            next(gens[0])
            for k in range(ntl):
                g = gens[k]
                nxt = None
                if k + 1 < ntl:
                    nxt = gens[k + 1] = tile_gen(k + 1)
                next(g)
                if not passF and nxt is not None:
                    next(nxt)
                next(g)
                if not passF and nxt is not None:
                    next(nxt)
                if passF:
                    next(g)
                    if nxt is not None:
                        next(nxt)
                    next(g, None)
                    if nxt is not None:
                        next(nxt)
                else:
                    next(g, None)
            P.flush()

    def mlp_phase(l, last):
        TM = 1024
        segs = []
        if NP > 0:
            segs.append((0, NP * TP, 0))
        segs.append((NP * TP, TS, 1))
        with ExitStack() as ph:
            def sb(name, shape, dt):
                return ph.enter_context(nc.sbuf_tensor(f"{name}_p{P.phase_no}", list(shape), dt))
            xts = [sb("xt2a", [128, 8, TM], F32), sb("xt2b", [128, 8, TM], F32)]
            hts = [sb("ht2a", [128, 8, TM], BF16), sb("ht2b", [128, 8, TM], BF16)]
            hid = sb("hid", [128, 32, TM], BF16)
            sq = sb("sq2", [128, 2, 512], BF16)
            rs = sb("rs_m", [128, 512], F32)
            tmpf = sb("tmpf2", [128, 2, 512], F32)
            rl = sb("rl", [128, 2, 512], BF16)
            P.nws = 2
            P.w_rr = 0
            wt = sb("wt2", [128, P.nws, 4096], BF16)
            dst = yT if last else xn
            tiles = []
            for (s0, slen, cnd) in segs:
                for t0 in range(s0, s0 + slen, TM):
                    tiles.append((t0, min(TM, s0 + slen - t0), cnd))

            def front(k):
                t0, TT, cnd = tiles[k]
                sl2 = k % 2
                xt, ht = xts[sl2], hts[sl2]
                xkn, hkn = f"xt{sl2}", f"ht{sl2}"
                P.dma(xt[:, :, 0:TT], dap(xm, t0, [[NTOK, 128], [128 * NTOK, 8], [1, TT]]), (), [(xkn, fc) for fc in range(8)])
                norm_mod(xt, xkn, ht, hkn, sq, rs, tmpf, A2, 24, l, cnd, TT)

            front(0)
            for k, (t0, TT, cnd) in enumerate(tiles):
                sl2 = k % 2
                xt, ht = xts[sl2], hts[sl2]
                xkn, hkn = f"xt{sl2}", f"ht{sl2}"
                XK = [(xkn, fc) for fc in range(8)]
                HK = [(hkn, fc) for fc in range(8)]

                def ev_ff1(mc, ps, psk, n0, nn):
                    i2 = (mc + n0 // 512) % 2
                    P.act(rl[:, i2, 0:nn], ps[:, 0:nn], AF.Relu, [psk], [("rl", i2)])
                    P.tt("dve", hid[:, mc, n0:n0 + nn], rl[:, i2, 0:nn], rl[:, i2, 0:nn], ALU.mult, [("rl", i2)], [("hid", mc)])
                linear(wt, "w_ff1", l, D, DFF, 0, DFF, lambda kc, n0, nn: ht[:, kc, n0:n0 + nn], TT, ev_ff1, HK)

                if k + 1 < len(tiles):
                    front(k + 1)

                def ev_ff2(mc, ps, psk, n0, nn):
                    P.stt(xt[:, mc, n0:n0 + nn], ps[:, 0:nn], modv[:, l, 40 + mc, cnd:cnd + 1], xt[:, mc, n0:n0 + nn], ALU.mult, ALU.add,
                          [psk, "modv", (xkn, mc)], [(xkn, mc)])
                linear(wt, "w_ff2", l, DFF, D, 0, D, lambda kc, n0, nn: hid[:, kc, n0:n0 + nn], TT, ev_ff2,
                       [("hid", i) for i in range(32)])
                if last:
                    for n0 in range(0, TT, 512):
                        nn = min(512, TT - n0)
                        pn, pnk = psum()
                        for fc in range(8):
                            i2 = fc % 2
                            P.act(sq[:, i2, 0:nn], xt[:, fc, n0:n0 + nn], AF.Square, [(xkn, fc)], [("sq", i2)])
                            P.mm(pn[:, 0:nn], ones_b[:], sq[:, i2, 0:nn], fc == 0, fc == 7, [("sq", i2), "ones_b"], [pnk])
                        P.act(rs[:, 0:nn], pn[:, 0:nn], AF.Ln, [pnk], ["rs"], scale=1.0 / D, bias=EPS)
                        P.act(rs[:, 0:nn], rs[:, 0:nn], AF.Exp, ["rs"], ["rs"], scale=-0.5)
                        for fc in range(8):
                            P.stt(xt[:, fc, n0:n0 + nn], xt[:, fc, n0:n0 + nn], fng[:, fc:fc + 1], rs[:, 0:nn], ALU.mult, ALU.mult,
                                  [(xkn, fc), "fng", "rs"], [(xkn, fc)])
                P.dma(dap(dst, t0, [[NTOK, 128], [128 * NTOK, 8], [1, TT]]), xt[:, :, 0:TT], XK, ["dst"])
            P.flush()

    for l in range(DEPTH):
        if l == 0:
            mixer_phase(l, 1, False, xT, xp)
            mixer_phase(l, 0, True, xp, None)
        else:
            mixer_phase(l, 1, False, xn, None)
            mixer_phase(l, 0, True, xn, None)
        mlp_phase(l, l == DEPTH - 1)
    P.final_wait()
    build.n_inst = P.n_inst
    return nc


def _fm_vec(v):
    v = np.asarray(v, np.float32)
    lead = v.shape[:-1]
    n = v.shape[-1] // 128
    v = v.reshape(lead + (n, 128))
    v = np.moveaxis(v, -1, 0)
    return np.ascontiguousarray(v).reshape(128, -1)


def _constants():
    j = np.arange(128)
    ident = np.eye(128, dtype=np.float32)
    trif = (j[:, None] <= j[None, :]).astype(np.float32)
    trib = (j[:, None] >= j[None, :]).astype(np.float32)
    blk = j // 16
    bmf = (blk[None, :] >= blk[:, None]).astype(np.float32)
    bmb = (blk[None, :] <= blk[:, None]).astype(np.float32)
    rmask = np.ones((128, 512), np.float32)
    rmask[:, ::128] = 0.0
    cst = np.concatenate([ident, trif, trib, bmf, bmb, rmask, np.zeros((128, 128), np.float32)], axis=1)
    csm = np.zeros((128, 32), np.float32)
    top = (j < 64)
    csm[:, 0] = np.where(top, -1.0, 1.0)
    hp = math.pi / 2
    csm[:, 1] = np.where(top, hp, math.pi)
    csm[:, 2] = np.where(top, 0.0, hp)
    csm[:, 3] = np.where(top, hp, math.pi)
    csm[:, 4] = np.where(top, math.pi, 3 * hp)
    csm[:, 5] = hp
    csm[:, 6] = 0.0
    csm[:, 7:15] = np.arange(1, 9, dtype=np.float32)[None, :]
    csm[:, 15:23] = (8 - np.arange(8, dtype=np.float32))[None, :]
    csm[:, 23:32] = (8.0 * np.arange(9, dtype=np.float32))[None, :]
    quarter = D // 4
    omega = (1.0 / (10000.0 ** (np.arange(quarter, dtype=np.float32) / quarter))).astype(np.float32)
    rr = np.arange(64, dtype=np.float32)
    ang = (rr[:, None] * omega[None, :]).astype(np.float32)
    tab = np.concatenate([np.sin(ang), np.cos(ang)], axis=1).astype(np.float32)
    fm = np.ascontiguousarray(tab.T.reshape(4, 128, 64).transpose(1, 0, 2)).reshape(128, 256)
    pos = np.concatenate([fm, fm], axis=1).astype(np.float32)
    return cst, csm, pos


def _prep_shared(inp):
    f = lambda a: np.ascontiguousarray(np.asarray(a, np.float32))
    sh = {}
    sh["w_mod"] = f(inp["w_mod"])
    sh["b_modT"] = _fm_vec(inp["b_mod"])
    sh["n1g"] = _fm_vec(inp["norm1_g"])
    sh["n2g"] = _fm_vec(inp["norm2_g"])
    sh["fng"] = _fm_vec(inp["final_g"])
    sh["w_in"] = f(inp["w_in"])
    sh["w_pg"] = f(inp["w_proj_gla"])
    sh["w_glu"] = f(inp["w_glu"])
    sh["w_ps"] = f(inp["w_proj_s5"])
    sh["w_out"] = f(inp["w_out"])
    sh["w_ff1"] = f(inp["w_ff1"])
    sh["w_ff2"] = f(inp["w_ff2"])
    sh["w_gu"] = np.ascontiguousarray(f(inp["w_gate_up"]).transpose(2, 0, 1, 3)).reshape(16, -1)
    sh["bgT"] = _fm_vec(inp["b_gate"])
    sh["gngT"] = _fm_vec(inp["gla_norm_g"])
    sh["bgluT"] = _fm_vec(inp["b_glu"])
    d = f(inp["s5_d"]).reshape(DEPTH, 32, 16)
    d8 = np.broadcast_to(d.transpose(2, 0, 1)[None], (8, 16, DEPTH, 32))
    sh["d8"] = np.ascontiguousarray(d8).reshape(128, -1)

    def klay(a):
        a = f(a).transpose(3, 0, 1, 2)
        return np.ascontiguousarray(np.concatenate([a, a], axis=0)).reshape(128, -1)
    sh["lre2"] = klay(inp["s5_lam_re"])
    sh["lim2"] = klay(inp["s5_lam_im"])
    ls = np.broadcast_to(f(inp["s5_log_step"])[None], (128, DEPTH, 2, 32))
    sh["lst2"] = np.ascontiguousarray(ls).reshape(128, -1)

    def klayB(a):
        a = f(a).transpose(3, 0, 1, 2, 4)
        return np.ascontiguousarray(np.concatenate([a, a], axis=0)).reshape(128, -1)

    def klayC(a):
        a = f(a).transpose(4, 0, 1, 2, 3)
        return np.ascontiguousarray(np.concatenate([a, a], axis=0)).reshape(128, -1)
    sh["Bre2"] = klayB(inp["s5_b_re"])
    sh["Bim2"] = klayB(inp["s5_b_im"])
    sh["Cre2"] = klayC(inp["s5_c_re"])
    sh["Cim2"] = klayC(inp["s5_c_im"])
    cst, csm, pos = _constants()
    sh["cst"], sh["csm"], sh["pos"] = cst, csm, pos
    return sh


def _prep_core(inp, core, NP, TP, TS):
    f = lambda a: np.asarray(a, np.float32)
    xp = f(inp["x_prompt"])[core * NP:(core + 1) * NP].reshape(NP * TP, D)
    xs = f(inp["x_sample"])[core]
    m = {}
    m["xT"] = np.ascontiguousarray(np.concatenate([xp, xs], axis=0).T)
    cond = np.stack([f(inp["c_ctx"]), f(inp["c"])[core]], axis=-1)
    m["condT"] = np.ascontiguousarray(cond.reshape(8, 128, 2).transpose(1, 0, 2)).reshape(128, 16)
    m["gla0"] = np.ascontiguousarray(f(inp["cache_gla_state"])[core]).reshape(-1, 256)
    re = f(inp["state_s5_re"])[core].transpose(3, 0, 1, 2)
    im = f(inp["state_s5_im"])[core].transpose(3, 0, 1, 2)
    m["s5x0"] = np.ascontiguousarray(np.concatenate([re, im], axis=0)).reshape(128, -1)
    m["s5x0s"] = np.ascontiguousarray(np.concatenate([im, re], axis=0)).reshape(128, -1)
    return m


_NC_CACHE = {}


def run_cfg(inp, NP, TP, TS, ncores):
    key = (NP, TP, TS)
    if key not in _NC_CACHE:
        _NC_CACHE[key] = build(key)
    nc = _NC_CACHE[key]
    sh = _prep_shared(inp)
    in_maps = []
    for c in range(ncores):
        m = dict(sh)
        m.update(_prep_core(inp, c, NP, TP, TS))
        in_maps.append(m)
    res = run_bass_kernel_spmd(nc, in_maps, core_ids=list(range(ncores)))
    B = NP * ncores
    y_prompt = np.zeros((B, TP, D), np.float32)
    y_sample = np.zeros((ncores, TS, D), np.float32)
    gla = np.zeros((B, DEPTH, 2, H, DK, DV), np.float32)
    s5re = np.zeros((B, DEPTH, 2, G, 64), np.float32)
    s5im = np.zeros((B, DEPTH, 2, G, 64), np.float32)
    for c in range(ncores):
        r = res.results[c]
        y = np.asarray(r["yT"], np.float32).T
        y_prompt[c * NP:(c + 1) * NP] = y[:NP * TP].reshape(NP, TP, D)
        y_sample[c] = y[NP * TP:]
        gla[c * NP:(c + 1) * NP] = np.asarray(r["gla_out"], np.float32).reshape(NP, DEPTH, 2, H, DK, DV)
        s = np.asarray(r["s5_out"], np.float32).reshape(2, 64, NP, DEPTH, 2, G)
        s5re[c * NP:(c + 1) * NP] = s[0].transpose(1, 2, 3, 4, 0)
        s5im[c * NP:(c + 1) * NP] = s[1].transpose(1, 2, 3, 4, 0)
    return (y_prompt, y_sample, gla, s5re, s5im)


def kernel(**inputs):
    return run_cfg(inputs, 4, 256, 4096, 8)
```

```python
import math
from contextlib import ExitStack

import numpy as np
import concourse.bass as bass
import concourse.mybir as mybir
from concourse.bass_utils import run_bass_kernel_spmd

F32 = mybir.dt.float32
BF16 = mybir.dt.bfloat16
I32 = mybir.dt.int32
AF = mybir.ActivationFunctionType
ALU = mybir.AluOpType

D = 1024
KC = 8
H = 4
DK = 128
DV = 256
G = 32
DFF = 4096
DIN = 5664
DEPTH = 2
EPS = 1e-6
C_Q, C_K, C_V, C_R, C_GLR, C_U, C_GA, C_GB = 0, 512, 1024, 2048, 3072, 3104, 3616, 4640
TWO_PI = 2.0 * math.pi
NLANES = 24
NHW = 16
NWS = 3


class Op:
    __slots__ = ("eng", "fn", "deps", "needed", "cnt", "dma", "lane", "lane_total", "lane_prev")

    def __init__(self, eng, fn, dma):
        self.eng = eng
        self.fn = fn
        self.dma = dma
        self.deps = ()
        self.needed = False
        self.cnt = 0
        self.lane = -1
        self.lane_total = 0
        self.lane_prev = 0


class KeyState:
    __slots__ = ("w", "r")

    def __init__(self):
        self.w = None
        self.r = []


class Prog:
    ENGS = ("pe", "act", "dve", "pool", "sp")

    def __init__(self, nc, gstack):
        self.nc = nc
        self.gstack = gstack
        self.ops = []
        self.keys = {}
        self.lane_sems = [gstack.enter_context(nc.semaphore(f"lane{i}")) for i in range(NLANES)]
        self.lane_tot = [0] * NLANES
        self.next_lane = 0
        self.next_sw_lane = NHW
        self.prev_final = []
        self.phase_no = 0
        self.ps_rr = 0
        self.w_rr = 0
        self.n_inst = 0

    def op(self, eng, fn, reads=(), writes=(), dma=False):
        o = Op(eng, fn, dma)
        deps = []
        for k in reads:
            st = self.keys.get(k)
            if st is None:
                st = self.keys[k] = KeyState()
            if st.w is not None:
                deps.append(st.w)
        for k in writes:
            st = self.keys.get(k)
            if st is None:
                st = self.keys[k] = KeyState()
            if st.w is not None:
                deps.append(st.w)
            deps.extend(st.r)
        for k in reads:
            self.keys[k].r.append(o)
        for k in writes:
            st = self.keys[k]
            st.w = o
            st.r = []
        if dma:
            if eng == "pool":
                o.lane = self.next_sw_lane
                self.next_sw_lane = NHW + (self.next_sw_lane + 1 - NHW) % (NLANES - NHW)
            else:
                o.lane = self.next_lane
                self.next_lane = (self.next_lane + 1) % NHW
            o.lane_prev = self.lane_tot[o.lane]
            self.lane_tot[o.lane] += 16
            o.lane_total = self.lane_tot[o.lane]
            o.needed = True
        o.deps = [d for d in deps if d is not o and not (d.eng == "pe" and eng == "pe" and not d.dma)]
        self.ops.append(o)
        return o

    def flush(self):
        nc = self.nc
        ops = self.ops
        self.ops = []
        self.keys = {}
        self.phase_no += 1
        sems = {e: self.gstack.enter_context(nc.semaphore(f"ph{self.phase_no}_{e}")) for e in self.ENGS}
        for o in ops:
            for d in o.deps:
                d.needed = True
        per = {e: [] for e in self.ENGS}
        for o in ops:
            per[o.eng].append(o)
        for e in self.ENGS:
            if per[e]:
                last = per[e][-1]
                last.needed = True
        cnt = {e: 0 for e in self.ENGS}
        for o in ops:
            if o.needed and not o.dma:
                cnt[o.eng] += 1
                o.cnt = cnt[o.eng]
        prev_final = self.prev_final
        lane_sems = self.lane_sems
        self.n_inst += len(ops)

        def emit(engname, eng):
            seen = {}

            def wait(sem, val):
                if val <= 0:
                    return
                k = id(sem)
                if seen.get(k, 0) >= val:
                    return
                seen[k] = val
                eng.wait_ge(sem, val)

            for (s, v) in prev_final:
                wait(s, v)
            for o in per[engname]:
                for d in o.deps:
                    if d.dma:
                        wait(lane_sems[d.lane], d.lane_total)
                    else:
                        wait(sems[d.eng], d.cnt)
                if o.dma:
                    wait(lane_sems[o.lane], o.lane_prev)
                ins = o.fn(eng)
                if o.dma:
                    ins.then_inc(lane_sems[o.lane], 16)
                elif o.needed:
                    ins.then_inc(sems[o.eng], 1)

        with nc.Block() as block:
            if per["pe"]:
                @block.tensor
                def _(eng):
                    emit("pe", eng)
            if per["act"]:
                @block.scalar
                def _(eng):
                    emit("act", eng)
            if per["dve"]:
                @block.vector
                def _(eng):
                    emit("dve", eng)
            if per["pool"]:
                @block.gpsimd
                def _(eng):
                    emit("pool", eng)
            if per["sp"]:
                @block.sync
                def _(eng):
                    emit("sp", eng)
        pf = []
        for e in self.ENGS:
            if cnt[e] > 0:
                pf.append((sems[e], cnt[e]))
        for i in range(NLANES):
            if self.lane_tot[i] > 0:
                pf.append((lane_sems[i], self.lane_tot[i]))
        self.prev_final = pf

    def final_wait(self):
        nc = self.nc
        pf = self.prev_final
        with nc.Block() as block:
            @block.sync
            def _(eng):
                for (s, v) in pf:
                    eng.wait_ge(s, v)

    def mm(self, out, lhsT, rhs, start, stop, r, w):
        return self.op("pe", lambda e: e.matmul(out, lhsT=lhsT, rhs=rhs, start=start, stop=stop), r, w)

    def transpose(self, out, in_, ident, r, w):
        return self.op("pe", lambda e: e.transpose(out=out, in_=in_, identity=ident), r, w)

    def act(self, out, in_, func, r, w, scale=1.0, bias=None):
        if bias is None:
            return self.op("act", lambda e: e.activation(out=out, in_=in_, func=func, scale=scale), r, w)
        return self.op("act", lambda e: e.activation(out=out, in_=in_, func=func, scale=scale, bias=bias), r, w)

    def tt(self, eng, out, in0, in1, op, r, w):
        return self.op(eng, lambda e: e.tensor_tensor(out=out, in0=in0, in1=in1, op=op), r, w)

    def ts(self, eng, out, in0, s1, s2, op0, op1, r, w):
        if s2 is None:
            return self.op(eng, lambda e: e.tensor_scalar(out=out, in0=in0, scalar1=s1, scalar2=None, op0=op0), r, w)
        return self.op(eng, lambda e: e.tensor_scalar(out=out, in0=in0, scalar1=s1, scalar2=s2, op0=op0, op1=op1), r, w)

    def stt(self, out, in0, scalar, in1, op0, op1, r, w):
        return self.op("dve", lambda e: e.scalar_tensor_tensor(out=out, in0=in0, scalar=scalar, in1=in1, op0=op0, op1=op1), r, w)

    def copy(self, eng, out, in_, r, w):
        if eng == "act":
            return self.act(out, in_, AF.Copy, r, w)
        return self.op(eng, lambda e: e.tensor_copy(out=out, in_=in_), r, w)

    def memset(self, eng, ap, val, w):
        return self.op(eng, lambda e: e.memset(ap, val), (), w)

    def dma(self, out, in_, r, w, eng="sp", **kw):
        return self.op(eng, lambda e: e.dma_start(out=out, in_=in_, **kw), r, w, dma=True)


def sap(t, off, dims, rowsize):
    return bass.AP(t, off, [[rowsize, 128]] + [list(d) for d in dims])


def sapp(t, p0, npart, off, dims, rowsize):
    return bass.AP(t, p0 * rowsize + off, [[rowsize, npart]] + [list(d) for d in dims])


def dap(t, off, dims):
    return bass.AP(t, off, [list(d) for d in dims])


BIGW = (("w_in", D, DIN), ("w_pg", D, D), ("w_glu", 512, 512), ("w_ps", 512, D), ("w_out", D, D),
        ("w_ff1", D, DFF), ("w_ff2", DFF, D))


def build(cfg, debug=None):
    NP, TP, TS = cfg
    NTOK = NP * TP + TS
    nc = bass.Bass("TRN2", target_bir_lowering=False)
    gs = ExitStack()

    def din(name, shape, dt=F32):
        return nc.dram_tensor(name, list(shape), dt, kind="ExternalInput")

    def dout(name, shape, dt=F32):
        return nc.dram_tensor(name, list(shape), dt, kind="ExternalOutput")

    def dscr(name, shape, dt):
        return nc.dram_tensor(name, list(shape), dt, kind="Internal")

    def gsb(name, shape, dt):
        return gs.enter_context(nc.sbuf_tensor(name, list(shape), dt))

    xT = din("xT", [D, NTOK])
    condT = din("condT", [128, 16])
    w_mod = din("w_mod", [DEPTH, D, 6 * D])
    b_modT = din("b_modT", [128, DEPTH * 48])
    n1g_d = din("n1g", [128, DEPTH * 8])
    n2g_d = din("n2g", [128, DEPTH * 8])
    fng_d = din("fng", [128, 8])
    wsrc = {n: din(n, [DEPTH, k, m]) for (n, k, m) in BIGW}
    w_gu = din("w_gu", [16, DEPTH * 2 * 512])
    bg_d = din("bgT", [128, DEPTH * 2 * 4])
    gng_d = din("gngT", [128, DEPTH * 2])
    bglu_d = din("bgluT", [128, DEPTH * 4])
    d8_d = din("d8", [128, DEPTH * 32])
    lre_d = din("lre2", [128, DEPTH * 2 * 32])
    lim_d = din("lim2", [128, DEPTH * 2 * 32])
    lst_d = din("lst2", [128, DEPTH * 2 * 32])
    Bre_d = din("Bre2", [128, DEPTH * 2 * 32 * 16])
    Bim_d = din("Bim2", [128, DEPTH * 2 * 32 * 16])
    Cre_d = din("Cre2", [128, DEPTH * 2 * 32 * 16])
    Cim_d = din("Cim2", [128, DEPTH * 2 * 32 * 16])
    gla0_d = din("gla0", [DEPTH * 2 * 4 * 128, 256])
    x0_d = din("s5x0", [128, DEPTH * 2 * 32])
    x0s_d = din("s5x0s", [128, DEPTH * 2 * 32])
    cst_d = din("cst", [128, 1280])
    csm_d = din("csm", [128, 32])
    pos_d = din("pos", [128, 512])

    yT = dout("yT", [D, NTOK])
    gla_out = dout("gla_out", [NP * DEPTH * 2 * 4 * 128, 256])
    s5_out = dout("s5_out", [128, NP * DEPTH * 2 * 32])

    STREAMS = {"w_inA": ("w_in", 0, 3072, D, DIN), "w_inB": ("w_in", C_U, 2560, D, DIN), "w_inG": ("w_in", C_GLR, 32, D, DIN),
               "w_pg": ("w_pg", 0, D, D, D), "w_glu": ("w_glu", 0, 512, 512, 512), "w_ps": ("w_ps", 0, D, 512, D),
               "w_out": ("w_out", 0, D, D, D), "w_ff1": ("w_ff1", 0, DFF, D, DFF), "w_ff2": ("w_ff2", 0, D, DFF, D)}

    def sgeom(sn):
        src, c0, ncols, K, Msrc = STREAMS[sn]
        KCn = K // 128
        bw = min(512, 4096 // KCn, ncols)
        return src, c0, ncols, K, Msrc, KCn, bw, ncols // bw
    wb = {}
    for sn in STREAMS:
        _, _, _, _, _, KCn_, bw_, nblk_ = sgeom(sn)
        wb[sn] = dscr(sn + "_bf", [DEPTH * nblk_ * 128, KCn_ * bw_], BF16)

    def wstream(name, m0):
        if name == "w_in":
            if C_GLR <= m0 < C_U:
                return "w_inG", None
            sn = "w_inA" if m0 < C_GLR else "w_inB"
        else:
            sn = name
        _, c0, _, _, _, _, bw, _ = sgeom(sn)
        assert (m0 - c0) % bw == 0, (name, m0)
        return sn, (m0 - c0) // bw

    def wblock_ap(sn, l, blk):
        _, _, _, _, _, KCn, bw, nblk = sgeom(sn)
        return dap(wb[sn], (l * nblk + blk) * 128 * KCn * bw, [[KCn * bw, 128], [1, KCn * bw]])
    xm = dscr("xm", [D, NTOK], F32)
    xn = dscr("xn", [D, NTOK], F32)
    xp = dscr("xp", [D, NTOK], F32)
    obs = dscr("obs", [D, NTOK], BF16)
    ybs = dscr("ybs", [512, NTOK], BF16)
    D1 = dscr("D1", [8, 512, 64], BF16)
    D2 = dscr("D2", [8, 512, 64], BF16)
    stabd = dscr("stabd", [DEPTH * 2 * 4, 128, 4096], BF16)
    abrd = dscr("abrd", [DEPTH * 2, 128, 1152], F32)

    P = Prog(nc, gs)

    ident_f = gsb("ident_f", [128, 128], F32)
    ident_b = gsb("ident_b", [128, 128], BF16)
    ones_b = gsb("ones_b", [128, 128], BF16)
    trim = gsb("trim", [128, 2, 128], BF16)
    bmask = gsb("bmask", [128, 2, 128], F32)
    rmask = gsb("rmask", [128, 512], F32)
    csm = gsb("csm_s", [128, 32], F32)
    pos = gsb("pos_s", [128, 512], F32)
    modv = gsb("modv", [128, DEPTH, 48, 2], F32)
    A1 = gsb("A1", [128, DEPTH, 8, 2], F32)
    A2 = gsb("A2", [128, DEPTH, 8, 2], F32)
    n1g = gsb("n1g_s", [128, DEPTH * 8], F32)
    n2g = gsb("n2g_s", [128, DEPTH * 8], F32)
    fng = gsb("fng_s", [128, 8], F32)
    bgneg = gsb("bgneg", [128, DEPTH * 2 * 4], F32)
    gng = gsb("gng_s", [128, DEPTH * 2], F32)
    bglu = gsb("bglu_s", [128, DEPTH * 4], F32)
    d8 = gsb("d8_s", [128, DEPTH * 32], F32)
    wg = gsb("wg", [16, DEPTH * 2 * 512], BF16)
    AAt = gsb("AAt", [128, DEPTH * 2, 2, 32], F32)
    BBt = gsb("BBt", [128, DEPTH * 2, 2, 32], F32)
    NPSF = 6
    psf = [gs.enter_context(nc.psum_tensor(f"ps{i}", [128, 512], F32)) for i in range(NPSF)]
    psbs = [gs.enter_context(nc.psum_tensor(f"psb{i}", [128, 1024], BF16)) for i in range(2)]

    def psum():
        i = P.ps_rr
        P.ps_rr = (i + 1) % NPSF
        return psf[i], ("ps", i)

    SGN = csm[:, 0:1]
    PHS = [csm[:, 1 + i:2 + i] for i in range(4)]
    PHC = [csm[:, 5:6], csm[:, 6:7]]

    with ExitStack() as ph:
        def sb(name, shape, dt):
            return ph.enter_context(nc.sbuf_tensor(f"{name}_p{P.phase_no}", list(shape), dt))

        for sn in STREAMS:
            src, c0, ncols, K, Msrc, KCn, bw, nblk = sgeom(sn)
            for l in range(DEPTH):
                for blk in range(nblk):
                    P.dma(out=dap(wb[sn], (l * nblk + blk) * 128 * KCn * bw, [[KCn * bw, 128], [bw, KCn], [1, bw]]),
                          in_=dap(wsrc[src], l * K * Msrc + c0 + blk * bw, [[Msrc, 128], [128 * Msrc, KCn], [1, bw]]),
                          r=(), w=[("wb", sn)], eng="pool")

        cstt = sb("cstt", [128, 1280], F32)
        P.dma(cstt[:], cst_d.ap(), (), ["cstt"])
        P.dma(csm[:], csm_d.ap(), (), ["csm"])
        P.dma(pos[:], pos_d.ap(), (), ["pos"])
        P.dma(n1g[:], n1g_d.ap(), (), ["n1g"])
        P.dma(n2g[:], n2g_d.ap(), (), ["n2g"])
        P.dma(fng[:], fng_d.ap(), (), ["fng"])
        P.dma(gng[:], gng_d.ap(), (), ["gng"])
        P.dma(bglu[:], bglu_d.ap(), (), ["bglu"])
        P.dma(d8[:], d8_d.ap(), (), ["d8"])
        bgt = sb("bgt", [128, DEPTH * 8], F32)
        P.dma(bgt[:], bg_d.ap(), (), ["bgt"])
        P.act(bgneg[:], bgt[:], AF.Copy, ["bgt"], ["bgneg"], scale=-1.0)
        wgf = sb("wgf", [16, DEPTH * 2 * 512], F32)
        P.dma(wgf[:], w_gu.ap(), (), ["wgf"])
        P.copy("dve", wg[:], wgf[:], ["wgf"], ["wg"])
        P.copy("dve", ident_f[:], cstt[:, 0:128], ["cstt"], ["ident_f"])
        P.copy("dve", ident_b[:], cstt[:, 0:128], ["cstt"], ["ident_b"])
        P.memset("dve", ones_b[:], 1.0, ["ones_b"])
        P.copy("dve", trim[:, 0, :], cstt[:, 128:256], ["cstt"], ["trim"])
        P.copy("dve", trim[:, 1, :], cstt[:, 256:384], ["cstt"], ["trim"])
        P.copy("dve", bmask[:, 0, :], cstt[:, 384:512], ["cstt"], ["bmask"])
        P.copy("dve", bmask[:, 1, :], cstt[:, 512:640], ["cstt"], ["bmask"])
        P.copy("dve", rmask[:], cstt[:, 640:1152], ["cstt"], ["rmask"])

        condt = sb("condt", [128, 16], F32)
        sct = sb("sct", [128, 8, 2], F32)
        bmod = sb("bmod", [128, DEPTH * 48], F32)
        wm = sb("wm", [128, 2, 8 * 512], F32)
        P.dma(condt[:], condT.ap(), (), ["condt"])
        P.dma(bmod[:], b_modT.ap(), (), ["bmod"])
        P.act(sct[:], condt[:].rearrange("p (k c) -> p k c", c=2), AF.Silu, ["condt"], ["sct"])
        for l in range(DEPTH):
            pm, pmk = psf[l], ("ps", l)
            for blk in range(12):
                slot = blk % 2
                P.dma(out=sap(wm, slot * 4096, [[512, 8], [1, 512]], 8192),
                      in_=dap(w_mod, l * D * 6 * D + blk * 512, [[6 * D, 128], [128 * 6 * D, 8], [1, 512]]),
                      r=(), w=[("wm", slot)])
                for mc4 in range(4):
                    mc = blk * 4 + mc4
                    for kc in range(8):
                        P.mm(pm[:, mc * 2:mc * 2 + 2],
                             sap(wm, slot * 4096 + kc * 512 + mc4 * 128, [[1, 128]], 8192),
                             sct[:, kc, :], kc == 0, kc == 7, [("wm", slot), "sct"], [pmk])
            P.tt("dve", modv[:, l, :, :], sap(pm, 0, [[2, 48], [1, 2]], 512),
                 sap(bmod, l * 48, [[1, 48], [0, 2]], DEPTH * 48), ALU.add, [pmk, "bmod"], ["modv"])
            for (Ax, ng, ngk, off) in ((A1, n1g, "n1g", 8), (A2, n2g, "n2g", 32)):
                P.ts("dve", Ax[:, l, :, :], modv[:, l, off:off + 8, :], 1.0, None, ALU.add, None, ["modv"], ["A12"])
                P.tt("dve", Ax[:, l, :, :], Ax[:, l, :, :], sap(ng, l * 8, [[1, 8], [0, 2]], DEPTH * 8), ALU.mult,
                     ["A12", ngk], ["A12"])

        lre = sb("lre", [128, DEPTH * 2 * 32], F32)
        lim = sb("lim", [128, DEPTH * 2 * 32], F32)
        lst = sb("lst", [128, DEPTH * 2 * 32], F32)
        P.dma(lre[:], lre_d.ap(), (), ["lre"])
        P.dma(lim[:], lim_d.ap(), (), ["lim"])
        P.dma(lst[:], lst_d.ap(), (), ["lst"])
        Bre = sb("Bre", [128, 512], F32)
        Bim = sb("Bim", [128, 512], F32)
        Cre = sb("Cre", [128, 512], F32)
        Cim = sb("Cim", [128, 512], F32)
        s32 = {n: sb("s5_" + n, [128, 32], F32) for n in
               ("st", "ar", "ai", "e1", "c1", "s1", "lbr", "lbi", "nr", "den", "t1", "t2", "wr", "wi", "m8", "th8")}
        s256 = {n: sb("s5v_" + n, [128, 288], F32) for n in
                ("th", "ea", "mgn", "mgp", "tA", "tB", "sn", "X1", "X2", "Y1", "Y2")}
        ni = sb("s5_ni", [128, 288], I32)
        abr_t = sb("abr_t", [128, 1152], F32)
        bbr = sb("bbr", [128, 512], F32)
        bbi = sb("bbi", [128, 512], F32)
        bt = sb("bt", [128, 512], F32)
        BZ = sb("BZ", [128, 4096], F32)
        QC = sb("QC", [128, 4096], F32)
        T3 = sb("T3", [128, 4096], F32)
        tWB = sb("tWB", [128, 4096], BF16)
        tWT = sb("tWT", [128, 4096], BF16)
        tM = sb("tM", [128, 4096], BF16)
        tQC = sb("tQC", [128, 4096], BF16)
        Mtmp = sb("Mtmp", [128, 512], F32)

        def sinr(out, outk, th, thk, phase, n):
            tA, tB = s256["tA"][:, 0:n], s256["tB"][:, 0:n]
            P.ts("dve", tA, th, phase, None, ALU.add, None, [thk, "csm"], ["tA"])
            P.ts("dve", ni[:, 0:n], tA, 1.0 / TWO_PI, None, ALU.mult, None, ["tA"], ["ni"])
            P.copy("dve", tB, ni[:, 0:n], ["ni"], ["tB"])
            P.stt(tA, tB, -TWO_PI, tA, ALU.mult, ALU.add, ["tA", "tB"], ["tA"])
            P.ts("dve", tA, tA, 3.14159, -3.14159, ALU.min, ALU.max, ["tA"], ["tA"])
            P.act(out, tA, AF.Sin, ["tA"], [outk])

        def mul(out, a, b, r, w):
            P.tt("dve", out, a, b, ALU.mult, r, w)

        for l in range(DEPTH):
            for dr in range(2):
                ld = l * 2 + dr
                sl = slice(ld * 32, ld * 32 + 32)
                for (t_s, t_d, nm) in ((Bre, Bre_d, "Bre"), (Bim, Bim_d, "Bim"), (Cre, Cre_d, "Cre"), (Cim, Cim_d, "Cim")):
                    P.dma(t_s[:], dap(t_d, ld * 512, [[DEPTH * 2 * 512, 128], [1, 512]]), (), [nm])
                S = {k: v[:] for k, v in s32.items()}
                P.act(S["st"], lst[:, sl], AF.Exp, ["lst"], ["st"])
                mul(S["ar"], lre[:, sl], S["st"], ["lre", "st"], ["ar"])
                mul(S["ai"], lim[:, sl], S["st"], ["lim", "st"], ["ai"])
                P.act(S["e1"], S["ar"], AF.Exp, ["ar"], ["e1"])
                sinr(S["c1"], "c1", S["ai"], "ai", PHC[0], 32)
                sinr(S["s1"], "s1", S["ai"], "ai", PHC[1], 32)
                mul(S["lbr"], S["e1"], S["c1"], ["e1", "c1"], ["lbr"])
                mul(S["lbi"], S["e1"], S["s1"], ["e1", "s1"], ["lbi"])
                P.ts("dve", S["nr"], S["lbr"], -1.0, None, ALU.add, None, ["lbr"], ["nr"])
                mul(S["t1"], lre[:, sl], lre[:, sl], ["lre"], ["t1"])
                mul(S["t2"], lim[:, sl], lim[:, sl], ["lim"], ["t2"])
                P.tt("dve", S["den"], S["t1"], S["t2"], ALU.add, ["t1", "t2"], ["den"])
                P.op("dve", lambda e, o=S["den"]: e.reciprocal(out=o, in_=o), ["den"], ["den"])
                mul(S["t1"], S["nr"], lre[:, sl], ["nr", "lre"], ["t1"])
                mul(S["t2"], S["lbi"], lim[:, sl], ["lbi", "lim"], ["t2"])
                P.tt("dve", S["wr"], S["t1"], S["t2"], ALU.add, ["t1", "t2"], ["wr"])
                mul(S["wr"], S["wr"], S["den"], ["wr", "den"], ["wr"])
                mul(S["t1"], S["lbi"], lre[:, sl], ["lbi", "lre"], ["t1"])
                mul(S["t2"], S["nr"], lim[:, sl], ["nr", "lim"], ["t2"])
                P.tt("dve", S["wi"], S["t1"], S["t2"], ALU.subtract, ["t1", "t2"], ["wi"])
                mul(S["wi"], S["wi"], S["den"], ["wi", "den"], ["wi"])
                wrb = sap(s32["wr"], 0, [[1, 32], [0, 16]], 32)
                wib = sap(s32["wi"], 0, [[1, 32], [0, 16]], 32)
                v3 = lambda t: t[:].rearrange("p (g m) -> p g m", m=16)
                mul(v3(bbr), v3(Bre), wrb, ["Bre", "wr"], ["bbr"])
                mul(v3(bt), v3(Bim), wib, ["Bim", "wi"], ["bt"])
                P.tt("dve", bbr[:], bbr[:], bt[:], ALU.subtract, ["bbr", "bt"], ["bbr"])
                mul(v3(bbi), v3(Bim), wrb, ["Bim", "wr"], ["bbi"])
                mul(v3(bt), v3(Bre), wib, ["Bre", "wi"], ["bt"])
                P.tt("dve", bbi[:], bbi[:], bt[:], ALU.add, ["bbi", "bt"], ["bbi"])
                P.act(S["m8"], S["ar"], AF.Exp, ["ar"], ["m8"], scale=8.0)
                P.ts("dve", S["th8"], S["ai"], 8.0, None, ALU.mult, None, ["ai"], ["th8"])
                sinr(S["c1"], "c1", S["th8"], "th8", PHC[0], 32)
                sinr(S["s1"], "s1", S["th8"], "th8", PHC[1], 32)
                mul(AAt[:, ld, 0, :], S["m8"], S["c1"], ["m8", "c1"], ["AAt"])
                mul(AAt[:, ld, 1, :], S["m8"], S["c1"], ["m8", "c1"], ["AAt"])
                mul(BBt[:, ld, 0, :], S["m8"], S["s1"], ["m8", "s1"], ["BBt"])
                P.ts("dve", BBt[:, ld, 1, :], BBt[:, ld, 0, :], -1.0, None, ALU.mult, None, ["BBt"], ["BBt"])
                W9 = {k: v[:, 0:288] for k, v in s256.items()}
                v9 = lambda t: t[:, 0:288].rearrange("p (j g) -> p j g", g=32)
                erb = sap(csm, 23, [[1, 9], [0, 32]], 32)
                mul(v9(s256["th"]), sap(s32["ai"], 0, [[0, 9], [1, 32]], 32), erb, ["ai", "csm"], ["th"])
                mul(v9(s256["ea"]), sap(s32["ar"], 0, [[0, 9], [1, 32]], 32), erb, ["ar", "csm"], ["ea"])
                P.act(W9["mgp"], W9["ea"], AF.Exp, ["ea"], ["mgp"])
                sinr(W9["sn"], "sn", W9["th"], "th", PHC[0], 288)
                mul(W9["X1"], W9["sn"], W9["mgp"], ["sn", "mgp"], ["X1"])
                sinr(W9["sn"], "sn", W9["th"], "th", PHC[1], 288)
                mul(W9["X2"], W9["sn"], W9["mgp"], ["sn", "mgp"], ["X2"])
                for (off_, src_, sc_) in ((0, "X1", 1.0), (32, "X1", 1.0), (64, "X2", 1.0), (96, "X2", -1.0)):
                    P.ts("dve", sap(abr_t, off_, [[128, 9], [1, 32]], 1152), v9(s256[src_]), sc_, None, ALU.mult, None, [src_], ["abr_t"])
                P.dma(dap(abrd, ld * 128 * 1152, [[1152, 128], [1, 1152]]), abr_t[:], ["abr_t"], [("abrd", ld)])
                V = {k: v[:, 0:256] for k, v in s256.items()}
                v2 = lambda t: t[:, 0:256].rearrange("p (j g) -> p j g", g=32)
                ejb = sap(csm, 7 + dr * 8, [[1, 8], [0, 32]], 32)
                mul(v2(s256["th"]), sap(s32["ai"], 0, [[0, 8], [1, 32]], 32), ejb, ["ai", "csm"], ["th"])
                mul(v2(s256["ea"]), sap(s32["ar"], 0, [[0, 8], [1, 32]], 32), ejb, ["ar", "csm"], ["ea"])
                P.act(V["mgn"], V["ea"], AF.Exp, ["ea"], ["mgn"], scale=-1.0)
                P.act(V["mgp"], V["ea"], AF.Exp, ["ea"], ["mgp"])
                for (nm, ph_i, mg) in (("X1", 0, "mgn"), ("X2", 1, "mgn"), ("Y1", 2, "mgp"), ("Y2", 3, "mgp")):
                    sinr(V["sn"], "sn", V["th"], "th", PHS[ph_i], 256)
                    mul(V[nm], V["sn"], V[mg], ["sn", mg], [nm])
                v4 = lambda t: t[:].rearrange("p (g j m) -> p g j m", j=8, m=16)
                xv = lambda t: sap(t, 0, [[1, 32], [32, 8], [0, 16]], 288)
                bv = lambda t: sap(t, 0, [[16, 32], [0, 8], [1, 16]], 512)
                mul(v4(BZ), xv(s256["X1"]), bv(bbr), ["X1", "bbr"], ["BZ"])
                mul(v4(T3), xv(s256["X2"]), bv(bbi), ["X2", "bbi"], ["T3"])
                P.tt("dve", BZ[:], BZ[:], T3[:], ALU.add, ["BZ", "T3"], ["BZ"])
                mul(v4(QC), xv(s256["Y1"]), bv(Cre), ["Y1", "Cre"], ["QC"])
                mul(v4(T3), xv(s256["Y2"]), bv(Cim), ["Y2", "Cim"], ["T3"])
                P.tt("dve", QC[:], QC[:], T3[:], ALU.add, ["QC", "T3"], ["QC"])
                P.copy("act", tQC[:], QC[:], ["QC"], ["tQC"])
                for gq in range(8):
                    pa, pak = psum()
                    for gi in range(4):
                        g = gq * 4 + gi
                        P.mm(pa[:, gi * 128:(gi + 1) * 128], BZ[:, g * 128:(g + 1) * 128], ident_f[:], True, True,
                             ["BZ", "ident_f"], [pak])
                    pv = pa[:].rearrange("p (a k) -> p a k", k=128)
                    gsl = slice(gq * 512, gq * 512 + 512)
                    P.copy("act", tWB[:, gsl], pa[:], [pak], ["tWB"])
                    tw = tWT[:, gsl].rearrange("p (a k) -> p a k", k=128)
                    P.act(tw[:, :, 0:64], pv[:, :, 64:128], AF.Copy, [pak], ["tWT"], scale=-1.0)
                    P.copy("dve", tw[:, :, 64:128], pv[:, :, 0:64], [pak], ["tWT"])
                    pb_, pbk = psum()
                    for gi in range(4):
                        g = gq * 4 + gi
                        P.mm(pb_[:, gi * 128:(gi + 1) * 128], BZ[:, g * 128:(g + 1) * 128], QC[:, g * 128:(g + 1) * 128],
                             True, True, ["BZ", "QC"], [pbk])
                    P.tt("dve", Mtmp[:].rearrange("p (a k) -> p a k", k=128), pb_[:].rearrange("p (a k) -> p a k", k=128),
                         sap(bmask, dr * 128, [[0, 4], [1, 128]], 256), ALU.mult, [pbk, "bmask"], ["Mtmp"])
                    for gi in range(4):
                        g = gq * 4 + gi
                        sc_ = d8[:, l * 32 + g:l * 32 + g + 1] if dr == 0 else 0.0
                        P.stt(tM[:, g * 128:(g + 1) * 128], ident_f[:], sc_, Mtmp[:, gi * 128:(gi + 1) * 128],
                              ALU.mult, ALU.add, ["ident_f", "d8", "Mtmp"], ["tM"])
                for k, (tt_, nm) in enumerate(((tWB, "tWB"), (tWT, "tWT"), (tM, "tM"), (tQC, "tQC"))):
                    P.dma(dap(stabd, (ld * 4 + k) * 128 * 4096, [[4096, 128], [1, 4096]]), tt_[:], [nm], [("stabd", ld)])
        P.flush()

    seqs = [(i * TP, TP, 0, i) for i in range(NP)] + [(NP * TP, TS, 1, -1)]

    def linear(wt, name, l, K, M, m0, m1, rhs_fn, N, evac, rkeys):
        KCn = K // 128
        sn, blk0 = wstream(name, m0)
        if blk0 is None:
            bw = 32
            blocks = [(0, m0 - C_GLR, m1 - m0)]
        else:
            bw = sgeom(sn)[6]
            assert (m1 - m0) % bw == 0
            blocks = [(blk0 + i, 0, bw) for i in range((m1 - m0) // bw)]
        done = 0
        for (blk, cofs, cuse) in blocks:
            slot = P.w_rr
            P.w_rr = (slot + 1) % P.nws
            P.dma(out=sap(wt, slot * 4096, [[1, KCn * bw]], P.nws * 4096), in_=wblock_ap(sn, l, blk),
                  r=(), w=[("wt", slot)], eng="pool")
            for c0 in range(cofs, cofs + cuse, 128):
                cw = min(128, cofs + cuse - c0)
                for n0 in range(0, N, 512):
                    nn = min(512, N - n0)
                    ps, psk = psum()
                    for kc in range(KCn):
                        P.mm(ps[0:cw, 0:nn], sap(wt, slot * 4096 + kc * bw + c0, [[1, cw]], P.nws * 4096), rhs_fn(kc, n0, nn),
                             kc == 0, kc == KCn - 1, [("wt", slot)] + rkeys, [psk])
                    evac(done // 128, ps, psk, n0, nn)
                done += cw

    def norm_mod(xt, xrow, ht, hrow, sq, rs, tmpf, Ax, shift0, l, cnd, N):
        xk = xrow or "xt"
        hk = hrow or "ht"
        for n0 in range(0, N, 512):
            nn = min(512, N - n0)
            pn, pnk = psum()
            for fc in range(8):
                i2 = fc % 2
                P.act(sq[:, i2, 0:nn], xt[:, fc, n0:n0 + nn], AF.Square, [(xk, fc)], [("sq", i2)])
                P.mm(pn[:, 0:nn], ones_b[:], sq[:, i2, 0:nn], fc == 0, fc == 7, [("sq", i2), "ones_b"], [pnk])
            P.act(rs[:, 0:nn], pn[:, 0:nn], AF.Ln, [pnk], ["rs"], scale=1.0 / D, bias=EPS)
            P.act(rs[:, 0:nn], rs[:, 0:nn], AF.Exp, ["rs"], ["rs"], scale=-0.5)
            for fc in range(8):
                i2 = fc % 2
                P.stt(tmpf[:, i2, 0:nn], xt[:, fc, n0:n0 + nn], Ax[:, l, fc, cnd:cnd + 1], rs[:, 0:nn], ALU.mult, ALU.mult,
                      [(xk, fc), "rs", "A12"], [("tmpf", i2)])
                P.act(ht[:, fc, n0:n0 + nn], tmpf[:, i2, 0:nn], AF.Identity, [("tmpf", i2), "modv"], [(hk, fc)],
                      bias=modv[:, l, shift0 + fc, cnd:cnd + 1])

    def mixer_phase(l, dr, passF, xsrc, xdst0):
        ld = l * 2 + dr
        TM = 512
        with ExitStack() as ph:
            def sb(name, shape, dt):
                return ph.enter_context(nc.sbuf_tensor(f"{name}_p{P.phase_no}", list(shape), dt))
            xt = sb("xt", [128, 8, TM], F32)
            ht = sb("ht", [128, 8, TM], BF16)
            sq = sb("sq", [128, 2, TM], BF16)
            rs = sb("rs", [128, TM], F32)
            tmpf = sb("tmpf", [128, 2, TM], F32)
            qk = sb("qk", [128, 8, TM], BF16)
            vT = sb("vT", [128, 4, 1024], BF16)
            glr = sb("glr", [16, TM], BF16)
            e1 = sb("e1", [128, 2, TM], F32)
            cums = sb("cums", [128, 2, TM], F32)
            ee = sb("ee", [128, 8, TM], BF16)
            etot = sb("etot", [128, 4, 4], F32)
            scm = sb("scm", [128, 16, 128], BF16)
            kT = sb("kT", [128, 8, 128], BF16)
            Sf = sb("Sf", [128, 4, 256], F32)
            Sb = sb("Sb", [128, 4, 256], BF16)
            ob = sb("ob", [128, 8, TM], BF16)
            Uj = sb("Uj", [128, 4, 512], BF16)
            U8 = sb("U8", [128, 2, 32, 64], BF16)
            DD = sb("DD", [128, 2, 32, 64], F32)
            Zt = sb("Zt", [128, 2, 64], F32)
            P1 = sb("P1", [128, 64], F32)
            P2 = sb("P2", [128, 64], F32)
            Tb = sb("Tb", [128, 512], F32)
            P2b = sb("P2b", [128, 512], F32)
            Zs = sb("Zs", [128, 9, 64], F32)
            CX = sb("CX", [128, 512], F32)
            CY = sb("CY", [128, 512], F32)
            abr = sb("abr", [128, 1152], F32)
            Sg = sb("Sg", [128, 32, 64], BF16)
            Yim = sb("Yim", [128, 4, TM], BF16)
            stab = sb("stab", [128, 4, 4096], BF16)
            P.nws = 3
            P.w_rr = 0
            wt = sb("wt", [128, P.nws, 4096], BF16)
            x0s = sb("x0s", [128, 32], F32)
            Zfin = sb("Zfin", [128, 32], F32)

            sgt = sq
            titems = []
            merge = (NP % 2 == 0) and (2 * TP == TM)
            si = 0
            while si < len(seqs):
                (tok0_, T_, cnd_, pidx_) = seqs[si]
                if merge and pidx_ >= 0:
                    titems.append((tok0_, TM, cnd_, pidx_, 0, True, True, (pidx_, pidx_ + 1)))
                    si += 2
                    continue
                TT_ = min(TM, T_)
                nt_ = T_ // TT_
                order_ = list(range(nt_)) if dr == 0 else list(range(nt_ - 1, -1, -1))
                for n_, ti_ in enumerate(order_):
                    titems.append((tok0_ + ti_ * TT_, TT_, cnd_, pidx_, ti_, n_ == 0, n_ == nt_ - 1, (pidx_,)))
                si += 1
            items = [(t_[0], t_[1]) for t_ in titems]

            XKall = [("xt", fc) for fc in range(8)]

            def load_x(k):
                t0_, TT_ = items[k]
                P.dma(xt[:, :, 0:TT_], dap(xsrc, t0_, [[NTOK, 128], [128 * NTOK, 8], [1, TT_]]), (), XKall)

            Y8v = e1.bitcast(BF16)
            load_x(0)
            P.dma(abr[:], dap(abrd, ld * 128 * 1152, [[1152, 128], [1, 1152]]), (), ["abr"])
            for k in range(4):
                P.dma(stab[:, k, :], dap(stabd, (ld * 4 + k) * 128 * 4096, [[4096, 128], [1, 4096]]), (), ["stab"])


            def init_states(pidx):
                if pidx < 0:
                    P.dma(Sf[:], dap(gla0_d, ld * 4 * 128 * 256, [[256, 128], [128 * 256, 4], [1, 256]]), (),
                          [("Sf", h) for h in range(4)])
                    P.copy("dve", Sb[:], Sf[:], [("Sf", h) for h in range(4)], [("Sb", h) for h in range(4)])
                    P.dma(Zt[:, 0, 0:32], x0_d.ap()[:, ld * 32:ld * 32 + 32], (), [("Zt", 0)])
                    P.dma(x0s[:], x0s_d.ap()[:, ld * 32:ld * 32 + 32], (), ["x0s"])
                    P.ts("dve", Zt[:, 0, 32:64], x0s[:], SGN, None, ALU.mult, None, ["x0s", "csm"], [("Zt", 0)])
                else:
                    P.memset("dve", Sf[:], 0.0, [("Sf", h) for h in range(4)])
                    P.memset("dve", Sb[:], 0.0, [("Sb", h) for h in range(4)])
                    P.memset("dve", Zt[:, 0, :], 0.0, [("Zt", 0)])

            def tile_gen(k):
                (t0, TT, cnd, pidx, ti, first, lastt, pids) = titems[k]
                two = len(pids) == 2
                pfirst = (pids[0] if dr == 0 else pids[-1])
                plast = (pids[-1] if dr == 0 else pids[0])
                NC = TT // 8
                nch = TT // 128
                ub = k % 2
                XK = [("xt", fc) for fc in range(8)]
                HK = [("ht", fc) for fc in range(8)]
                if l == 0 and pidx < 0 and not passF:
                    r0 = (ti * TT) // 64
                    nr_ = TT // 64
                    P.tt("dve", sap(xt, 0, [[TM, 4], [64, nr_], [1, 64]], 8 * TM), sap(xt, 0, [[TM, 4], [64, nr_], [1, 64]], 8 * TM),
                         sap(pos, r0, [[64, 4], [1, nr_], [0, 64]], 512), ALU.add, XK[0:4] + ["pos"], XK[0:4])
                    P.tt("dve", sap(xt, 4 * TM, [[TM, 4], [64, nr_], [1, 64]], 8 * TM),
                         sap(xt, 4 * TM, [[TM, 4], [64, nr_], [1, 64]], 8 * TM),
                         sap(pos, 256, [[64, 4], [0, nr_], [1, 64]], 512), ALU.add, XK[4:8] + ["pos"], XK[4:8])
                if xdst0 is not None:
                    P.dma(dap(xdst0, t0, [[NTOK, 128], [128 * NTOK, 8], [1, TT]]), xt[:, :, 0:TT], XK, ["xp"])
                norm_mod(xt, None, ht, None, sq, rs, tmpf, A1, 0, l, cnd, TT)
                if k + 1 < len(items):
                    load_x(k + 1)
                yield
                rhs_h = lambda kc, n0, nn: ht[:, kc, n0:n0 + nn]

                slot = P.w_rr
                P.w_rr = (slot + 1) % P.nws
                P.dma(out=sap(wt, slot * 4096, [[1, 4096]], P.nws * 4096), in_=wblock_ap("w_inB", l, 0),
                      r=(), w=[("wt", slot)], eng="pool")
                for uc in range(4):
                    ps, psk = psum()
                    for kc in range(8):
                        P.mm(ps[:, 0:TT], sap(wt, slot * 4096 + kc * 512 + uc * 128, [[1, 128]], P.nws * 4096),
                             sap(ht, kc * TM, [[1, 8], [8, NC]], 8 * TM), kc == 0, kc == 7, [("wt", slot), ("ht", kc)], [psk])
                    P.copy("act", Uj[:, uc, 0:TT], ps[:, 0:TT], [psk], [("Uj", uc)])
                for uc in range(4):
                    P.dma(dap(D1, uc * 128 * 64, [[64, 128], [512 * 64, 8], [1, NC]]), sap(Uj, uc * 512, [[NC, 8], [1, NC]], 2048),
                          [("Uj", uc)], ["D1"])
                for j in range(8):
                    P.dma(sapp(U8, 16 * j, 16, ub * 2048, [[64, 32], [1, NC]], 4096), dap(D1, j * 512 * 64, [[64, 16], [16 * 64, 32], [1, NC]]),
                          ["D1"], [("U8", ub)])

                def ev_qk(mc, ps, psk, n0, nn):
                    if mc < 4:
                        P.act(qk[:, mc, 0:nn], ps[:, 0:nn], AF.Copy, [psk], [("qk", mc)], scale=DK ** -0.5)
                    else:
                        P.copy("act", qk[:, mc, 0:nn], ps[:, 0:nn], [psk], [("qk", mc)])
                linear(wt, "w_in", l, D, DIN, C_Q, C_K + 512, rhs_h, TT, ev_qk, HK)

                def ev_glr(mc, ps, psk, n0, nn):
                    P.copy("act", glr[0:16, 0:nn], ps[0:16, 0:nn], [psk], ["glr"])
                linear(wt, "w_in", l, D, DIN, C_GLR + 16 * dr, C_GLR + 16 * dr + 16, rhs_h, TT, ev_glr, HK)

                for half in range(2):
                    slot = P.w_rr
                    P.w_rr = (slot + 1) % P.nws
                    P.dma(out=sap(wt, slot * 4096, [[1, 4096]], P.nws * 4096), in_=wblock_ap("w_inA", l, 2 + half),
                          r=(), w=[("wt", slot)], eng="pool")
                    for tc in range(nch):
                        ps, psk = psum()
                        for kc in range(8):
                            P.mm(ps[:, 0:512], ht[:, kc, tc * 128:(tc + 1) * 128], sap(wt, slot * 4096 + kc * 512, [[1, 512]], P.nws * 4096),
                                 kc == 0, kc == 7, [("wt", slot), ("ht", kc)], [psk])
                        P.copy("act", vT[:, tc, half * 512:(half + 1) * 512], ps[:, 0:512],
                               [psk], [("vT", tc)])

                yield
                if first:
                    init_states(pidx)
                for hh in range(4):
                    s2 = hh % 2
                    pz, pzk = psum()
                    P.mm(pz[:, 0:TT], wg[0:16, ld * 512 + hh * 128:ld * 512 + (hh + 1) * 128], glr[0:16, 0:TT], True, True,
                         ["wg", "glr"], [pzk])
                    P.act(e1[:, s2, 0:TT], pz[:, 0:TT], AF.Exp, [pzk, "bgneg"], [("e1", s2)], scale=-1.0,
                          bias=bgneg[:, ld * 4 + hh:ld * 4 + hh + 1])
                    P.act(e1[:, s2, 0:TT], e1[:, s2, 0:TT], AF.Ln, [("e1", s2)], [("e1", s2)], bias=1.0)
                    if dr == 0:
                        d1_ap, o_ap = e1[:, s2, 0:TT], cums[:, s2, 0:TT]
                    else:
                        d1_ap = sap(e1, s2 * TM + TT - 1, [[-1, TT]], 2 * TM)
                        o_ap = sap(cums, s2 * TM + TT - 1, [[-1, TT]], 2 * TM)
                    P.op("dve", lambda e, o=o_ap, d1=d1_ap, n=TT: e.tensor_tensor_scan(
                        out=o, data0=rmask[:, 0:n], data1=d1, initial=0.0, op0=ALU.mult, op1=ALU.add),
                        [("e1", s2), "rmask"], [("cums", s2)])
                    P.act(ee[:, hh, 0:TT], cums[:, s2, 0:TT], AF.Exp, [("cums", s2)], [("ee", hh)], scale=-1.0 / 16)
                    P.act(ee[:, 4 + hh, 0:TT], cums[:, s2, 0:TT], AF.Exp, [("cums", s2)], [("ee", 4 + hh)], scale=1.0 / 16)
                    P.act(etot[:, hh, 0:nch], sap(cums, s2 * TM + (127 if dr == 0 else 0), [[128, nch]], 2 * TM), AF.Exp,
                          [("cums", s2)], ["etot"], scale=-1.0 / 16)
                    P.tt("pool", qk[:, hh, 0:TT], qk[:, hh, 0:TT], ee[:, hh, 0:TT], ALU.mult, [("qk", hh), ("ee", hh)], [("qk", hh)])
                    P.tt("pool", qk[:, 4 + hh, 0:TT], qk[:, 4 + hh, 0:TT], ee[:, 4 + hh, 0:TT], ALU.mult,
                         [("qk", 4 + hh), ("ee", 4 + hh)], [("qk", 4 + hh)])

                OK_ = [("ob", h) for h in range(4)]
                if passF:
                    P.dma(ob[:, :, 0:TT], dap(obs, t0, [[NTOK, 128], [128 * NTOK, 8], [1, TT]]), (), OK_)

                corder = list(range(nch)) if dr == 0 else list(range(nch - 1, -1, -1))

                def stage1m(c):
                    cs = slice(c * 128, (c + 1) * 128)
                    for hh in range(4):
                        ix = c * 4 + hh
                        psc, psck = psum()
                        P.mm(psc[:, 0:128], qk[:, 4 + hh, cs], qk[:, hh, cs], True, True, [("qk", 4 + hh), ("qk", hh)], [psck])
                        P.tt("dve", scm[:, ix, :], psc[:, 0:128], trim[:, dr, :], ALU.mult, [psck, "trim"], [("scm", ix)])

                def stage1(c):
                    cs = slice(c * 128, (c + 1) * 128)
                    for hh in range(4):
                        ix = (c % 2) * 4 + hh
                        i2 = ix % 2
                        P.transpose(psbs[i2][:, 0:128], qk[:, 4 + hh, cs], ident_b[:], [("qk", 4 + hh), "ident_b"],
                                    [("psb", i2)])
                        P.copy("act", kT[:, ix, :], psbs[i2][:, 0:128], [("psb", i2)], [("kT", ix)])

                for c_ in corder:
                    stage1m(c_)
                stage1(corder[0])

                for gq in range(4):
                    pa, pak = psum()
                    pb_, pbk = psum()
                    for gi in range(8):
                        g = gq * 8 + gi
                        P.mm(pa[:, gi * NC:(gi + 1) * NC], stab[:, 0, g * 128:(g + 1) * 128], U8[:, ub, g, 0:NC], True, True,
                             ["stab", ("U8", ub)], [pak])
                    for gi in range(8):
                        g = gq * 8 + gi
                        P.mm(pb_[:, gi * NC:(gi + 1) * NC], stab[:, 1, g * 128:(g + 1) * 128], U8[:, ub, g, 0:NC], True, True,
                             ["stab", ("U8", ub)], [pbk])
                    P.copy("act", DD[:, 0, gq * 8:(gq + 1) * 8, 0:NC], pa[:, 0:8 * NC].rearrange("p (a c) -> p a c", c=NC), [pak], ["DD"])
                    P.copy("act", DD[:, 1, gq * 8:(gq + 1) * 8, 0:NC], pb_[:, 0:8 * NC].rearrange("p (a c) -> p a c", c=NC), [pbk], ["DD"])

                NB = NC // 8
                cst_ = 8 if dr == 0 else -8
                rs_ = 1 if dr == 0 else -1
                off = (lambda r, b0: r + 8 * b0) if dr == 0 else (lambda r, b0: NC - 1 - r - 8 * b0)
                dd_full = lambda r, nb: sap(DD, off(r, 0), [[2048, 2], [64, 32], [cst_, nb]], 4096)
                dd_swap = lambda r, nb: sap(DD, off(r, 0) + 2048, [[-2048, 2], [64, 32], [cst_, nb]], 4096)
                AAb = lambda r, nb: sap(abr, r * 128, [[32, 2], [1, 32], [0, nb]], 1152)
                BBb = lambda r, nb: sap(abr, r * 128 + 64, [[32, 2], [1, 32], [0, nb]], 1152)
                Tv = sap(Tb, 0, [[256, 2], [8, 32], [1, NB]], 512)
                Tsw = sap(Tb, 256, [[-256, 2], [8, 32], [1, NB]], 512)
                P2v = sap(P2b, 0, [[256, 2], [8, 32], [1, NB]], 512)
                v3 = lambda t: t[:].rearrange("p (a g) -> p a g", g=32)

                def scan_gen():
                    for r in range(8):
                        if r == 0:
                            P.tt("dve", P2v, dd_swap(0, NB), BBb(1, NB), ALU.mult, ["DD", "abr"], ["P2b"])
                            P.tt("dve", Tv, dd_full(0, NB), AAb(1, NB), ALU.mult, ["DD", "abr"], ["Tb"])
                        else:
                            P.tt("dve", Tv, dd_full(r - 1, NB), dd_full(r, NB), ALU.add, ["DD"], ["Tb"])
                            P.tt("dve", P2v, Tsw, BBb(1, NB), ALU.mult, ["Tb", "abr"], ["P2b"])
                            P.tt("dve", Tv, Tv, AAb(1, NB), ALU.mult, ["Tb", "abr"], ["Tb"])
                        P.tt("dve", dd_full(r, NB), Tv, P2v, ALU.add, ["Tb", "P2b"], ["DD"])
                        yield
                    P.copy("dve", Zs[:, 0, :], Zt[:, 0, :], [("Zt", 0)], ["Zs"])
                    for b in range(NB):
                        P.tt("dve", P1[:], Zs[:, b, :], sap(abr, 8 * 128, [[1, 64]], 1152), ALU.mult, ["Zs", "abr"], ["P1"])
                        P.tt("dve", v3(P2), sap(Zs, b * 64 + 32, [[-32, 2], [1, 32]], 576), sap(abr, 8 * 128 + 64, [[32, 2], [1, 32]], 1152),
                             ALU.mult, ["Zs", "abr"], ["P2"])
                        P.tt("dve", P1[:], P1[:], P2[:], ALU.add, ["P1", "P2"], ["P1"])
                        P.tt("dve", sap(Zs, (b + 1) * 64, [[32, 2], [1, 32]], 576), v3(P1), sap(DD, off(7, b), [[2048, 2], [64, 32]], 4096),
                             ALU.add, ["P1", "DD"], ["Zs"])
                        if two and b == NB // 2 - 1:
                            so_ = (pfirst * DEPTH + l) * 2 + dr
                            P.copy("dve", Zfin[:], Zs[:, b + 1, 0:32], ["Zs"], ["Zfin"])
                            P.dma(s5_out.ap()[:, so_ * 32:so_ * 32 + 32], Zfin[:], ["Zfin"], ["s5_out"])
                            P.memset("dve", Zs[:, b + 1, :], 0.0, ["Zs"])
                        yield
                    for gqr in range(4):
                        g0 = gqr * 8
                        CXv = sap(CX, 0, [[NB * 8, 8], [8, NB], [1, 8]], 512)
                        CYv = sap(CY, 0, [[NB * 8, 8], [8, NB], [1, 8]], 512)
                        P.tt("dve", CXv, sap(Zs, g0, [[1, 8], [64, NB], [0, 8]], 576), sap(abr, g0, [[1, 8], [0, NB], [128, 8]], 1152),
                             ALU.mult, ["Zs", "abr"], ["CX"])
                        P.tt("dve", CYv, sap(Zs, 32 + g0, [[1, 8], [64, NB], [0, 8]], 576), sap(abr, 64 + g0, [[1, 8], [0, NB], [128, 8]], 1152),
                             ALU.mult, ["Zs", "abr"], ["CY"])
                        P.tt("dve", CXv, CXv, CYv, ALU.add, ["CX", "CY"], ["CX"])
                        P.copy("dve", sap(Sg, off(0, 0) + g0 * 64, [[64, 8], [cst_, NB]], 2048), sap(CX, 0, [[NB * 8, 8], [8, NB]], 512),
                               ["CX"], ["Sg"])
                        P.tt("dve", sap(Sg, off(1, 0) + g0 * 64, [[64, 8], [cst_, NB], [rs_, 7]], 2048),
                             sap(DD, off(0, 0) + g0 * 64, [[64, 8], [cst_, NB], [rs_, 7]], 4096),
                             sap(CX, 1, [[NB * 8, 8], [8, NB], [1, 7]], 512), ALU.add, ["DD", "CX"], ["Sg"])
                        yield
                    P.copy("dve", Zt[:, 0, :], Zs[:, NB, :], ["Zs"], [("Zt", 0)])

                chain = scan_gen()

                def pump(n):
                    for _ in range(n):
                        if next(chain, "done") == "done":
                            return

                def chain_finish():
                    pump(64)

                yield
                if passF:
                    def ev_ga(mc, ps, psk, n0, nn):
                        P.act(ee[:, mc, 0:nn], ps[:, 0:nn], AF.Sigmoid, [psk], [("ee", mc)])
                    linear(wt, "w_in", l, D, DIN, C_GA, C_GA + 1024, rhs_h, TT, ev_ga, HK)
                for ci, c in enumerate(corder):
                    cs = slice(c * 128, (c + 1) * 128)
                    if two and ci == nch // 2:
                        so_ = (pfirst * DEPTH + l) * 2 + dr
                        P.dma(dap(gla_out, so_ * 4 * 128 * 256, [[256, 128], [128 * 256, 4], [1, 256]]), Sf[:],
                              [("Sf", h) for h in range(4)], ["gla_out"])
                        P.memset("pool", Sf[:], 0.0, [("Sf", h) for h in range(4)])
                        P.memset("pool", Sb[:], 0.0, [("Sb", h) for h in range(4)])
                    if ci + 1 < nch:
                        stage1(corder[ci + 1])
                    for hh in range(4):
                        ix = (c % 2) * 4 + hh
                        sx = c * 4 + hh
                        po, pok = psum()
                        for v2 in range(2):
                            P.mm(po[:, v2 * 128:(v2 + 1) * 128], vT[:, c, hh * 256 + v2 * 128:hh * 256 + (v2 + 1) * 128],
                                 scm[:, sx, :], True, False, [("vT", c), ("scm", sx)], [pok])
                            P.mm(po[:, v2 * 128:(v2 + 1) * 128], Sb[:, hh, v2 * 128:(v2 + 1) * 128], qk[:, hh, cs], False, not passF,
                                 [("Sb", hh), ("qk", hh)], [pok])
                            if passF:
                                P.mm(po[:, v2 * 128:(v2 + 1) * 128], ident_b[:], ob[:, hh * 2 + v2, cs], False, True,
                                     ["ident_b", ("ob", hh)], [pok])
                        o_out = ob[:, hh * 2:hh * 2 + 2, cs]
                        o_in = po[:, 0:256].rearrange("p (a k) -> p a k", k=128)
                        P.copy("act", o_out, o_in, [pok], [("ob", hh)])
                        pd, pdk = psum()
                        P.mm(pd[:, 0:256], kT[:, ix, :], vT[:, c, hh * 256:(hh + 1) * 256], True, False, [("kT", ix), ("vT", c)], [pdk])
                        P.mm(pd[:, 0:256], ident_f[:], Sf[:, hh, :], False, True, ["ident_f", ("Sf", hh)], [pdk])
                        P.act(Sf[:, hh, :], pd[:, 0:256], AF.Identity, [pdk, "etot"], [("Sf", hh)], scale=etot[:, hh, c:c + 1])
                        P.act(Sb[:, hh, :], pd[:, 0:256], AF.Identity, [pdk, "etot"], [("Sb", hh)], scale=etot[:, hh, c:c + 1])
                        pump(2)

                def s5_out_block():
                    chain_finish()
                    for gq in range(4):
                        py, pyk = psum()
                        for gi in range(8):
                            g = gq * 8 + gi
                            P.mm(py[:, gi * NC:(gi + 1) * NC], stab[:, 2, g * 128:(g + 1) * 128], U8[:, ub, g, 0:NC], True, False,
                                 ["stab", ("U8", ub)], [pyk])
                            P.mm(py[:, gi * NC:(gi + 1) * NC], stab[:, 3, g * 128:(g + 1) * 128], Sg[:, g, 0:NC], False, True,
                                 ["stab", "Sg"], [pyk])
                        P.copy("act", sap(Y8v, gq * 512, [[64, 8], [1, NC]], 2048),
                               py[:, 0:8 * NC].rearrange("p (a c) -> p a c", c=NC), [pyk], [("e1", 0), ("e1", 1)])
                    for i in range(8):
                        P.dma(dap(D2, i * 512 * 64, [[64, 16], [16 * 64, 32], [1, NC]]), sapp(Y8v, 16 * i, 16, 0, [[64, 32], [1, NC]], 2048),
                              [("e1", 0), ("e1", 1)], ["D2"])
                    for uc in range(4):
                        P.dma(sap(Yim, uc * TM, [[NC, 8], [1, NC]], 4 * TM), dap(D2, uc * 128 * 64, [[64, 128], [512 * 64, 8], [1, NC]]),
                              ["D2"], ["Yim"])


                yield
                if not passF:
                    s5_out_block()
                    P.dma(dap(ybs, t0, [[NTOK, 128], [128 * NTOK, 4], [1, TT]]), Yim[:, :, 0:TT], ["Yim"], ["ybs"])
                    P.dma(dap(obs, t0, [[NTOK, 128], [128 * NTOK, 8], [1, TT]]), ob[:, :, 0:TT], OK_, ["obs"])
                else:

                    VK = [("vT", i) for i in range(4)]
                    rs2 = [e1[:, 0, 0:TT], e1[:, 1, 0:TT], cums[:, 0, 0:TT], cums[:, 1, 0:TT]]
                    rs2k = [("e1", 0), ("e1", 1), ("cums", 0), ("cums", 1)]
                    for hh in range(4):
                        pn, pnk = psum()
                        for v2 in range(2):
                            P.act(sq[:, v2, 0:TT], ob[:, hh * 2 + v2, 0:TT], AF.Square, [("ob", hh)], [("sq", v2)])
                            P.mm(pn[:, 0:TT], ones_b[:], sq[:, v2, 0:TT], v2 == 0, v2 == 1, [("sq", v2), "ones_b"], [pnk])
                        P.act(rs2[hh], pn[:, 0:TT], AF.Ln, [pnk], [rs2k[hh]], scale=1.0 / DV, bias=EPS)
                        P.act(rs2[hh], rs2[hh], AF.Exp, [rs2k[hh]], [rs2k[hh]], scale=-0.5)

                    def ev_r(vc, ps, psk, n0, nn):
                        i2 = vc % 2
                        hh = vc // 2
                        P.act(sgt[:, i2, 0:nn], ps[:, 0:nn], AF.Silu, [psk], [("sq", i2)])
                        P.stt(tmpf[:, i2, 0:nn], ob[:, vc, 0:nn], gng[:, l * 2 + i2:l * 2 + i2 + 1], rs2[hh], ALU.mult, ALU.mult,
                              [("ob", hh), "gng", rs2k[hh]], [("tmpf", i2)])
                        P.tt("dve", ob[:, vc, 0:nn], tmpf[:, i2, 0:nn], sgt[:, i2, 0:nn], ALU.mult, [("tmpf", i2), ("sq", i2)], [("ob", hh)])
                    linear(wt, "w_in", l, D, DIN, C_R, C_R + 1024, rhs_h, TT, ev_r, HK)


                    s5_out_block()

                    def ev_pg(mc, ps, psk, n0, nn):
                        P.tt("dve", qk[:, mc, 0:nn], ps[:, 0:nn], ee[:, mc, 0:nn], ALU.mult, [psk, ("ee", mc)], [("qk", mc)])
                    linear(wt, "w_pg", l, D, D, 0, D, lambda kc, n0, nn: ob[:, kc, n0:n0 + nn], TT, ev_pg, OK_)

                    def ev_gb(mc, ps, psk, n0, nn):
                        P.act(ee[:, mc, 0:nn], ps[:, 0:nn], AF.Sigmoid, [psk], [("ee", mc)])
                    linear(wt, "w_in", l, D, DIN, C_GB, C_GB + 1024, rhs_h, TT, ev_gb, HK)
                    yield

                    ybv = sap(vT, 2048, [[TM, 4], [1, TT]], 4096)
                    P.dma(ybv, dap(ybs, t0, [[NTOK, 128], [128 * NTOK, 4], [1, TT]]), (), VK[2:4])
                    P.tt("dve", ybv, Yim[:, :, 0:TT], ybv, ALU.add, ["Yim"] + VK[2:4], VK[2:4])
                    P.act(Yim[:, :, 0:TT], ybv, AF.Gelu_apprx_tanh, VK[2:4], ["Yim"])

                    def ev_glu(mc, ps, psk, n0, nn):
                        i2 = mc % 2
                        P.act(sgt[:, i2, 0:nn], ps[:, 0:nn], AF.Sigmoid, [psk, "bglu"], [("sq", i2)], bias=bglu[:, l * 4 + mc:l * 4 + mc + 1])
                        P.tt("dve", sap(vT, mc * TM, [[1, nn]], 4096), Yim[:, mc, 0:nn], sgt[:, i2, 0:nn], ALU.mult,
                             ["Yim", ("sq", i2)], [("vT", mc // 2)])
                    linear(wt, "w_glu", l, 512, 512, 0, 512, lambda kc, n0, nn: Yim[:, kc, n0:n0 + nn], TT, ev_glu, ["Yim"])

                    def ev_ps(mc, ps, psk, n0, nn):
                        i2 = mc % 2
                        P.tt("dve", tmpf[:, i2, 0:nn].rearrange("p (c i) -> p c i", i=8), sap(ps, 0, [[1, NC], [NC, 8]], 512),
                             ee[:, mc, 0:nn].rearrange("p (c i) -> p c i", i=8), ALU.mult, [psk, ("ee", mc)], [("tmpf", i2)])
                        P.tt("dve", qk[:, mc, 0:nn], tmpf[:, i2, 0:nn], qk[:, mc, 0:nn], ALU.add, [("tmpf", i2), ("qk", mc)], [("qk", mc)])
                    linear(wt, "w_ps", l, 512, D, 0, D, lambda kc, n0, nn: sap(vT, kc * TM + n0, [[1, nn]], 4096), TT, ev_ps, VK[0:2])

                    def ev_out(mc, ps, psk, n0, nn):
                        i2 = mc % 2
                        P.dma(tmpf[:, i2, 0:nn], dap(xsrc, mc * 128 * NTOK + t0, [[NTOK, 128], [1, nn]]), (), [("tmpf", i2)])
                        P.stt(tmpf[:, i2, 0:nn], ps[:, 0:nn], modv[:, l, 16 + mc, cnd:cnd + 1], tmpf[:, i2, 0:nn], ALU.mult, ALU.add,
                              [psk, "modv", ("tmpf", i2)], [("tmpf", i2)])
                        P.dma(dap(xm, mc * 128 * NTOK + t0, [[NTOK, 128], [1, nn]]), tmpf[:, i2, 0:nn], [("tmpf", i2)], ["xm"])
                    linear(wt, "w_out", l, D, D, 0, D, lambda kc, n0, nn: qk[:, kc, n0:n0 + nn], TT, ev_out, [("qk", i) for i in range(8)])

                if lastt and pidx >= 0:
                    so = (plast * DEPTH + l) * 2 + dr
                    P.dma(dap(gla_out, so * 4 * 128 * 256, [[256, 128], [128 * 256, 4], [1, 256]]), Sf[:],
                          [("Sf", h) for h in range(4)], ["gla_out"])
                    P.dma(s5_out.ap()[:, so * 32:so * 32 + 32], Zt[:, 0, 0:32], [("Zt", 0)], ["s5_out"])

            ntl = len(titems)
            gens = {0: tile_gen(0)}
            next(gens[0])
            next(gens[0])
            for k in range(ntl):
                g = gens[k]
                nxt = None
                if k + 1 < ntl:
                    nxt = gens[k + 1] = tile_gen(k + 1)
                next(g)
                if not passF and nxt is not None:
                    next(nxt)
                next(g)
                if not passF and nxt is not None:
                    next(nxt)
                if passF:
                    next(g)
                    if nxt is not None:
                        next(nxt)
                    next(g, None)
                    if nxt is not None:
                        next(nxt)
                else:
                    next(g, None)
            P.flush()

    def mlp_phase(l, last):
        TM = 1024
        segs = []
        if NP > 0:
            segs.append((0, NP * TP, 0))
        segs.append((NP * TP, TS, 1))
        with ExitStack() as ph:
            def sb(name, shape, dt):
                return ph.enter_context(nc.sbuf_tensor(f"{name}_p{P.phase_no}", list(shape), dt))
            xts = [sb("xt2a", [128, 8, TM], F32), sb("xt2b", [128, 8, TM], F32)]
            hts = [sb("ht2a", [128, 8, TM], BF16), sb("ht2b", [128, 8, TM], BF16)]
            hid = sb("hid", [128, 32, TM], BF16)
            sq = sb("sq2", [128, 2, 512], BF16)
            rs = sb("rs_m", [128, 512], F32)
            tmpf = sb("tmpf2", [128, 2, 512], F32)
            rl = sb("rl", [128, 2, 512], BF16)
            P.nws = 2
            P.w_rr = 0
            wt = sb("wt2", [128, P.nws, 4096], BF16)
            dst = yT if last else xn
            tiles = []
            for (s0, slen, cnd) in segs:
                for t0 in range(s0, s0 + slen, TM):
                    tiles.append((t0, min(TM, s0 + slen - t0), cnd))

            def front(k):
                t0, TT, cnd = tiles[k]
                sl2 = k % 2
                xt, ht = xts[sl2], hts[sl2]
                xkn, hkn = f"xt{sl2}", f"ht{sl2}"
                P.dma(xt[:, :, 0:TT], dap(xm, t0, [[NTOK, 128], [128 * NTOK, 8], [1, TT]]), (), [(xkn, fc) for fc in range(8)])
                norm_mod(xt, xkn, ht, hkn, sq, rs, tmpf, A2, 24, l, cnd, TT)

            front(0)
            for k, (t0, TT, cnd) in enumerate(tiles):
                sl2 = k % 2
                xt, ht = xts[sl2], hts[sl2]
                xkn, hkn = f"xt{sl2}", f"ht{sl2}"
                XK = [(xkn, fc) for fc in range(8)]
                HK = [(hkn, fc) for fc in range(8)]

                def ev_ff1(mc, ps, psk, n0, nn):
                    i2 = (mc + n0 // 512) % 2
                    P.act(rl[:, i2, 0:nn], ps[:, 0:nn], AF.Relu, [psk], [("rl", i2)])
                    P.tt("dve", hid[:, mc, n0:n0 + nn], rl[:, i2, 0:nn], rl[:, i2, 0:nn], ALU.mult, [("rl", i2)], [("hid", mc)])
                linear(wt, "w_ff1", l, D, DFF, 0, DFF, lambda kc, n0, nn: ht[:, kc, n0:n0 + nn], TT, ev_ff1, HK)

                if k + 1 < len(tiles):
                    front(k + 1)

                def ev_ff2(mc, ps, psk, n0, nn):
                    P.stt(xt[:, mc, n0:n0 + nn], ps[:, 0:nn], modv[:, l, 40 + mc, cnd:cnd + 1], xt[:, mc, n0:n0 + nn], ALU.mult, ALU.add,
                          [psk, "modv", (xkn, mc)], [(xkn, mc)])
                linear(wt, "w_ff2", l, DFF, D, 0, D, lambda kc, n0, nn: hid[:, kc, n0:n0 + nn], TT, ev_ff2,
                       [("hid", i) for i in range(32)])
                if last:
                    for n0 in range(0, TT, 512):
                        nn = min(512, TT - n0)
                        pn, pnk = psum()
                        for fc in range(8):
                            i2 = fc % 2
                            P.act(sq[:, i2, 0:nn], xt[:, fc, n0:n0 + nn], AF.Square, [(xkn, fc)], [("sq", i2)])
                            P.mm(pn[:, 0:nn], ones_b[:], sq[:, i2, 0:nn], fc == 0, fc == 7, [("sq", i2), "ones_b"], [pnk])
                        P.act(rs[:, 0:nn], pn[:, 0:nn], AF.Ln, [pnk], ["rs"], scale=1.0 / D, bias=EPS)
                        P.act(rs[:, 0:nn], rs[:, 0:nn], AF.Exp, ["rs"], ["rs"], scale=-0.5)
                        for fc in range(8):
                            P.stt(xt[:, fc, n0:n0 + nn], xt[:, fc, n0:n0 + nn], fng[:, fc:fc + 1], rs[:, 0:nn], ALU.mult, ALU.mult,
                                  [(xkn, fc), "fng", "rs"], [(xkn, fc)])
                P.dma(dap(dst, t0, [[NTOK, 128], [128 * NTOK, 8], [1, TT]]), xt[:, :, 0:TT], XK, ["dst"])
            P.flush()

    for l in range(DEPTH):
        if l == 0:
            mixer_phase(l, 1, False, xT, xp)
            mixer_phase(l, 0, True, xp, None)
        else:
            mixer_phase(l, 1, False, xn, None)
            mixer_phase(l, 0, True, xn, None)
        mlp_phase(l, l == DEPTH - 1)
    P.final_wait()
    build.n_inst = P.n_inst
    return nc


def _fm_vec(v):
    v = np.asarray(v, np.float32)
    lead = v.shape[:-1]
    n = v.shape[-1] // 128
    v = v.reshape(lead + (n, 128))
    v = np.moveaxis(v, -1, 0)
    return np.ascontiguousarray(v).reshape(128, -1)


def _constants():
    j = np.arange(128)
    ident = np.eye(128, dtype=np.float32)
    trif = (j[:, None] <= j[None, :]).astype(np.float32)
    trib = (j[:, None] >= j[None, :]).astype(np.float32)
    blk = j // 16
    bmf = (blk[None, :] >= blk[:, None]).astype(np.float32)
    bmb = (blk[None, :] <= blk[:, None]).astype(np.float32)
    rmask = np.ones((128, 512), np.float32)
    rmask[:, ::128] = 0.0
    cst = np.concatenate([ident, trif, trib, bmf, bmb, rmask, np.zeros((128, 128), np.float32)], axis=1)
    csm = np.zeros((128, 32), np.float32)
    top = (j < 64)
    csm[:, 0] = np.where(top, -1.0, 1.0)
    hp = math.pi / 2
    csm[:, 1] = np.where(top, hp, math.pi)
    csm[:, 2] = np.where(top, 0.0, hp)
    csm[:, 3] = np.where(top, hp, math.pi)
    csm[:, 4] = np.where(top, math.pi, 3 * hp)
    csm[:, 5] = hp
    csm[:, 6] = 0.0
    csm[:, 7:15] = np.arange(1, 9, dtype=np.float32)[None, :]
    csm[:, 15:23] = (8 - np.arange(8, dtype=np.float32))[None, :]
    csm[:, 23:32] = (8.0 * np.arange(9, dtype=np.float32))[None, :]
    quarter = D // 4
    omega = (1.0 / (10000.0 ** (np.arange(quarter, dtype=np.float32) / quarter))).astype(np.float32)
    rr = np.arange(64, dtype=np.float32)
    ang = (rr[:, None] * omega[None, :]).astype(np.float32)
    tab = np.concatenate([np.sin(ang), np.cos(ang)], axis=1).astype(np.float32)
    fm = np.ascontiguousarray(tab.T.reshape(4, 128, 64).transpose(1, 0, 2)).reshape(128, 256)
    pos = np.concatenate([fm, fm], axis=1).astype(np.float32)
    return cst, csm, pos


def _prep_shared(inp):
    f = lambda a: np.ascontiguousarray(np.asarray(a, np.float32))
    sh = {}
    sh["w_mod"] = f(inp["w_mod"])
    sh["b_modT"] = _fm_vec(inp["b_mod"])
    sh["n1g"] = _fm_vec(inp["norm1_g"])
    sh["n2g"] = _fm_vec(inp["norm2_g"])
    sh["fng"] = _fm_vec(inp["final_g"])
    sh["w_in"] = f(inp["w_in"])
    sh["w_pg"] = f(inp["w_proj_gla"])
    sh["w_glu"] = f(inp["w_glu"])
    sh["w_ps"] = f(inp["w_proj_s5"])
    sh["w_out"] = f(inp["w_out"])
    sh["w_ff1"] = f(inp["w_ff1"])
    sh["w_ff2"] = f(inp["w_ff2"])
    sh["w_gu"] = np.ascontiguousarray(f(inp["w_gate_up"]).transpose(2, 0, 1, 3)).reshape(16, -1)
    sh["bgT"] = _fm_vec(inp["b_gate"])
    sh["gngT"] = _fm_vec(inp["gla_norm_g"])
    sh["bgluT"] = _fm_vec(inp["b_glu"])
    d = f(inp["s5_d"]).reshape(DEPTH, 32, 16)
    d8 = np.broadcast_to(d.transpose(2, 0, 1)[None], (8, 16, DEPTH, 32))
    sh["d8"] = np.ascontiguousarray(d8).reshape(128, -1)

    def klay(a):
        a = f(a).transpose(3, 0, 1, 2)
        return np.ascontiguousarray(np.concatenate([a, a], axis=0)).reshape(128, -1)
    sh["lre2"] = klay(inp["s5_lam_re"])
    sh["lim2"] = klay(inp["s5_lam_im"])
    ls = np.broadcast_to(f(inp["s5_log_step"])[None], (128, DEPTH, 2, 32))
    sh["lst2"] = np.ascontiguousarray(ls).reshape(128, -1)

    def klayB(a):
        a = f(a).transpose(3, 0, 1, 2, 4)
        return np.ascontiguousarray(np.concatenate([a, a], axis=0)).reshape(128, -1)

    def klayC(a):
        a = f(a).transpose(4, 0, 1, 2, 3)
        return np.ascontiguousarray(np.concatenate([a, a], axis=0)).reshape(128, -1)
    sh["Bre2"] = klayB(inp["s5_b_re"])
    sh["Bim2"] = klayB(inp["s5_b_im"])
    sh["Cre2"] = klayC(inp["s5_c_re"])
    sh["Cim2"] = klayC(inp["s5_c_im"])
    cst, csm, pos = _constants()
    sh["cst"], sh["csm"], sh["pos"] = cst, csm, pos
    return sh


def _prep_core(inp, core, NP, TP, TS):
    f = lambda a: np.asarray(a, np.float32)
    xp = f(inp["x_prompt"])[core * NP:(core + 1) * NP].reshape(NP * TP, D)
    xs = f(inp["x_sample"])[core]
    m = {}
    m["xT"] = np.ascontiguousarray(np.concatenate([xp, xs], axis=0).T)
    cond = np.stack([f(inp["c_ctx"]), f(inp["c"])[core]], axis=-1)
    m["condT"] = np.ascontiguousarray(cond.reshape(8, 128, 2).transpose(1, 0, 2)).reshape(128, 16)
    m["gla0"] = np.ascontiguousarray(f(inp["cache_gla_state"])[core]).reshape(-1, 256)
    re = f(inp["state_s5_re"])[core].transpose(3, 0, 1, 2)
    im = f(inp["state_s5_im"])[core].transpose(3, 0, 1, 2)
    m["s5x0"] = np.ascontiguousarray(np.concatenate([re, im], axis=0)).reshape(128, -1)
    m["s5x0s"] = np.ascontiguousarray(np.concatenate([im, re], axis=0)).reshape(128, -1)
    return m


_NC_CACHE = {}


def run_cfg(inp, NP, TP, TS, ncores):
    key = (NP, TP, TS)
    if key not in _NC_CACHE:
        _NC_CACHE[key] = build(key)
    nc = _NC_CACHE[key]
    sh = _prep_shared(inp)
    in_maps = []
    for c in range(ncores):
        m = dict(sh)
        m.update(_prep_core(inp, c, NP, TP, TS))
        in_maps.append(m)
    res = run_bass_kernel_spmd(nc, in_maps, core_ids=list(range(ncores)))
    B = NP * ncores
    y_prompt = np.zeros((B, TP, D), np.float32)
    y_sample = np.zeros((ncores, TS, D), np.float32)
    gla = np.zeros((B, DEPTH, 2, H, DK, DV), np.float32)
    s5re = np.zeros((B, DEPTH, 2, G, 64), np.float32)
    s5im = np.zeros((B, DEPTH, 2, G, 64), np.float32)
    for c in range(ncores):
        r = res.results[c]
        y = np.asarray(r["yT"], np.float32).T
        y_prompt[c * NP:(c + 1) * NP] = y[:NP * TP].reshape(NP, TP, D)
        y_sample[c] = y[NP * TP:]
        gla[c * NP:(c + 1) * NP] = np.asarray(r["gla_out"], np.float32).reshape(NP, DEPTH, 2, H, DK, DV)
        s = np.asarray(r["s5_out"], np.float32).reshape(2, 64, NP, DEPTH, 2, G)
        s5re[c * NP:(c + 1) * NP] = s[0].transpose(1, 2, 3, 4, 0)
        s5im[c * NP:(c + 1) * NP] = s[1].transpose(1, 2, 3, 4, 0)
    return (y_prompt, y_sample, gla, s5re, s5im)


def kernel(**inputs):
    return run_cfg(inputs, 4, 256, 4096, 8)
```

```python
import math
from contextlib import ExitStack

import numpy as np
import concourse.bass as bass
import concourse.mybir as mybir
from concourse.bass_utils import run_bass_kernel_spmd

F32 = mybir.dt.float32
BF16 = mybir.dt.bfloat16
I32 = mybir.dt.int32
AF = mybir.ActivationFunctionType
ALU = mybir.AluOpType

D = 1024
KC = 8
H = 4
DK = 128
DV = 256
G = 32
DFF = 4096
DIN = 5664
DEPTH = 2
EPS = 1e-6
C_Q, C_K, C_V, C_R, C_GLR, C_U, C_GA, C_GB = 0, 512, 1024, 2048, 3072, 3104, 3616, 4640
TWO_PI = 2.0 * math.pi
NLANES = 24
NHW = 16
NWS = 3


class Op:
    __slots__ = ("eng", "fn", "deps", "needed", "cnt", "dma", "lane", "lane_total", "lane_prev")

    def __init__(self, eng, fn, dma):
        self.eng = eng
        self.fn = fn
        self.dma = dma
        self.deps = ()
        self.needed = False
        self.cnt = 0
        self.lane = -1
        self.lane_total = 0
        self.lane_prev = 0


class KeyState:
    __slots__ = ("w", "r")

    def __init__(self):
        self.w = None
        self.r = []


class Prog:
    ENGS = ("pe", "act", "dve", "pool", "sp")

    def __init__(self, nc, gstack):
        self.nc = nc
        self.gstack = gstack
        self.ops = []
        self.keys = {}
        self.lane_sems = [gstack.enter_context(nc.semaphore(f"lane{i}")) for i in range(NLANES)]
        self.lane_tot = [0] * NLANES
        self.next_lane = 0
        self.next_sw_lane = NHW
        self.prev_final = []
        self.phase_no = 0
        self.ps_rr = 0
        self.w_rr = 0
        self.n_inst = 0

    def op(self, eng, fn, reads=(), writes=(), dma=False):
        o = Op(eng, fn, dma)
        deps = []
        for k in reads:
            st = self.keys.get(k)
            if st is None:
                st = self.keys[k] = KeyState()
            if st.w is not None:
                deps.append(st.w)
        for k in writes:
            st = self.keys.get(k)
            if st is None:
                st = self.keys[k] = KeyState()
            if st.w is not None:
                deps.append(st.w)
            deps.extend(st.r)
        for k in reads:
            self.keys[k].r.append(o)
        for k in writes:
            st = self.keys[k]
            st.w = o
            st.r = []
        if dma:
            if eng == "pool":
                o.lane = self.next_sw_lane
                self.next_sw_lane = NHW + (self.next_sw_lane + 1 - NHW) % (NLANES - NHW)
            else:
                o.lane = self.next_lane
                self.next_lane = (self.next_lane + 1) % NHW
            o.lane_prev = self.lane_tot[o.lane]
            self.lane_tot[o.lane] += 16
            o.lane_total = self.lane_tot[o.lane]
            o.needed = True
        o.deps = [d for d in deps if d is not o and not (d.eng == "pe" and eng == "pe" and not d.dma)]
        self.ops.append(o)
        return o

    def flush(self):
        nc = self.nc
        ops = self.ops
        self.ops = []
        self.keys = {}
        self.phase_no += 1
        sems = {e: self.gstack.enter_context(nc.semaphore(f"ph{self.phase_no}_{e}")) for e in self.ENGS}
        for o in ops:
            for d in o.deps:
                d.needed = True
        per = {e: [] for e in self.ENGS}
        for o in ops:
            per[o.eng].append(o)
        for e in self.ENGS:
            if per[e]:
                last = per[e][-1]
                last.needed = True
        cnt = {e: 0 for e in self.ENGS}
        for o in ops:
            if o.needed and not o.dma:
                cnt[o.eng] += 1
                o.cnt = cnt[o.eng]
        prev_final = self.prev_final
        lane_sems = self.lane_sems
        self.n_inst += len(ops)

        def emit(engname, eng):
            seen = {}

            def wait(sem, val):
                if val <= 0:
                    return
                k = id(sem)
                if seen.get(k, 0) >= val:
                    return
                seen[k] = val
                eng.wait_ge(sem, val)

            for (s, v) in prev_final:
                wait(s, v)
            for o in per[engname]:
                for d in o.deps:
                    if d.dma:
                        wait(lane_sems[d.lane], d.lane_total)
                    else:
                        wait(sems[d.eng], d.cnt)
                if o.dma:
                    wait(lane_sems[o.lane], o.lane_prev)
                ins = o.fn(eng)
                if o.dma:
                    ins.then_inc(lane_sems[o.lane], 16)
                elif o.needed:
                    ins.then_inc(sems[o.eng], 1)

        with nc.Block() as block:
            if per["pe"]:
                @block.tensor
                def _(eng):
                    emit("pe", eng)
            if per["act"]:
                @block.scalar
                def _(eng):
                    emit("act", eng)
            if per["dve"]:
                @block.vector
                def _(eng):
                    emit("dve", eng)
            if per["pool"]:
                @block.gpsimd
                def _(eng):
                    emit("pool", eng)
            if per["sp"]:
                @block.sync
                def _(eng):
                    emit("sp", eng)
        pf = []
        for e in self.ENGS:
            if cnt[e] > 0:
                pf.append((sems[e], cnt[e]))
        for i in range(NLANES):
            if self.lane_tot[i] > 0:
                pf.append((lane_sems[i], self.lane_tot[i]))
        self.prev_final = pf

    def final_wait(self):
        nc = self.nc
        pf = self.prev_final
        with nc.Block() as block:
            @block.sync
            def _(eng):
                for (s, v) in pf:
                    eng.wait_ge(s, v)

    def mm(self, out, lhsT, rhs, start, stop, r, w):
        return self.op("pe", lambda e: e.matmul(out, lhsT=lhsT, rhs=rhs, start=start, stop=stop), r, w)

    def transpose(self, out, in_, ident, r, w):
        return self.op("pe", lambda e: e.transpose(out=out, in_=in_, identity=ident), r, w)

    def act(self, out, in_, func, r, w, scale=1.0, bias=None):
        if bias is None:
            return self.op("act", lambda e: e.activation(out=out, in_=in_, func=func, scale=scale), r, w)
        return self.op("act", lambda e: e.activation(out=out, in_=in_, func=func, scale=scale, bias=bias), r, w)

    def tt(self, eng, out, in0, in1, op, r, w):
        return self.op(eng, lambda e: e.tensor_tensor(out=out, in0=in0, in1=in1, op=op), r, w)

    def ts(self, eng, out, in0, s1, s2, op0, op1, r, w):
        if s2 is None:
            return self.op(eng, lambda e: e.tensor_scalar(out=out, in0=in0, scalar1=s1, scalar2=None, op0=op0), r, w)
        return self.op(eng, lambda e: e.tensor_scalar(out=out, in0=in0, scalar1=s1, scalar2=s2, op0=op0, op1=op1), r, w)

    def stt(self, out, in0, scalar, in1, op0, op1, r, w):
        return self.op("dve", lambda e: e.scalar_tensor_tensor(out=out, in0=in0, scalar=scalar, in1=in1, op0=op0, op1=op1), r, w)

    def copy(self, eng, out, in_, r, w):
        if eng == "act":
            return self.act(out, in_, AF.Copy, r, w)
        return self.op(eng, lambda e: e.tensor_copy(out=out, in_=in_), r, w)

    def memset(self, eng, ap, val, w):
        return self.op(eng, lambda e: e.memset(ap, val), (), w)

    def dma(self, out, in_, r, w, eng="sp", **kw):
        return self.op(eng, lambda e: e.dma_start(out=out, in_=in_, **kw), r, w, dma=True)


def sap(t, off, dims, rowsize):
    return bass.AP(t, off, [[rowsize, 128]] + [list(d) for d in dims])


def sapp(t, p0, npart, off, dims, rowsize):
    return bass.AP(t, p0 * rowsize + off, [[rowsize, npart]] + [list(d) for d in dims])


def dap(t, off, dims):
    return bass.AP(t, off, [list(d) for d in dims])


BIGW = (("w_in", D, DIN), ("w_pg", D, D), ("w_glu", 512, 512), ("w_ps", 512, D), ("w_out", D, D),
        ("w_ff1", D, DFF), ("w_ff2", DFF, D))


def build(cfg, debug=None):
    NP, TP, TS = cfg
    NTOK = NP * TP + TS
    nc = bass.Bass("TRN2", target_bir_lowering=False)
    gs = ExitStack()

    def din(name, shape, dt=F32):
        return nc.dram_tensor(name, list(shape), dt, kind="ExternalInput")

    def dout(name, shape, dt=F32):
        return nc.dram_tensor(name, list(shape), dt, kind="ExternalOutput")

    def dscr(name, shape, dt):
        return nc.dram_tensor(name, list(shape), dt, kind="Internal")

    def gsb(name, shape, dt):
        return gs.enter_context(nc.sbuf_tensor(name, list(shape), dt))

    xT = din("xT", [D, NTOK])
    condT = din("condT", [128, 16])
    w_mod = din("w_mod", [DEPTH, D, 6 * D])
    b_modT = din("b_modT", [128, DEPTH * 48])
    n1g_d = din("n1g", [128, DEPTH * 8])
    n2g_d = din("n2g", [128, DEPTH * 8])
    fng_d = din("fng", [128, 8])
    wsrc = {n: din(n, [DEPTH, k, m]) for (n, k, m) in BIGW}
    w_gu = din("w_gu", [16, DEPTH * 2 * 512])
    bg_d = din("bgT", [128, DEPTH * 2 * 4])
    gng_d = din("gngT", [128, DEPTH * 2])
    bglu_d = din("bgluT", [128, DEPTH * 4])
    d8_d = din("d8", [128, DEPTH * 32])
    lre_d = din("lre2", [128, DEPTH * 2 * 32])
    lim_d = din("lim2", [128, DEPTH * 2 * 32])
    lst_d = din("lst2", [128, DEPTH * 2 * 32])
    Bre_d = din("Bre2", [128, DEPTH * 2 * 32 * 16])
    Bim_d = din("Bim2", [128, DEPTH * 2 * 32 * 16])
    Cre_d = din("Cre2", [128, DEPTH * 2 * 32 * 16])
    Cim_d = din("Cim2", [128, DEPTH * 2 * 32 * 16])
    gla0_d = din("gla0", [DEPTH * 2 * 4 * 128, 256])
    x0_d = din("s5x0", [128, DEPTH * 2 * 32])
    x0s_d = din("s5x0s", [128, DEPTH * 2 * 32])
    cst_d = din("cst", [128, 1280])
    csm_d = din("csm", [128, 32])
    pos_d = din("pos", [128, 512])

    yT = dout("yT", [D, NTOK])
    gla_out = dout("gla_out", [NP * DEPTH * 2 * 4 * 128, 256])
    s5_out = dout("s5_out", [128, NP * DEPTH * 2 * 32])

    STREAMS = {"w_inA": ("w_in", 0, 3072, D, DIN), "w_inB": ("w_in", C_U, 2560, D, DIN), "w_inG": ("w_in", C_GLR, 32, D, DIN),
               "w_pg": ("w_pg", 0, D, D, D), "w_glu": ("w_glu", 0, 512, 512, 512), "w_ps": ("w_ps", 0, D, 512, D),
               "w_out": ("w_out", 0, D, D, D), "w_ff1": ("w_ff1", 0, DFF, D, DFF), "w_ff2": ("w_ff2", 0, D, DFF, D)}

    def sgeom(sn):
        src, c0, ncols, K, Msrc = STREAMS[sn]
        KCn = K // 128
        bw = min(512, 4096 // KCn, ncols)
        return src, c0, ncols, K, Msrc, KCn, bw, ncols // bw
    wb = {}
    for sn in STREAMS:
        _, _, _, _, _, KCn_, bw_, nblk_ = sgeom(sn)
        wb[sn] = dscr(sn + "_bf", [DEPTH * nblk_ * 128, KCn_ * bw_], BF16)

    def wstream(name, m0):
        if name == "w_in":
            if C_GLR <= m0 < C_U:
                return "w_inG", None
            sn = "w_inA" if m0 < C_GLR else "w_inB"
        else:
            sn = name
        _, c0, _, _, _, _, bw, _ = sgeom(sn)
        assert (m0 - c0) % bw == 0, (name, m0)
        return sn, (m0 - c0) // bw

    def wblock_ap(sn, l, blk):
        _, _, _, _, _, KCn, bw, nblk = sgeom(sn)
        return dap(wb[sn], (l * nblk + blk) * 128 * KCn * bw, [[KCn * bw, 128], [1, KCn * bw]])
    xm = dscr("xm", [D, NTOK], F32)
    xn = dscr("xn", [D, NTOK], F32)
    xp = dscr("xp", [D, NTOK], F32)
    obs = dscr("obs", [D, NTOK], BF16)
    ybs = dscr("ybs", [512, NTOK], BF16)
    D1 = dscr("D1", [8, 512, 64], BF16)
    D2 = dscr("D2", [8, 512, 64], BF16)
    stabd = dscr("stabd", [DEPTH * 2 * 4, 128, 4096], BF16)
    abrd = dscr("abrd", [DEPTH * 2, 128, 1152], F32)

    P = Prog(nc, gs)

    ident_f = gsb("ident_f", [128, 128], F32)
    ident_b = gsb("ident_b", [128, 128], BF16)
    ones_b = gsb("ones_b", [128, 128], BF16)
    trim = gsb("trim", [128, 2, 128], BF16)
    bmask = gsb("bmask", [128, 2, 128], F32)
    rmask = gsb("rmask", [128, 512], F32)
    csm = gsb("csm_s", [128, 32], F32)
    pos = gsb("pos_s", [128, 512], F32)
    modv = gsb("modv", [128, DEPTH, 48, 2], F32)
    A1 = gsb("A1", [128, DEPTH, 8, 2], F32)
    A2 = gsb("A2", [128, DEPTH, 8, 2], F32)
    n1g = gsb("n1g_s", [128, DEPTH * 8], F32)
    n2g = gsb("n2g_s", [128, DEPTH * 8], F32)
    fng = gsb("fng_s", [128, 8], F32)
    bgneg = gsb("bgneg", [128, DEPTH * 2 * 4], F32)
    gng = gsb("gng_s", [128, DEPTH * 2], F32)
    bglu = gsb("bglu_s", [128, DEPTH * 4], F32)
    d8 = gsb("d8_s", [128, DEPTH * 32], F32)
    wg = gsb("wg", [16, DEPTH * 2 * 512], BF16)
    AAt = gsb("AAt", [128, DEPTH * 2, 2, 32], F32)
    BBt = gsb("BBt", [128, DEPTH * 2, 2, 32], F32)
    NPSF = 6
    psf = [gs.enter_context(nc.psum_tensor(f"ps{i}", [128, 512], F32)) for i in range(NPSF)]
    psbs = [gs.enter_context(nc.psum_tensor(f"psb{i}", [128, 1024], BF16)) for i in range(2)]

    def psum():
        i = P.ps_rr
        P.ps_rr = (i + 1) % NPSF
        return psf[i], ("ps", i)

    SGN = csm[:, 0:1]
    PHS = [csm[:, 1 + i:2 + i] for i in range(4)]
    PHC = [csm[:, 5:6], csm[:, 6:7]]

    with ExitStack() as ph:
        def sb(name, shape, dt):
            return ph.enter_context(nc.sbuf_tensor(f"{name}_p{P.phase_no}", list(shape), dt))

        for sn in STREAMS:
            src, c0, ncols, K, Msrc, KCn, bw, nblk = sgeom(sn)
            for l in range(DEPTH):
                for blk in range(nblk):
                    P.dma(out=dap(wb[sn], (l * nblk + blk) * 128 * KCn * bw, [[KCn * bw, 128], [bw, KCn], [1, bw]]),
                          in_=dap(wsrc[src], l * K * Msrc + c0 + blk * bw, [[Msrc, 128], [128 * Msrc, KCn], [1, bw]]),
                          r=(), w=[("wb", sn)], eng="pool")

        cstt = sb("cstt", [128, 1280], F32)
        P.dma(cstt[:], cst_d.ap(), (), ["cstt"])
        P.dma(csm[:], csm_d.ap(), (), ["csm"])
        P.dma(pos[:], pos_d.ap(), (), ["pos"])
        P.dma(n1g[:], n1g_d.ap(), (), ["n1g"])
        P.dma(n2g[:], n2g_d.ap(), (), ["n2g"])
        P.dma(fng[:], fng_d.ap(), (), ["fng"])
        P.dma(gng[:], gng_d.ap(), (), ["gng"])
        P.dma(bglu[:], bglu_d.ap(), (), ["bglu"])
        P.dma(d8[:], d8_d.ap(), (), ["d8"])
        bgt = sb("bgt", [128, DEPTH * 8], F32)
        P.dma(bgt[:], bg_d.ap(), (), ["bgt"])
        P.act(bgneg[:], bgt[:], AF.Copy, ["bgt"], ["bgneg"], scale=-1.0)
        wgf = sb("wgf", [16, DEPTH * 2 * 512], F32)
        P.dma(wgf[:], w_gu.ap(), (), ["wgf"])
        P.copy("dve", wg[:], wgf[:], ["wgf"], ["wg"])
        P.copy("dve", ident_f[:], cstt[:, 0:128], ["cstt"], ["ident_f"])
        P.copy("dve", ident_b[:], cstt[:, 0:128], ["cstt"], ["ident_b"])
        P.memset("dve", ones_b[:], 1.0, ["ones_b"])
        P.copy("dve", trim[:, 0, :], cstt[:, 128:256], ["cstt"], ["trim"])
        P.copy("dve", trim[:, 1, :], cstt[:, 256:384], ["cstt"], ["trim"])
        P.copy("dve", bmask[:, 0, :], cstt[:, 384:512], ["cstt"], ["bmask"])
        P.copy("dve", bmask[:, 1, :], cstt[:, 512:640], ["cstt"], ["bmask"])
        P.copy("dve", rmask[:], cstt[:, 640:1152], ["cstt"], ["rmask"])

        condt = sb("condt", [128, 16], F32)
        sct = sb("sct", [128, 8, 2], F32)
        bmod = sb("bmod", [128, DEPTH * 48], F32)
        wm = sb("wm", [128, 2, 8 * 512], F32)
        P.dma(condt[:], condT.ap(), (), ["condt"])
        P.dma(bmod[:], b_modT.ap(), (), ["bmod"])
        P.act(sct[:], condt[:].rearrange("p (k c) -> p k c", c=2), AF.Silu, ["condt"], ["sct"])
        for l in range(DEPTH):
            pm, pmk = psf[l], ("ps", l)
            for blk in range(12):
                slot = blk % 2
                P.dma(out=sap(wm, slot * 4096, [[512, 8], [1, 512]], 8192),
                      in_=dap(w_mod, l * D * 6 * D + blk * 512, [[6 * D, 128], [128 * 6 * D, 8], [1, 512]]),
                      r=(), w=[("wm", slot)])
                for mc4 in range(4):
                    mc = blk * 4 + mc4
                    for kc in range(8):
                        P.mm(pm[:, mc * 2:mc * 2 + 2],
                             sap(wm, slot * 4096 + kc * 512 + mc4 * 128, [[1, 128]], 8192),
                             sct[:, kc, :], kc == 0, kc == 7, [("wm", slot), "sct"], [pmk])
            P.tt("dve", modv[:, l, :, :], sap(pm, 0, [[2, 48], [1, 2]], 512),
                 sap(bmod, l * 48, [[1, 48], [0, 2]], DEPTH * 48), ALU.add, [pmk, "bmod"], ["modv"])
            for (Ax, ng, ngk, off) in ((A1, n1g, "n1g", 8), (A2, n2g, "n2g", 32)):
                P.ts("dve", Ax[:, l, :, :], modv[:, l, off:off + 8, :], 1.0, None, ALU.add, None, ["modv"], ["A12"])
                P.tt("dve", Ax[:, l, :, :], Ax[:, l, :, :], sap(ng, l * 8, [[1, 8], [0, 2]], DEPTH * 8), ALU.mult,
                     ["A12", ngk], ["A12"])

        lre = sb("lre", [128, DEPTH * 2 * 32], F32)
        lim = sb("lim", [128, DEPTH * 2 * 32], F32)
        lst = sb("lst", [128, DEPTH * 2 * 32], F32)
        P.dma(lre[:], lre_d.ap(), (), ["lre"])
        P.dma(lim[:], lim_d.ap(), (), ["lim"])
        P.dma(lst[:], lst_d.ap(), (), ["lst"])
        Bre = sb("Bre", [128, 512], F32)
        Bim = sb("Bim", [128, 512], F32)
        Cre = sb("Cre", [128, 512], F32)
        Cim = sb("Cim", [128, 512], F32)
        s32 = {n: sb("s5_" + n, [128, 32], F32) for n in
               ("st", "ar", "ai", "e1", "c1", "s1", "lbr", "lbi", "nr", "den", "t1", "t2", "wr", "wi", "m8", "th8")}
        s256 = {n: sb("s5v_" + n, [128, 288], F32) for n in
                ("th", "ea", "mgn", "mgp", "tA", "tB", "sn", "X1", "X2", "Y1", "Y2")}
        ni = sb("s5_ni", [128, 288], I32)
        abr_t = sb("abr_t", [128, 1152], F32)
        bbr = sb("bbr", [128, 512], F32)
        bbi = sb("bbi", [128, 512], F32)
        bt = sb("bt", [128, 512], F32)
        BZ = sb("BZ", [128, 4096], F32)
        QC = sb("QC", [128, 4096], F32)
        T3 = sb("T3", [128, 4096], F32)
        tWB = sb("tWB", [128, 4096], BF16)
        tWT = sb("tWT", [128, 4096], BF16)
        tM = sb("tM", [128, 4096], BF16)
        tQC = sb("tQC", [128, 4096], BF16)
        Mtmp = sb("Mtmp", [128, 512], F32)

        def sinr(out, outk, th, thk, phase, n):
            tA, tB = s256["tA"][:, 0:n], s256["tB"][:, 0:n]
            P.ts("dve", tA, th, phase, None, ALU.add, None, [thk, "csm"], ["tA"])
            P.ts("dve", ni[:, 0:n], tA, 1.0 / TWO_PI, None, ALU.mult, None, ["tA"], ["ni"])
            P.copy("dve", tB, ni[:, 0:n], ["ni"], ["tB"])
            P.stt(tA, tB, -TWO_PI, tA, ALU.mult, ALU.add, ["tA", "tB"], ["tA"])
            P.ts("dve", tA, tA, 3.14159, -3.14159, ALU.min, ALU.max, ["tA"], ["tA"])
            P.act(out, tA, AF.Sin, ["tA"], [outk])

        def mul(out, a, b, r, w):
            P.tt("dve", out, a, b, ALU.mult, r, w)

        for l in range(DEPTH):
            for dr in range(2):
                ld = l * 2 + dr
                sl = slice(ld * 32, ld * 32 + 32)
                for (t_s, t_d, nm) in ((Bre, Bre_d, "Bre"), (Bim, Bim_d, "Bim"), (Cre, Cre_d, "Cre"), (Cim, Cim_d, "Cim")):
                    P.dma(t_s[:], dap(t_d, ld * 512, [[DEPTH * 2 * 512, 128], [1, 512]]), (), [nm])
                S = {k: v[:] for k, v in s32.items()}
                P.act(S["st"], lst[:, sl], AF.Exp, ["lst"], ["st"])
                mul(S["ar"], lre[:, sl], S["st"], ["lre", "st"], ["ar"])
                mul(S["ai"], lim[:, sl], S["st"], ["lim", "st"], ["ai"])
                P.act(S["e1"], S["ar"], AF.Exp, ["ar"], ["e1"])
                sinr(S["c1"], "c1", S["ai"], "ai", PHC[0], 32)
                sinr(S["s1"], "s1", S["ai"], "ai", PHC[1], 32)
                mul(S["lbr"], S["e1"], S["c1"], ["e1", "c1"], ["lbr"])
                mul(S["lbi"], S["e1"], S["s1"], ["e1", "s1"], ["lbi"])
                P.ts("dve", S["nr"], S["lbr"], -1.0, None, ALU.add, None, ["lbr"], ["nr"])
                mul(S["t1"], lre[:, sl], lre[:, sl], ["lre"], ["t1"])
                mul(S["t2"], lim[:, sl], lim[:, sl], ["lim"], ["t2"])
                P.tt("dve", S["den"], S["t1"], S["t2"], ALU.add, ["t1", "t2"], ["den"])
                P.op("dve", lambda e, o=S["den"]: e.reciprocal(out=o, in_=o), ["den"], ["den"])
                mul(S["t1"], S["nr"], lre[:, sl], ["nr", "lre"], ["t1"])
                mul(S["t2"], S["lbi"], lim[:, sl], ["lbi", "lim"], ["t2"])
                P.tt("dve", S["wr"], S["t1"], S["t2"], ALU.add, ["t1", "t2"], ["wr"])
                mul(S["wr"], S["wr"], S["den"], ["wr", "den"], ["wr"])
                mul(S["t1"], S["lbi"], lre[:, sl], ["lbi", "lre"], ["t1"])
                mul(S["t2"], S["nr"], lim[:, sl], ["nr", "lim"], ["t2"])
                P.tt("dve", S["wi"], S["t1"], S["t2"], ALU.subtract, ["t1", "t2"], ["wi"])
                mul(S["wi"], S["wi"], S["den"], ["wi", "den"], ["wi"])
                wrb = sap(s32["wr"], 0, [[1, 32], [0, 16]], 32)
                wib = sap(s32["wi"], 0, [[1, 32], [0, 16]], 32)
                v3 = lambda t: t[:].rearrange("p (g m) -> p g m", m=16)
                mul(v3(bbr), v3(Bre), wrb, ["Bre", "wr"], ["bbr"])
                mul(v3(bt), v3(Bim), wib, ["Bim", "wi"], ["bt"])
                P.tt("dve", bbr[:], bbr[:], bt[:], ALU.subtract, ["bbr", "bt"], ["bbr"])
                mul(v3(bbi), v3(Bim), wrb, ["Bim", "wr"], ["bbi"])
                mul(v3(bt), v3(Bre), wib, ["Bre", "wi"], ["bt"])
                P.tt("dve", bbi[:], bbi[:], bt[:], ALU.add, ["bbi", "bt"], ["bbi"])
                P.act(S["m8"], S["ar"], AF.Exp, ["ar"], ["m8"], scale=8.0)
                P.ts("dve", S["th8"], S["ai"], 8.0, None, ALU.mult, None, ["ai"], ["th8"])
                sinr(S["c1"], "c1", S["th8"], "th8", PHC[0], 32)
                sinr(S["s1"], "s1", S["th8"], "th8", PHC[1], 32)
                mul(AAt[:, ld, 0, :], S["m8"], S["c1"], ["m8", "c1"], ["AAt"])
                mul(AAt[:, ld, 1, :], S["m8"], S["c1"], ["m8", "c1"], ["AAt"])
                mul(BBt[:, ld, 0, :], S["m8"], S["s1"], ["m8", "s1"], ["BBt"])
                P.ts("dve", BBt[:, ld, 1, :], BBt[:, ld, 0, :], -1.0, None, ALU.mult, None, ["BBt"], ["BBt"])
                W9 = {k: v[:, 0:288] for k, v in s256.items()}
                v9 = lambda t: t[:, 0:288].rearrange("p (j g) -> p j g", g=32)
                erb = sap(csm, 23, [[1, 9], [0, 32]], 32)
                mul(v9(s256["th"]), sap(s32["ai"], 0, [[0, 9], [1, 32]], 32), erb, ["ai", "csm"], ["th"])
                mul(v9(s256["ea"]), sap(s32["ar"], 0, [[0, 9], [1, 32]], 32), erb, ["ar", "csm"], ["ea"])
                P.act(W9["mgp"], W9["ea"], AF.Exp, ["ea"], ["mgp"])
                sinr(W9["sn"], "sn", W9["th"], "th", PHC[0], 288)
                mul(W9["X1"], W9["sn"], W9["mgp"], ["sn", "mgp"], ["X1"])
                sinr(W9["sn"], "sn", W9["th"], "th", PHC[1], 288)
                mul(W9["X2"], W9["sn"], W9["mgp"], ["sn", "mgp"], ["X2"])
                for (off_, src_, sc_) in ((0, "X1", 1.0), (32, "X1", 1.0), (64, "X2", 1.0), (96, "X2", -1.0)):
                    P.ts("dve", sap(abr_t, off_, [[128, 9], [1, 32]], 1152), v9(s256[src_]), sc_, None, ALU.mult, None, [src_], ["abr_t"])
                P.dma(dap(abrd, ld * 128 * 1152, [[1152, 128], [1, 1152]]), abr_t[:], ["abr_t"], [("abrd", ld)])
                V = {k: v[:, 0:256] for k, v in s256.items()}
                v2 = lambda t: t[:, 0:256].rearrange("p (j g) -> p j g", g=32)
                ejb = sap(csm, 7 + dr * 8, [[1, 8], [0, 32]], 32)
                mul(v2(s256["th"]), sap(s32["ai"], 0, [[0, 8], [1, 32]], 32), ejb, ["ai", "csm"], ["th"])
                mul(v2(s256["ea"]), sap(s32["ar"], 0, [[0, 8], [1, 32]], 32), ejb, ["ar", "csm"], ["ea"])
                P.act(V["mgn"], V["ea"], AF.Exp, ["ea"], ["mgn"], scale=-1.0)
                P.act(V["mgp"], V["ea"], AF.Exp, ["ea"], ["mgp"])
                for (nm, ph_i, mg) in (("X1", 0, "mgn"), ("X2", 1, "mgn"), ("Y1", 2, "mgp"), ("Y2", 3, "mgp")):
                    sinr(V["sn"], "sn", V["th"], "th", PHS[ph_i], 256)
                    mul(V[nm], V["sn"], V[mg], ["sn", mg], [nm])
                v4 = lambda t: t[:].rearrange("p (g j m) -> p g j m", j=8, m=16)
                xv = lambda t: sap(t, 0, [[1, 32], [32, 8], [0, 16]], 288)
                bv = lambda t: sap(t, 0, [[16, 32], [0, 8], [1, 16]], 512)
                mul(v4(BZ), xv(s256["X1"]), bv(bbr), ["X1", "bbr"], ["BZ"])
                mul(v4(T3), xv(s256["X2"]), bv(bbi), ["X2", "bbi"], ["T3"])
                P.tt("dve", BZ[:], BZ[:], T3[:], ALU.add, ["BZ", "T3"], ["BZ"])
                mul(v4(QC), xv(s256["Y1"]), bv(Cre), ["Y1", "Cre"], ["QC"])
                mul(v4(T3), xv(s256["Y2"]), bv(Cim), ["Y2", "Cim"], ["T3"])
                P.tt("dve", QC[:], QC[:], T3[:], ALU.add, ["QC", "T3"], ["QC"])
                P.copy("act", tQC[:], QC[:], ["QC"], ["tQC"])
                for gq in range(8):
                    pa, pak = psum()
                    for gi in range(4):
                        g = gq * 4 + gi
                        P.mm(pa[:, gi * 128:(gi + 1) * 128], BZ[:, g * 128:(g + 1) * 128], ident_f[:], True, True,
                             ["BZ", "ident_f"], [pak])
                    pv = pa[:].rearrange("p (a k) -> p a k", k=128)
                    gsl = slice(gq * 512, gq * 512 + 512)
                    P.copy("act", tWB[:, gsl], pa[:], [pak], ["tWB"])
                    tw = tWT[:, gsl].rearrange("p (a k) -> p a k", k=128)
                    P.act(tw[:, :, 0:64], pv[:, :, 64:128], AF.Copy, [pak], ["tWT"], scale=-1.0)
                    P.copy("dve", tw[:, :, 64:128], pv[:, :, 0:64], [pak], ["tWT"])
                    pb_, pbk = psum()
                    for gi in range(4):
                        g = gq * 4 + gi
                        P.mm(pb_[:, gi * 128:(gi + 1) * 128], BZ[:, g * 128:(g + 1) * 128], QC[:, g * 128:(g + 1) * 128],
                             True, True, ["BZ", "QC"], [pbk])
                    P.tt("dve", Mtmp[:].rearrange("p (a k) -> p a k", k=128), pb_[:].rearrange("p (a k) -> p a k", k=128),
                         sap(bmask, dr * 128, [[0, 4], [1, 128]], 256), ALU.mult, [pbk, "bmask"], ["Mtmp"])
                    for gi in range(4):
                        g = gq * 4 + gi
                        sc_ = d8[:, l * 32 + g:l * 32 + g + 1] if dr == 0 else 0.0
                        P.stt(tM[:, g * 128:(g + 1) * 128], ident_f[:], sc_, Mtmp[:, gi * 128:(gi + 1) * 128],
                              ALU.mult, ALU.add, ["ident_f", "d8", "Mtmp"], ["tM"])
                for k, (tt_, nm) in enumerate(((tWB, "tWB"), (tWT, "tWT"), (tM, "tM"), (tQC, "tQC"))):
                    P.dma(dap(stabd, (ld * 4 + k) * 128 * 4096, [[4096, 128], [1, 4096]]), tt_[:], [nm], [("stabd", ld)])
        P.flush()

    seqs = [(i * TP, TP, 0, i) for i in range(NP)] + [(NP * TP, TS, 1, -1)]

    def linear(wt, name, l, K, M, m0, m1, rhs_fn, N, evac, rkeys):
        KCn = K // 128
        sn, blk0 = wstream(name, m0)
        if blk0 is None:
            bw = 32
            blocks = [(0, m0 - C_GLR, m1 - m0)]
        else:
            bw = sgeom(sn)[6]
            assert (m1 - m0) % bw == 0
            blocks = [(blk0 + i, 0, bw) for i in range((m1 - m0) // bw)]
        done = 0
        for (blk, cofs, cuse) in blocks:
            slot = P.w_rr
            P.w_rr = (slot + 1) % P.nws
            P.dma(out=sap(wt, slot * 4096, [[1, KCn * bw]], P.nws * 4096), in_=wblock_ap(sn, l, blk),
                  r=(), w=[("wt", slot)], eng="pool")
            for c0 in range(cofs, cofs + cuse, 128):
                cw = min(128, cofs + cuse - c0)
                for n0 in range(0, N, 512):
                    nn = min(512, N - n0)
                    ps, psk = psum()
                    for kc in range(KCn):
                        P.mm(ps[0:cw, 0:nn], sap(wt, slot * 4096 + kc * bw + c0, [[1, cw]], P.nws * 4096), rhs_fn(kc, n0, nn),
                             kc == 0, kc == KCn - 1, [("wt", slot)] + rkeys, [psk])
                    evac(done // 128, ps, psk, n0, nn)
                done += cw

    def norm_mod(xt, xrow, ht, hrow, sq, rs, tmpf, Ax, shift0, l, cnd, N):
        xk = xrow or "xt"
        hk = hrow or "ht"
        for n0 in range(0, N, 512):
            nn = min(512, N - n0)
            pn, pnk = psum()
            for fc in range(8):
                i2 = fc % 2
                P.act(sq[:, i2, 0:nn], xt[:, fc, n0:n0 + nn], AF.Square, [(xk, fc)], [("sq", i2)])
                P.mm(pn[:, 0:nn], ones_b[:], sq[:, i2, 0:nn], fc == 0, fc == 7, [("sq", i2), "ones_b"], [pnk])
            P.act(rs[:, 0:nn], pn[:, 0:nn], AF.Ln, [pnk], ["rs"], scale=1.0 / D, bias=EPS)
            P.act(rs[:, 0:nn], rs[:, 0:nn], AF.Exp, ["rs"], ["rs"], scale=-0.5)
            for fc in range(8):
                i2 = fc % 2
                P.stt(tmpf[:, i2, 0:nn], xt[:, fc, n0:n0 + nn], Ax[:, l, fc, cnd:cnd + 1], rs[:, 0:nn], ALU.mult, ALU.mult,
                      [(xk, fc), "rs", "A12"], [("tmpf", i2)])
                P.act(ht[:, fc, n0:n0 + nn], tmpf[:, i2, 0:nn], AF.Identity, [("tmpf", i2), "modv"], [(hk, fc)],
                      bias=modv[:, l, shift0 + fc, cnd:cnd + 1])

    def mixer_phase(l, dr, passF, xsrc, xdst0):
        ld = l * 2 + dr
        TM = 512
        with ExitStack() as ph:
            def sb(name, shape, dt):
                return ph.enter_context(nc.sbuf_tensor(f"{name}_p{P.phase_no}", list(shape), dt))
            xt = sb("xt", [128, 8, TM], F32)
            ht = sb("ht", [128, 8, TM], BF16)
            sq = sb("sq", [128, 2, TM], BF16)
            rs = sb("rs", [128, TM], F32)
            tmpf = sb("tmpf", [128, 2, TM], F32)
            qk = sb("qk", [128, 8, TM], BF16)
            vT = sb("vT", [128, 4, 1024], BF16)
            glr = sb("glr", [16, TM], BF16)
            e1 = sb("e1", [128, 2, TM], F32)
            cums = sb("cums", [128, 2, TM], F32)
            ee = sb("ee", [128, 8, TM], BF16)
            etot = sb("etot", [128, 4, 4], F32)
            scm = sb("scm", [128, 16, 128], BF16)
            kT = sb("kT", [128, 8, 128], BF16)
            Sf = sb("Sf", [128, 4, 256], F32)
            Sb = sb("Sb", [128, 4, 256], BF16)
            ob = sb("ob", [128, 8, TM], BF16)
            Uj = sb("Uj", [128, 4, 512], BF16)
            U8 = sb("U8", [128, 2, 32, 64], BF16)
            DD = sb("DD", [128, 2, 32, 64], F32)
            Zt = sb("Zt", [128, 2, 64], F32)
            P1 = sb("P1", [128, 64], F32)
            P2 = sb("P2", [128, 64], F32)
            Tb = sb("Tb", [128, 512], F32)
            P2b = sb("P2b", [128, 512], F32)
            Zs = sb("Zs", [128, 9, 64], F32)
            CX = sb("CX", [128, 512], F32)
            CY = sb("CY", [128, 512], F32)
            abr = sb("abr", [128, 1152], F32)
            Sg = sb("Sg", [128, 32, 64], BF16)
            Yim = sb("Yim", [128, 4, TM], BF16)
            stab = sb("stab", [128, 4, 4096], BF16)
            P.nws = 3
            P.w_rr = 0
            wt = sb("wt", [128, P.nws, 4096], BF16)
            x0s = sb("x0s", [128, 32], F32)
            Zfin = sb("Zfin", [128, 32], F32)

            P.dma(abr[:], dap(abrd, ld * 128 * 1152, [[1152, 128], [1, 1152]]), (), ["abr"])
            sgt = sq
            for k in range(4):
                P.dma(stab[:, k, :], dap(stabd, (ld * 4 + k) * 128 * 4096, [[4096, 128], [1, 4096]]), (), ["stab"])

            XKall = [("xt", fc) for fc in range(8)]

            def load_x(k):
                t0_, TT_ = items[k]
                P.dma(xt[:, :, 0:TT_], dap(xsrc, t0_, [[NTOK, 128], [128 * NTOK, 8], [1, TT_]]), (), XKall)

            Y8v = e1.bitcast(BF16)
            titems = []
            merge = (NP % 2 == 0) and (2 * TP == TM)
            si = 0
            while si < len(seqs):
                (tok0_, T_, cnd_, pidx_) = seqs[si]
                if merge and pidx_ >= 0:
                    titems.append((tok0_, TM, cnd_, pidx_, 0, True, True, (pidx_, pidx_ + 1)))
                    si += 2
                    continue
                TT_ = min(TM, T_)
                nt_ = T_ // TT_
                order_ = list(range(nt_)) if dr == 0 else list(range(nt_ - 1, -1, -1))
                for n_, ti_ in enumerate(order_):
                    titems.append((tok0_ + ti_ * TT_, TT_, cnd_, pidx_, ti_, n_ == 0, n_ == nt_ - 1, (pidx_,)))
                si += 1
            items = [(t_[0], t_[1]) for t_ in titems]

            load_x(0)

            def init_states(pidx):
                if pidx < 0:
                    P.dma(Sf[:], dap(gla0_d, ld * 4 * 128 * 256, [[256, 128], [128 * 256, 4], [1, 256]]), (),
                          [("Sf", h) for h in range(4)])
                    P.copy("dve", Sb[:], Sf[:], [("Sf", h) for h in range(4)], [("Sb", h) for h in range(4)])
                    P.dma(Zt[:, 0, 0:32], x0_d.ap()[:, ld * 32:ld * 32 + 32], (), [("Zt", 0)])
                    P.dma(x0s[:], x0s_d.ap()[:, ld * 32:ld * 32 + 32], (), ["x0s"])
                    P.ts("dve", Zt[:, 0, 32:64], x0s[:], SGN, None, ALU.mult, None, ["x0s", "csm"], [("Zt", 0)])
                else:
                    P.memset("dve", Sf[:], 0.0, [("Sf", h) for h in range(4)])
                    P.memset("dve", Sb[:], 0.0, [("Sb", h) for h in range(4)])
                    P.memset("dve", Zt[:, 0, :], 0.0, [("Zt", 0)])

            def tile_gen(k):
                (t0, TT, cnd, pidx, ti, first, lastt, pids) = titems[k]
                two = len(pids) == 2
                pfirst = (pids[0] if dr == 0 else pids[-1])
                plast = (pids[-1] if dr == 0 else pids[0])
                NC = TT // 8
                nch = TT // 128
                ub = k % 2
                XK = [("xt", fc) for fc in range(8)]
                HK = [("ht", fc) for fc in range(8)]
                if l == 0 and pidx < 0 and not passF:
                    r0 = (ti * TT) // 64
                    nr_ = TT // 64
                    P.tt("dve", sap(xt, 0, [[TM, 4], [64, nr_], [1, 64]], 8 * TM), sap(xt, 0, [[TM, 4], [64, nr_], [1, 64]], 8 * TM),
                         sap(pos, r0, [[64, 4], [1, nr_], [0, 64]], 512), ALU.add, XK[0:4] + ["pos"], XK[0:4])
                    P.tt("dve", sap(xt, 4 * TM, [[TM, 4], [64, nr_], [1, 64]], 8 * TM),
                         sap(xt, 4 * TM, [[TM, 4], [64, nr_], [1, 64]], 8 * TM),
                         sap(pos, 256, [[64, 4], [0, nr_], [1, 64]], 512), ALU.add, XK[4:8] + ["pos"], XK[4:8])
                if xdst0 is not None:
                    P.dma(dap(xdst0, t0, [[NTOK, 128], [128 * NTOK, 8], [1, TT]]), xt[:, :, 0:TT], XK, ["xp"])
                norm_mod(xt, None, ht, None, sq, rs, tmpf, A1, 0, l, cnd, TT)
                if k + 1 < len(items):
                    load_x(k + 1)
                yield
                rhs_h = lambda kc, n0, nn: ht[:, kc, n0:n0 + nn]

                slot = P.w_rr
                P.w_rr = (slot + 1) % P.nws
                P.dma(out=sap(wt, slot * 4096, [[1, 4096]], P.nws * 4096), in_=wblock_ap("w_inB", l, 0),
                      r=(), w=[("wt", slot)], eng="pool")
                for uc in range(4):
                    ps, psk = psum()
                    for kc in range(8):
                        P.mm(ps[:, 0:TT], sap(wt, slot * 4096 + kc * 512 + uc * 128, [[1, 128]], P.nws * 4096),
                             sap(ht, kc * TM, [[1, 8], [8, NC]], 8 * TM), kc == 0, kc == 7, [("wt", slot), ("ht", kc)], [psk])
                    P.copy("act", Uj[:, uc, 0:TT], ps[:, 0:TT], [psk], [("Uj", uc)])
                for uc in range(4):
                    P.dma(dap(D1, uc * 128 * 64, [[64, 128], [512 * 64, 8], [1, NC]]), sap(Uj, uc * 512, [[NC, 8], [1, NC]], 2048),
                          [("Uj", uc)], ["D1"])
                for j in range(8):
                    P.dma(sapp(U8, 16 * j, 16, ub * 2048, [[64, 32], [1, NC]], 4096), dap(D1, j * 512 * 64, [[64, 16], [16 * 64, 32], [1, NC]]),
                          ["D1"], [("U8", ub)])

                def ev_qk(mc, ps, psk, n0, nn):
                    if mc < 4:
                        P.act(qk[:, mc, 0:nn], ps[:, 0:nn], AF.Copy, [psk], [("qk", mc)], scale=DK ** -0.5)
                    else:
                        P.copy("act", qk[:, mc, 0:nn], ps[:, 0:nn], [psk], [("qk", mc)])
                linear(wt, "w_in", l, D, DIN, C_Q, C_K + 512, rhs_h, TT, ev_qk, HK)

                def ev_glr(mc, ps, psk, n0, nn):
                    P.copy("act", glr[0:16, 0:nn], ps[0:16, 0:nn], [psk], ["glr"])
                linear(wt, "w_in", l, D, DIN, C_GLR + 16 * dr, C_GLR + 16 * dr + 16, rhs_h, TT, ev_glr, HK)

                for half in range(2):
                    slot = P.w_rr
                    P.w_rr = (slot + 1) % P.nws
                    P.dma(out=sap(wt, slot * 4096, [[1, 4096]], P.nws * 4096), in_=wblock_ap("w_inA", l, 2 + half),
                          r=(), w=[("wt", slot)], eng="pool")
                    for tc in range(nch):
                        ps, psk = psum()
                        for kc in range(8):
                            P.mm(ps[:, 0:512], ht[:, kc, tc * 128:(tc + 1) * 128], sap(wt, slot * 4096 + kc * 512, [[1, 512]], P.nws * 4096),
                                 kc == 0, kc == 7, [("wt", slot), ("ht", kc)], [psk])
                        P.copy("act", vT[:, tc, half * 512:(half + 1) * 512], ps[:, 0:512],
                               [psk], [("vT", tc)])

                yield
                if first:
                    init_states(pidx)
                for hh in range(4):
                    s2 = hh % 2
                    pz, pzk = psum()
                    P.mm(pz[:, 0:TT], wg[0:16, ld * 512 + hh * 128:ld * 512 + (hh + 1) * 128], glr[0:16, 0:TT], True, True,
                         ["wg", "glr"], [pzk])
                    P.act(e1[:, s2, 0:TT], pz[:, 0:TT], AF.Exp, [pzk, "bgneg"], [("e1", s2)], scale=-1.0,
                          bias=bgneg[:, ld * 4 + hh:ld * 4 + hh + 1])
                    P.act(e1[:, s2, 0:TT], e1[:, s2, 0:TT], AF.Ln, [("e1", s2)], [("e1", s2)], bias=1.0)
                    if dr == 0:
                        d1_ap, o_ap = e1[:, s2, 0:TT], cums[:, s2, 0:TT]
                    else:
                        d1_ap = sap(e1, s2 * TM + TT - 1, [[-1, TT]], 2 * TM)
                        o_ap = sap(cums, s2 * TM + TT - 1, [[-1, TT]], 2 * TM)
                    P.op("dve", lambda e, o=o_ap, d1=d1_ap, n=TT: e.tensor_tensor_scan(
                        out=o, data0=rmask[:, 0:n], data1=d1, initial=0.0, op0=ALU.mult, op1=ALU.add),
                        [("e1", s2), "rmask"], [("cums", s2)])
                    P.act(ee[:, hh, 0:TT], cums[:, s2, 0:TT], AF.Exp, [("cums", s2)], [("ee", hh)], scale=-1.0 / 16)
                    P.act(ee[:, 4 + hh, 0:TT], cums[:, s2, 0:TT], AF.Exp, [("cums", s2)], [("ee", 4 + hh)], scale=1.0 / 16)
                    P.act(etot[:, hh, 0:nch], sap(cums, s2 * TM + (127 if dr == 0 else 0), [[128, nch]], 2 * TM), AF.Exp,
                          [("cums", s2)], ["etot"], scale=-1.0 / 16)
                    P.tt("pool", qk[:, hh, 0:TT], qk[:, hh, 0:TT], ee[:, hh, 0:TT], ALU.mult, [("qk", hh), ("ee", hh)], [("qk", hh)])
                    P.tt("pool", qk[:, 4 + hh, 0:TT], qk[:, 4 + hh, 0:TT], ee[:, 4 + hh, 0:TT], ALU.mult,
                         [("qk", 4 + hh), ("ee", 4 + hh)], [("qk", 4 + hh)])

                OK_ = [("ob", h) for h in range(4)]
                if passF:
                    P.dma(ob[:, :, 0:TT], dap(obs, t0, [[NTOK, 128], [128 * NTOK, 8], [1, TT]]), (), OK_)

                corder = list(range(nch)) if dr == 0 else list(range(nch - 1, -1, -1))

                def stage1m(c):
                    cs = slice(c * 128, (c + 1) * 128)
                    for hh in range(4):
                        ix = c * 4 + hh
                        psc, psck = psum()
                        P.mm(psc[:, 0:128], qk[:, 4 + hh, cs], qk[:, hh, cs], True, True, [("qk", 4 + hh), ("qk", hh)], [psck])
                        P.tt("dve", scm[:, ix, :], psc[:, 0:128], trim[:, dr, :], ALU.mult, [psck, "trim"], [("scm", ix)])

                def stage1(c):
                    cs = slice(c * 128, (c + 1) * 128)
                    for hh in range(4):
                        ix = (c % 2) * 4 + hh
                        i2 = ix % 2
                        P.transpose(psbs[i2][:, 0:128], qk[:, 4 + hh, cs], ident_b[:], [("qk", 4 + hh), "ident_b"],
                                    [("psb", i2)])
                        P.copy("act", kT[:, ix, :], psbs[i2][:, 0:128], [("psb", i2)], [("kT", ix)])

                for c_ in corder:
                    stage1m(c_)
                stage1(corder[0])

                for gq in range(4):
                    pa, pak = psum()
                    pb_, pbk = psum()
                    for gi in range(8):
                        g = gq * 8 + gi
                        P.mm(pa[:, gi * NC:(gi + 1) * NC], stab[:, 0, g * 128:(g + 1) * 128], U8[:, ub, g, 0:NC], True, True,
                             ["stab", ("U8", ub)], [pak])
                    for gi in range(8):
                        g = gq * 8 + gi
                        P.mm(pb_[:, gi * NC:(gi + 1) * NC], stab[:, 1, g * 128:(g + 1) * 128], U8[:, ub, g, 0:NC], True, True,
                             ["stab", ("U8", ub)], [pbk])
                    P.copy("act", DD[:, 0, gq * 8:(gq + 1) * 8, 0:NC], pa[:, 0:8 * NC].rearrange("p (a c) -> p a c", c=NC), [pak], ["DD"])
                    P.copy("act", DD[:, 1, gq * 8:(gq + 1) * 8, 0:NC], pb_[:, 0:8 * NC].rearrange("p (a c) -> p a c", c=NC), [pbk], ["DD"])

                NB = NC // 8
                cst_ = 8 if dr == 0 else -8
                rs_ = 1 if dr == 0 else -1
                off = (lambda r, b0: r + 8 * b0) if dr == 0 else (lambda r, b0: NC - 1 - r - 8 * b0)
                dd_full = lambda r, nb: sap(DD, off(r, 0), [[2048, 2], [64, 32], [cst_, nb]], 4096)
                dd_swap = lambda r, nb: sap(DD, off(r, 0) + 2048, [[-2048, 2], [64, 32], [cst_, nb]], 4096)
                AAb = lambda r, nb: sap(abr, r * 128, [[32, 2], [1, 32], [0, nb]], 1152)
                BBb = lambda r, nb: sap(abr, r * 128 + 64, [[32, 2], [1, 32], [0, nb]], 1152)
                Tv = sap(Tb, 0, [[256, 2], [8, 32], [1, NB]], 512)
                Tsw = sap(Tb, 256, [[-256, 2], [8, 32], [1, NB]], 512)
                P2v = sap(P2b, 0, [[256, 2], [8, 32], [1, NB]], 512)
                v3 = lambda t: t[:].rearrange("p (a g) -> p a g", g=32)

                def scan_gen():
                    for r in range(8):
                        if r == 0:
                            P.tt("dve", P2v, dd_swap(0, NB), BBb(1, NB), ALU.mult, ["DD", "abr"], ["P2b"])
                            P.tt("dve", Tv, dd_full(0, NB), AAb(1, NB), ALU.mult, ["DD", "abr"], ["Tb"])
                        else:
                            P.tt("dve", Tv, dd_full(r - 1, NB), dd_full(r, NB), ALU.add, ["DD"], ["Tb"])
                            P.tt("dve", P2v, Tsw, BBb(1, NB), ALU.mult, ["Tb", "abr"], ["P2b"])
                            P.tt("dve", Tv, Tv, AAb(1, NB), ALU.mult, ["Tb", "abr"], ["Tb"])
                        P.tt("dve", dd_full(r, NB), Tv, P2v, ALU.add, ["Tb", "P2b"], ["DD"])
                        yield
                    P.copy("dve", Zs[:, 0, :], Zt[:, 0, :], [("Zt", 0)], ["Zs"])
                    for b in range(NB):
                        P.tt("dve", P1[:], Zs[:, b, :], sap(abr, 8 * 128, [[1, 64]], 1152), ALU.mult, ["Zs", "abr"], ["P1"])
                        P.tt("dve", v3(P2), sap(Zs, b * 64 + 32, [[-32, 2], [1, 32]], 576), sap(abr, 8 * 128 + 64, [[32, 2], [1, 32]], 1152),
                             ALU.mult, ["Zs", "abr"], ["P2"])
                        P.tt("dve", P1[:], P1[:], P2[:], ALU.add, ["P1", "P2"], ["P1"])
                        P.tt("dve", sap(Zs, (b + 1) * 64, [[32, 2], [1, 32]], 576), v3(P1), sap(DD, off(7, b), [[2048, 2], [64, 32]], 4096),
                             ALU.add, ["P1", "DD"], ["Zs"])
                        if two and b == NB // 2 - 1:
                            so_ = (pfirst * DEPTH + l) * 2 + dr
                            P.copy("dve", Zfin[:], Zs[:, b + 1, 0:32], ["Zs"], ["Zfin"])
                            P.dma(s5_out.ap()[:, so_ * 32:so_ * 32 + 32], Zfin[:], ["Zfin"], ["s5_out"])
                            P.memset("dve", Zs[:, b + 1, :], 0.0, ["Zs"])
                        yield
                    for gqr in range(4):
                        g0 = gqr * 8
                        CXv = sap(CX, 0, [[NB * 8, 8], [8, NB], [1, 8]], 512)
                        CYv = sap(CY, 0, [[NB * 8, 8], [8, NB], [1, 8]], 512)
                        P.tt("dve", CXv, sap(Zs, g0, [[1, 8], [64, NB], [0, 8]], 576), sap(abr, g0, [[1, 8], [0, NB], [128, 8]], 1152),
                             ALU.mult, ["Zs", "abr"], ["CX"])
                        P.tt("dve", CYv, sap(Zs, 32 + g0, [[1, 8], [64, NB], [0, 8]], 576), sap(abr, 64 + g0, [[1, 8], [0, NB], [128, 8]], 1152),
                             ALU.mult, ["Zs", "abr"], ["CY"])
                        P.tt("dve", CXv, CXv, CYv, ALU.add, ["CX", "CY"], ["CX"])
                        P.copy("dve", sap(Sg, off(0, 0) + g0 * 64, [[64, 8], [cst_, NB]], 2048), sap(CX, 0, [[NB * 8, 8], [8, NB]], 512),
                               ["CX"], ["Sg"])
                        P.tt("dve", sap(Sg, off(1, 0) + g0 * 64, [[64, 8], [cst_, NB], [rs_, 7]], 2048),
                             sap(DD, off(0, 0) + g0 * 64, [[64, 8], [cst_, NB], [rs_, 7]], 4096),
                             sap(CX, 1, [[NB * 8, 8], [8, NB], [1, 7]], 512), ALU.add, ["DD", "CX"], ["Sg"])
                        yield
                    P.copy("dve", Zt[:, 0, :], Zs[:, NB, :], ["Zs"], [("Zt", 0)])

                chain = scan_gen()

                def pump(n):
                    for _ in range(n):
                        if next(chain, "done") == "done":
                            return

                def chain_finish():
                    pump(64)

                yield
                if passF:
                    def ev_ga(mc, ps, psk, n0, nn):
                        P.act(ee[:, mc, 0:nn], ps[:, 0:nn], AF.Sigmoid, [psk], [("ee", mc)])
                        pump(1)
                    linear(wt, "w_in", l, D, DIN, C_GA, C_GA + 1024, rhs_h, TT, ev_ga, HK)
                for ci, c in enumerate(corder):
                    cs = slice(c * 128, (c + 1) * 128)
                    if two and ci == nch // 2:
                        so_ = (pfirst * DEPTH + l) * 2 + dr
                        P.dma(dap(gla_out, so_ * 4 * 128 * 256, [[256, 128], [128 * 256, 4], [1, 256]]), Sf[:],
                              [("Sf", h) for h in range(4)], ["gla_out"])
                        P.memset("pool", Sf[:], 0.0, [("Sf", h) for h in range(4)])
                        P.memset("pool", Sb[:], 0.0, [("Sb", h) for h in range(4)])
                    if ci + 1 < nch:
                        stage1(corder[ci + 1])
                    for hh in range(4):
                        ix = (c % 2) * 4 + hh
                        sx = c * 4 + hh
                        po, pok = psum()
                        for v2 in range(2):
                            P.mm(po[:, v2 * 128:(v2 + 1) * 128], vT[:, c, hh * 256 + v2 * 128:hh * 256 + (v2 + 1) * 128],
                                 scm[:, sx, :], True, False, [("vT", c), ("scm", sx)], [pok])
                            P.mm(po[:, v2 * 128:(v2 + 1) * 128], Sb[:, hh, v2 * 128:(v2 + 1) * 128], qk[:, hh, cs], False, not passF,
                                 [("Sb", hh), ("qk", hh)], [pok])
                            if passF:
                                P.mm(po[:, v2 * 128:(v2 + 1) * 128], ident_b[:], ob[:, hh * 2 + v2, cs], False, True,
                                     ["ident_b", ("ob", hh)], [pok])
                        o_out = ob[:, hh * 2:hh * 2 + 2, cs]
                        o_in = po[:, 0:256].rearrange("p (a k) -> p a k", k=128)
                        P.copy("act", o_out, o_in, [pok], [("ob", hh)])
                        pd, pdk = psum()
                        P.mm(pd[:, 0:256], kT[:, ix, :], vT[:, c, hh * 256:(hh + 1) * 256], True, False, [("kT", ix), ("vT", c)], [pdk])
                        P.mm(pd[:, 0:256], ident_f[:], Sf[:, hh, :], False, True, ["ident_f", ("Sf", hh)], [pdk])
                        P.act(Sf[:, hh, :], pd[:, 0:256], AF.Identity, [pdk, "etot"], [("Sf", hh)], scale=etot[:, hh, c:c + 1])
                        P.act(Sb[:, hh, :], pd[:, 0:256], AF.Identity, [pdk, "etot"], [("Sb", hh)], scale=etot[:, hh, c:c + 1])
                        pump(2)

                def s5_out_block():
                    chain_finish()
                    for gq in range(4):
                        py, pyk = psum()
                        for gi in range(8):
                            g = gq * 8 + gi
                            P.mm(py[:, gi * NC:(gi + 1) * NC], stab[:, 2, g * 128:(g + 1) * 128], U8[:, ub, g, 0:NC], True, False,
                                 ["stab", ("U8", ub)], [pyk])
                            P.mm(py[:, gi * NC:(gi + 1) * NC], stab[:, 3, g * 128:(g + 1) * 128], Sg[:, g, 0:NC], False, True,
                                 ["stab", "Sg"], [pyk])
                        P.copy("act", sap(Y8v, gq * 512, [[64, 8], [1, NC]], 2048),
                               py[:, 0:8 * NC].rearrange("p (a c) -> p a c", c=NC), [pyk], [("e1", 0), ("e1", 1)])
                    for i in range(8):
                        P.dma(dap(D2, i * 512 * 64, [[64, 16], [16 * 64, 32], [1, NC]]), sapp(Y8v, 16 * i, 16, 0, [[64, 32], [1, NC]], 2048),
                              [("e1", 0), ("e1", 1)], ["D2"])
                    for uc in range(4):
                        P.dma(sap(Yim, uc * TM, [[NC, 8], [1, NC]], 4 * TM), dap(D2, uc * 128 * 64, [[64, 128], [512 * 64, 8], [1, NC]]),
                              ["D2"], ["Yim"])


                yield
                if not passF:
                    s5_out_block()
                    P.dma(dap(ybs, t0, [[NTOK, 128], [128 * NTOK, 4], [1, TT]]), Yim[:, :, 0:TT], ["Yim"], ["ybs"])
                    P.dma(dap(obs, t0, [[NTOK, 128], [128 * NTOK, 8], [1, TT]]), ob[:, :, 0:TT], OK_, ["obs"])
                else:

                    VK = [("vT", i) for i in range(4)]
                    rs2 = [e1[:, 0, 0:TT], e1[:, 1, 0:TT], cums[:, 0, 0:TT], cums[:, 1, 0:TT]]
                    rs2k = [("e1", 0), ("e1", 1), ("cums", 0), ("cums", 1)]
                    for hh in range(4):
                        pn, pnk = psum()
                        for v2 in range(2):
                            P.act(sq[:, v2, 0:TT], ob[:, hh * 2 + v2, 0:TT], AF.Square, [("ob", hh)], [("sq", v2)])
                            P.mm(pn[:, 0:TT], ones_b[:], sq[:, v2, 0:TT], v2 == 0, v2 == 1, [("sq", v2), "ones_b"], [pnk])
                        P.act(rs2[hh], pn[:, 0:TT], AF.Ln, [pnk], [rs2k[hh]], scale=1.0 / DV, bias=EPS)
                        P.act(rs2[hh], rs2[hh], AF.Exp, [rs2k[hh]], [rs2k[hh]], scale=-0.5)

                    def ev_r(vc, ps, psk, n0, nn):
                        i2 = vc % 2
                        hh = vc // 2
                        P.act(sgt[:, i2, 0:nn], ps[:, 0:nn], AF.Silu, [psk], [("sq", i2)])
                        P.stt(tmpf[:, i2, 0:nn], ob[:, vc, 0:nn], gng[:, l * 2 + i2:l * 2 + i2 + 1], rs2[hh], ALU.mult, ALU.mult,
                              [("ob", hh), "gng", rs2k[hh]], [("tmpf", i2)])
                        P.tt("dve", ob[:, vc, 0:nn], tmpf[:, i2, 0:nn], sgt[:, i2, 0:nn], ALU.mult, [("tmpf", i2), ("sq", i2)], [("ob", hh)])
                    linear(wt, "w_in", l, D, DIN, C_R, C_R + 1024, rhs_h, TT, ev_r, HK)


                    s5_out_block()

                    def ev_pg(mc, ps, psk, n0, nn):
                        P.tt("dve", qk[:, mc, 0:nn], ps[:, 0:nn], ee[:, mc, 0:nn], ALU.mult, [psk, ("ee", mc)], [("qk", mc)])
                    linear(wt, "w_pg", l, D, D, 0, D, lambda kc, n0, nn: ob[:, kc, n0:n0 + nn], TT, ev_pg, OK_)

                    def ev_gb(mc, ps, psk, n0, nn):
                        P.act(ee[:, mc, 0:nn], ps[:, 0:nn], AF.Sigmoid, [psk], [("ee", mc)])
                    linear(wt, "w_in", l, D, DIN, C_GB, C_GB + 1024, rhs_h, TT, ev_gb, HK)
                    yield

                    ybv = sap(vT, 2048, [[TM, 4], [1, TT]], 4096)
                    P.dma(ybv, dap(ybs, t0, [[NTOK, 128], [128 * NTOK, 4], [1, TT]]), (), VK[2:4])
                    P.tt("dve", ybv, Yim[:, :, 0:TT], ybv, ALU.add, ["Yim"] + VK[2:4], VK[2:4])
                    P.act(Yim[:, :, 0:TT], ybv, AF.Gelu_apprx_tanh, VK[2:4], ["Yim"])

                    def ev_glu(mc, ps, psk, n0, nn):
                        i2 = mc % 2
                        P.act(sgt[:, i2, 0:nn], ps[:, 0:nn], AF.Sigmoid, [psk, "bglu"], [("sq", i2)], bias=bglu[:, l * 4 + mc:l * 4 + mc + 1])
                        P.tt("dve", sap(vT, mc * TM, [[1, nn]], 4096), Yim[:, mc, 0:nn], sgt[:, i2, 0:nn], ALU.mult,
                             ["Yim", ("sq", i2)], [("vT", mc // 2)])
                    linear(wt, "w_glu", l, 512, 512, 0, 512, lambda kc, n0, nn: Yim[:, kc, n0:n0 + nn], TT, ev_glu, ["Yim"])

                    def ev_ps(mc, ps, psk, n0, nn):
                        i2 = mc % 2
                        P.tt("dve", tmpf[:, i2, 0:nn].rearrange("p (c i) -> p c i", i=8), sap(ps, 0, [[1, NC], [NC, 8]], 512),
                             ee[:, mc, 0:nn].rearrange("p (c i) -> p c i", i=8), ALU.mult, [psk, ("ee", mc)], [("tmpf", i2)])
                        P.tt("dve", qk[:, mc, 0:nn], tmpf[:, i2, 0:nn], qk[:, mc, 0:nn], ALU.add, [("tmpf", i2), ("qk", mc)], [("qk", mc)])
                    linear(wt, "w_ps", l, 512, D, 0, D, lambda kc, n0, nn: sap(vT, kc * TM + n0, [[1, nn]], 4096), TT, ev_ps, VK[0:2])

                    def ev_out(mc, ps, psk, n0, nn):
                        i2 = mc % 2
                        P.dma(tmpf[:, i2, 0:nn], dap(xsrc, mc * 128 * NTOK + t0, [[NTOK, 128], [1, nn]]), (), [("tmpf", i2)])
                        P.stt(tmpf[:, i2, 0:nn], ps[:, 0:nn], modv[:, l, 16 + mc, cnd:cnd + 1], tmpf[:, i2, 0:nn], ALU.mult, ALU.add,
                              [psk, "modv", ("tmpf", i2)], [("tmpf", i2)])
                        P.dma(dap(xm, mc * 128 * NTOK + t0, [[NTOK, 128], [1, nn]]), tmpf[:, i2, 0:nn], [("tmpf", i2)], ["xm"])
                    linear(wt, "w_out", l, D, D, 0, D, lambda kc, n0, nn: qk[:, kc, n0:n0 + nn], TT, ev_out, [("qk", i) for i in range(8)])

                if lastt and pidx >= 0:
                    so = (plast * DEPTH + l) * 2 + dr
                    P.dma(dap(gla_out, so * 4 * 128 * 256, [[256, 128], [128 * 256, 4], [1, 256]]), Sf[:],
                          [("Sf", h) for h in range(4)], ["gla_out"])
                    P.dma(s5_out.ap()[:, so * 32:so * 32 + 32], Zt[:, 0, 0:32], [("Zt", 0)], ["s5_out"])

            ntl = len(titems)
            gens = {0: tile_gen(0)}
            next(gens[0])
            next(gens[0])
            for k in range(ntl):
                g = gens[k]
                nxt = None
                if k + 1 < ntl:
                    nxt = gens[k + 1] = tile_gen(k + 1)
                next(g)
                if not passF and nxt is not None:
                    next(nxt)
                next(g)
                if not passF and nxt is not None:
                    next(nxt)
                if passF:
                    next(g)
                    if nxt is not None:
                        next(nxt)
                    next(g, None)
                    if nxt is not None:
                        next(nxt)
                else:
                    next(g, None)
            P.flush()

    def mlp_phase(l, last):
        TM = 1024
        segs = []
        if NP > 0:
            segs.append((0, NP * TP, 0))
        segs.append((NP * TP, TS, 1))
        with ExitStack() as ph:
            def sb(name, shape, dt):
                return ph.enter_context(nc.sbuf_tensor(f"{name}_p{P.phase_no}", list(shape), dt))
            xts = [sb("xt2a", [128, 8, TM], F32), sb("xt2b", [128, 8, TM], F32)]
            hts = [sb("ht2a", [128, 8, TM], BF16), sb("ht2b", [128, 8, TM], BF16)]
            hid = sb("hid", [128, 32, TM], BF16)
            sq = sb("sq2", [128, 2, 512], BF16)
            rs = sb("rs_m", [128, 512], F32)
            tmpf = sb("tmpf2", [128, 2, 512], F32)
            rl = sb("rl", [128, 2, 512], BF16)
            P.nws = 2
            P.w_rr = 0
            wt = sb("wt2", [128, P.nws, 4096], BF16)
            dst = yT if last else xn
            tiles = []
            for (s0, slen, cnd) in segs:
                for t0 in range(s0, s0 + slen, TM):
                    tiles.append((t0, min(TM, s0 + slen - t0), cnd))

            def front(k):
                t0, TT, cnd = tiles[k]
                sl2 = k % 2
                xt, ht = xts[sl2], hts[sl2]
                xkn, hkn = f"xt{sl2}", f"ht{sl2}"
                P.dma(xt[:, :, 0:TT], dap(xm, t0, [[NTOK, 128], [128 * NTOK, 8], [1, TT]]), (), [(xkn, fc) for fc in range(8)])
                norm_mod(xt, xkn, ht, hkn, sq, rs, tmpf, A2, 24, l, cnd, TT)

            front(0)
            for k, (t0, TT, cnd) in enumerate(tiles):
                sl2 = k % 2
                xt, ht = xts[sl2], hts[sl2]
                xkn, hkn = f"xt{sl2}", f"ht{sl2}"
                XK = [(xkn, fc) for fc in range(8)]
                HK = [(hkn, fc) for fc in range(8)]

                def ev_ff1(mc, ps, psk, n0, nn):
                    i2 = (mc + n0 // 512) % 2
                    P.act(rl[:, i2, 0:nn], ps[:, 0:nn], AF.Relu, [psk], [("rl", i2)])
                    P.tt("dve", hid[:, mc, n0:n0 + nn], rl[:, i2, 0:nn], rl[:, i2, 0:nn], ALU.mult, [("rl", i2)], [("hid", mc)])
                linear(wt, "w_ff1", l, D, DFF, 0, DFF, lambda kc, n0, nn: ht[:, kc, n0:n0 + nn], TT, ev_ff1, HK)

                if k + 1 < len(tiles):
                    front(k + 1)

                def ev_ff2(mc, ps, psk, n0, nn):
                    P.stt(xt[:, mc, n0:n0 + nn], ps[:, 0:nn], modv[:, l, 40 + mc, cnd:cnd + 1], xt[:, mc, n0:n0 + nn], ALU.mult, ALU.add,
                          [psk, "modv", (xkn, mc)], [(xkn, mc)])
                linear(wt, "w_ff2", l, DFF, D, 0, D, lambda kc, n0, nn: hid[:, kc, n0:n0 + nn], TT, ev_ff2,
                       [("hid", i) for i in range(32)])
                if last:
                    for n0 in range(0, TT, 512):
                        nn = min(512, TT - n0)
                        pn, pnk = psum()
                        for fc in range(8):
                            i2 = fc % 2
                            P.act(sq[:, i2, 0:nn], xt[:, fc, n0:n0 + nn], AF.Square, [(xkn, fc)], [("sq", i2)])
                            P.mm(pn[:, 0:nn], ones_b[:], sq[:, i2, 0:nn], fc == 0, fc == 7, [("sq", i2), "ones_b"], [pnk])
                        P.act(rs[:, 0:nn], pn[:, 0:nn], AF.Ln, [pnk], ["rs"], scale=1.0 / D, bias=EPS)
                        P.act(rs[:, 0:nn], rs[:, 0:nn], AF.Exp, ["rs"], ["rs"], scale=-0.5)
                        for fc in range(8):
                            P.stt(xt[:, fc, n0:n0 + nn], xt[:, fc, n0:n0 + nn], fng[:, fc:fc + 1], rs[:, 0:nn], ALU.mult, ALU.mult,
                                  [(xkn, fc), "fng", "rs"], [(xkn, fc)])
                P.dma(dap(dst, t0, [[NTOK, 128], [128 * NTOK, 8], [1, TT]]), xt[:, :, 0:TT], XK, ["dst"])
            P.flush()

    for l in range(DEPTH):
        if l == 0:
            mixer_phase(l, 1, False, xT, xp)
            mixer_phase(l, 0, True, xp, None)
        else:
            mixer_phase(l, 1, False, xn, None)
            mixer_phase(l, 0, True, xn, None)
        mlp_phase(l, l == DEPTH - 1)
    P.final_wait()
    build.n_inst = P.n_inst
    return nc


def _fm_vec(v):
    v = np.asarray(v, np.float32)
    lead = v.shape[:-1]
    n = v.shape[-1] // 128
    v = v.reshape(lead + (n, 128))
    v = np.moveaxis(v, -1, 0)
    return np.ascontiguousarray(v).reshape(128, -1)


def _constants():
    j = np.arange(128)
    ident = np.eye(128, dtype=np.float32)
    trif = (j[:, None] <= j[None, :]).astype(np.float32)
    trib = (j[:, None] >= j[None, :]).astype(np.float32)
    blk = j // 16
    bmf = (blk[None, :] >= blk[:, None]).astype(np.float32)
    bmb = (blk[None, :] <= blk[:, None]).astype(np.float32)
    rmask = np.ones((128, 512), np.float32)
    rmask[:, ::128] = 0.0
    cst = np.concatenate([ident, trif, trib, bmf, bmb, rmask, np.zeros((128, 128), np.float32)], axis=1)
    csm = np.zeros((128, 32), np.float32)
    top = (j < 64)
    csm[:, 0] = np.where(top, -1.0, 1.0)
    hp = math.pi / 2
    csm[:, 1] = np.where(top, hp, math.pi)
    csm[:, 2] = np.where(top, 0.0, hp)
    csm[:, 3] = np.where(top, hp, math.pi)
    csm[:, 4] = np.where(top, math.pi, 3 * hp)
    csm[:, 5] = hp
    csm[:, 6] = 0.0
    csm[:, 7:15] = np.arange(1, 9, dtype=np.float32)[None, :]
    csm[:, 15:23] = (8 - np.arange(8, dtype=np.float32))[None, :]
    csm[:, 23:32] = (8.0 * np.arange(9, dtype=np.float32))[None, :]
    quarter = D // 4
    omega = (1.0 / (10000.0 ** (np.arange(quarter, dtype=np.float32) / quarter))).astype(np.float32)
    rr = np.arange(64, dtype=np.float32)
    ang = (rr[:, None] * omega[None, :]).astype(np.float32)
    tab = np.concatenate([np.sin(ang), np.cos(ang)], axis=1).astype(np.float32)
    fm = np.ascontiguousarray(tab.T.reshape(4, 128, 64).transpose(1, 0, 2)).reshape(128, 256)
    pos = np.concatenate([fm, fm], axis=1).astype(np.float32)
    return cst, csm, pos


def _prep_shared(inp):
    f = lambda a: np.ascontiguousarray(np.asarray(a, np.float32))
    sh = {}
    sh["w_mod"] = f(inp["w_mod"])
    sh["b_modT"] = _fm_vec(inp["b_mod"])
    sh["n1g"] = _fm_vec(inp["norm1_g"])
    sh["n2g"] = _fm_vec(inp["norm2_g"])
    sh["fng"] = _fm_vec(inp["final_g"])
    sh["w_in"] = f(inp["w_in"])
    sh["w_pg"] = f(inp["w_proj_gla"])
    sh["w_glu"] = f(inp["w_glu"])
    sh["w_ps"] = f(inp["w_proj_s5"])
    sh["w_out"] = f(inp["w_out"])
    sh["w_ff1"] = f(inp["w_ff1"])
    sh["w_ff2"] = f(inp["w_ff2"])
    sh["w_gu"] = np.ascontiguousarray(f(inp["w_gate_up"]).transpose(2, 0, 1, 3)).reshape(16, -1)
    sh["bgT"] = _fm_vec(inp["b_gate"])
    sh["gngT"] = _fm_vec(inp["gla_norm_g"])
    sh["bgluT"] = _fm_vec(inp["b_glu"])
    d = f(inp["s5_d"]).reshape(DEPTH, 32, 16)
    d8 = np.broadcast_to(d.transpose(2, 0, 1)[None], (8, 16, DEPTH, 32))
    sh["d8"] = np.ascontiguousarray(d8).reshape(128, -1)

    def klay(a):
        a = f(a).transpose(3, 0, 1, 2)
        return np.ascontiguousarray(np.concatenate([a, a], axis=0)).reshape(128, -1)
    sh["lre2"] = klay(inp["s5_lam_re"])
    sh["lim2"] = klay(inp["s5_lam_im"])
    ls = np.broadcast_to(f(inp["s5_log_step"])[None], (128, DEPTH, 2, 32))
    sh["lst2"] = np.ascontiguousarray(ls).reshape(128, -1)

    def klayB(a):
        a = f(a).transpose(3, 0, 1, 2, 4)
        return np.ascontiguousarray(np.concatenate([a, a], axis=0)).reshape(128, -1)

    def klayC(a):
        a = f(a).transpose(4, 0, 1, 2, 3)
        return np.ascontiguousarray(np.concatenate([a, a], axis=0)).reshape(128, -1)
    sh["Bre2"] = klayB(inp["s5_b_re"])
    sh["Bim2"] = klayB(inp["s5_b_im"])
    sh["Cre2"] = klayC(inp["s5_c_re"])
    sh["Cim2"] = klayC(inp["s5_c_im"])
    cst, csm, pos = _constants()
    sh["cst"], sh["csm"], sh["pos"] = cst, csm, pos
    return sh


def _prep_core(inp, core, NP, TP, TS):
    f = lambda a: np.asarray(a, np.float32)
    xp = f(inp["x_prompt"])[core * NP:(core + 1) * NP].reshape(NP * TP, D)
    xs = f(inp["x_sample"])[core]
    m = {}
    m["xT"] = np.ascontiguousarray(np.concatenate([xp, xs], axis=0).T)
    cond = np.stack([f(inp["c_ctx"]), f(inp["c"])[core]], axis=-1)
    m["condT"] = np.ascontiguousarray(cond.reshape(8, 128, 2).transpose(1, 0, 2)).reshape(128, 16)
    m["gla0"] = np.ascontiguousarray(f(inp["cache_gla_state"])[core]).reshape(-1, 256)
    re = f(inp["state_s5_re"])[core].transpose(3, 0, 1, 2)
    im = f(inp["state_s5_im"])[core].transpose(3, 0, 1, 2)
    m["s5x0"] = np.ascontiguousarray(np.concatenate([re, im], axis=0)).reshape(128, -1)
    m["s5x0s"] = np.ascontiguousarray(np.concatenate([im, re], axis=0)).reshape(128, -1)
    return m


_NC_CACHE = {}


def run_cfg(inp, NP, TP, TS, ncores):
    key = (NP, TP, TS)
    if key not in _NC_CACHE:
        _NC_CACHE[key] = build(key)
    nc = _NC_CACHE[key]
    sh = _prep_shared(inp)
    in_maps = []
    for c in range(ncores):
        m = dict(sh)
        m.update(_prep_core(inp, c, NP, TP, TS))
        in_maps.append(m)
    res = run_bass_kernel_spmd(nc, in_maps, core_ids=list(range(ncores)))
    B = NP * ncores
    y_prompt = np.zeros((B, TP, D), np.float32)
    y_sample = np.zeros((ncores, TS, D), np.float32)
    gla = np.zeros((B, DEPTH, 2, H, DK, DV), np.float32)
    s5re = np.zeros((B, DEPTH, 2, G, 64), np.float32)
    s5im = np.zeros((B, DEPTH, 2, G, 64), np.float32)
    for c in range(ncores):
        r = res.results[c]
        y = np.asarray(r["yT"], np.float32).T
        y_prompt[c * NP:(c + 1) * NP] = y[:NP * TP].reshape(NP, TP, D)
        y_sample[c] = y[NP * TP:]
        gla[c * NP:(c + 1) * NP] = np.asarray(r["gla_out"], np.float32).reshape(NP, DEPTH, 2, H, DK, DV)
        s = np.asarray(r["s5_out"], np.float32).reshape(2, 64, NP, DEPTH, 2, G)
        s5re[c * NP:(c + 1) * NP] = s[0].transpose(1, 2, 3, 4, 0)
        s5im[c * NP:(c + 1) * NP] = s[1].transpose(1, 2, 3, 4, 0)
    return (y_prompt, y_sample, gla, s5re, s5im)


def kernel(**inputs):
    return run_cfg(inputs, 4, 256, 4096, 8)
```
